# Optimizing a Trainium2 kernel written in Bass

```python
import jax, jax.numpy as jnp
from jax import lax
import numpy as np

D_MODEL = 1024
BATCH = 32
SEQ = 256
DEPTH = 1
DEC_BATCH = 8
DEC_SEQ = 4096
PAST_LEN = 512

GRID_W = 64
RWKV_WIDTH = D_MODEL // 2
HEAD_SIZE = 64
RWKV_HEADS = RWKV_WIDTH // HEAD_SIZE
CONV_WIDTH = D_MODEL - RWKV_WIDTH
CONV_GROUPS = 8
DECAY_LORA = 64
AAA_LORA = 64
GATE_LORA = 128
D_FF = 4 * D_MODEL
RWKV_SPLITS = (RWKV_WIDTH, RWKV_WIDTH, RWKV_WIDTH, DECAY_LORA, DECAY_LORA, AAA_LORA, AAA_LORA, GATE_LORA)
RWKV_OFFSETS = [int(o) for o in np.cumsum(RWKV_SPLITS)[:-1]]
RWKV_COLS = int(sum(RWKV_SPLITS))
IN_COLS = RWKV_COLS + 3 * CONV_WIDTH
NORM_EPS = 1e-6
GN_EPS = HEAD_SIZE * 1e-5

kernel_name = "hybrid_rwkv7_shortconv_diffusion_step"


def _rmsnorm(x, g):
    xf = x.astype(jnp.float32)
    y = xf * lax.rsqrt(jnp.mean(xf * xf, axis=-1, keepdims=True) + NORM_EPS)
    return (y * g.astype(jnp.float32)).astype(x.dtype)


def _neighbours(z, rows):
    b, l, ch = z.shape
    zg = z.reshape(b, 1, l, ch) if rows is None else z.reshape(b, rows, l // rows, ch)
    n = zg.shape[2]
    zp = jnp.pad(zg, ((0, 0), (0, 0), (1, 1), (0, 0)))
    prev = zp[:, :, :n].reshape(b, l, ch)
    nxt = zp[:, :, 2:].reshape(b, l, ch)
    return prev, nxt


def _rwkv_direction(k, wl, al, w0, w_w2, a0, w_a2, k_a):
    w = -jax.nn.softplus(-(w0 + jnp.tanh(wl) @ w_w2)) - 0.5
    decay = jnp.exp(-jnp.exp(w))
    a = jax.nn.sigmoid(a0 + al @ w_a2)
    k_dir = k * (1.0 + (a - 1.0) * k_a)
    return decay, a, k_dir


def _wkv7_scan(s0, r, w, k, v, kk, a, reverse):
    def step(s, inp):
        r_t, w_t, k_t, v_t, kk_t, a_t = inp
        sa = jnp.einsum('bhij,bhj->bhi', s, kk_t)
        s = (s * w_t[:, :, None, :]
             - sa[..., None] * (kk_t * a_t)[:, :, None, :]
             + v_t[..., None] * k_t[:, :, None, :])
        y = jnp.einsum('bhij,bhj->bhi', s, r_t)
        return s, y
    xs = tuple(jnp.swapaxes(t, 0, 1) for t in (r, w, k, v, kk, a))
    s, ys = lax.scan(step, s0, xs, reverse=reverse)
    return s, jnp.swapaxes(ys, 0, 1)


def _layer(x, mod, s_f0, s_b0, rows, norm1_g, w_in, mu_shift, w0_f, w_w2_f, w0_b, w_w2_b,
           a0_f, w_a2_f, a0_b, w_a2_b, w_g2, k_k, k_a, r_k, ln_x_w, ln_x_b,
           conv_w, conv_gain, w_out, norm2_g, w_ff1, w_ff2):
    b, l, _ = x.shape
    f32 = jnp.float32
    shift_a, scale_a, gate_a, shift_f, scale_f, gate_f = jnp.split(mod[:, None, :], 6, axis=-1)

    h = _rmsnorm(x, norm1_g) * (1 + scale_a) + shift_a
    z = h @ w_in
    z_rwkv = z[..., :RWKV_COLS].astype(f32)
    z_conv = z[..., RWKV_COLS:]

    prev, nxt = _neighbours(z_rwkv, rows)
    z_rwkv = z_rwkv + mu_shift * (0.5 * (prev + nxt) - z_rwkv)
    r, k, v, wl_f, wl_b, al_f, al_b, gl = jnp.split(z_rwkv, RWKV_OFFSETS, axis=-1)
    heads = lambda t: t.reshape(b, l, RWKV_HEADS, HEAD_SIZE)
    kk = heads(k * k_k)
    kk = kk * lax.rsqrt(jnp.sum(kk * kk, axis=-1, keepdims=True) + 1e-12)
    dec_f, a_f, k_f = _rwkv_direction(k, wl_f, al_f, w0_f, w_w2_f, a0_f, w_a2_f, k_a)
    dec_b, a_b, k_b = _rwkv_direction(k, wl_b, al_b, w0_b, w_w2_b, a0_b, w_a2_b, k_a)
    rh, vh, k_fh, k_bh = heads(r), heads(v), heads(k_f), heads(k_b)
    s_f, y_f = _wkv7_scan(s_f0.astype(f32), rh, heads(dec_f), k_fh, vh, kk, heads(a_f), reverse=False)
    s_b, y_b = _wkv7_scan(s_b0.astype(f32), rh, heads(dec_b), k_bh, vh, kk, heads(a_b), reverse=True)
    y = y_f + y_b
    y_mu = jnp.mean(y, axis=-1, keepdims=True)
    y_var = jnp.mean(jnp.square(y - y_mu), axis=-1, keepdims=True)
    yn = ((y - y_mu) * lax.rsqrt(y_var + GN_EPS)).reshape(b, l, RWKV_WIDTH) * ln_x_w + ln_x_b
    bonus = jnp.sum(rh * (k_fh + k_bh) * r_k, axis=-1, keepdims=True) * vh
    g = jax.nn.sigmoid(gl) @ w_g2
    o_rwkv = (yn + bonus.reshape(b, l, RWKV_WIDTH)) * g

    gate_b, gate_c, hc = jnp.split(z_conv, 3, axis=-1)
    u = gate_c * hc
    up, un = _neighbours(u, rows)
    o_conv = gate_b * (conv_w[0] * up + conv_w[1] * u + conv_w[2] * un)
    o_conv = _rmsnorm(o_conv, conv_gain)

    mix = jnp.concatenate([o_rwkv.astype(x.dtype), o_conv.astype(x.dtype)], axis=-1) @ w_out
    x = (x + gate_a * mix).astype(x.dtype)

    h = _rmsnorm(x, norm2_g) * (1 + scale_f) + shift_f
    f = jnp.square(jax.nn.relu(h @ w_ff1)) @ w_ff2
    x = (x + gate_f * f).astype(x.dtype)
    return x, s_f, s_b


def setup_inputs(seed: int = 0) -> dict:
    key = jax.random.key(seed)
    ks = iter(jax.random.split(key, 40))
    nrm = lambda shape, scale: scale * jax.random.normal(next(ks), shape, jnp.float32)
    L, D = DEPTH, D_MODEL
    st = (DEC_BATCH, L, RWKV_HEADS, HEAD_SIZE, HEAD_SIZE)
    return {
        "x_prompt": nrm((BATCH, SEQ, D), 1.0),
        "x_sample": nrm((DEC_BATCH, DEC_SEQ, D), 1.0),
        "c": nrm((DEC_BATCH, D), 1.0),
        "state_rwkv_fwd": nrm(st, 0.5),
        "state_rwkv_bwd": nrm(st, 0.5),
        "c_ctx": nrm((D,), 1.0),
        "w_mod": nrm((L, D, 6 * D), 0.5 * D ** -0.5),
        "b_mod": nrm((L, 6 * D), 0.02),
        "norm1_g": 1.0 + nrm((L, D), 0.02),
        "w_in": nrm((L, D, IN_COLS), D ** -0.5),
        "mu_shift": jax.random.uniform(next(ks), (L, RWKV_COLS), jnp.float32),
        "w0_f": -1.0 + nrm((L, RWKV_WIDTH), 0.5),
        "w_w2_f": nrm((L, DECAY_LORA, RWKV_WIDTH), 0.1 * DECAY_LORA ** -0.5),
        "w0_b": -1.0 + nrm((L, RWKV_WIDTH), 0.5),
        "w_w2_b": nrm((L, DECAY_LORA, RWKV_WIDTH), 0.1 * DECAY_LORA ** -0.5),
        "a0_f": nrm((L, RWKV_WIDTH), 0.1),
        "w_a2_f": nrm((L, AAA_LORA, RWKV_WIDTH), 0.5 * AAA_LORA ** -0.5),
        "a0_b": nrm((L, RWKV_WIDTH), 0.1),
        "w_a2_b": nrm((L, AAA_LORA, RWKV_WIDTH), 0.5 * AAA_LORA ** -0.5),
        "w_g2": nrm((L, GATE_LORA, RWKV_WIDTH), GATE_LORA ** -0.5),
        "k_k": 0.85 + nrm((L, RWKV_WIDTH), 0.02),
        "k_a": 1.0 + nrm((L, RWKV_WIDTH), 0.02),
        "r_k": nrm((L, RWKV_HEADS, HEAD_SIZE), 0.1),
        "ln_x_w": 1.0 + nrm((L, RWKV_WIDTH), 0.02),
        "ln_x_b": nrm((L, RWKV_WIDTH), 0.02),
        "conv_w": nrm((L, 3, CONV_WIDTH), 3 ** -0.5),
        "conv_gain": 1.0 + nrm((L, CONV_WIDTH), 0.02),
        "w_out": nrm((L, D, D), D ** -0.5),
        "norm2_g": 1.0 + nrm((L, D), 0.02),
        "w_ff1": nrm((L, D, D_FF), D ** -0.5),
        "w_ff2": nrm((L, D_FF, D), D_FF ** -0.5),
        "final_g": 1.0 + nrm((D,), 0.02),
    }


def reference(x_prompt, x_sample, c, state_rwkv_fwd, state_rwkv_bwd, c_ctx, w_mod, b_mod,
              norm1_g, w_in, mu_shift, w0_f, w_w2_f, w0_b, w_w2_b, a0_f, w_a2_f, a0_b, w_a2_b,
              w_g2, k_k, k_a, r_k, ln_x_w, ln_x_b, conv_w, conv_gain, w_out, norm2_g,
              w_ff1, w_ff2, final_g):
    rows = x_sample.shape[1] // GRID_W
    b_ctx = x_prompt.shape[0]
    zero_state = jnp.zeros((b_ctx, RWKV_HEADS, HEAD_SIZE, HEAD_SIZE), jnp.float32)
    x_ctx, x_lat = x_prompt, x_sample
    new_f, new_b = [], []
    for i in range(DEPTH):
        lw = dict(norm1_g=norm1_g[i], w_in=w_in[i], mu_shift=mu_shift[i],
                  w0_f=w0_f[i], w_w2_f=w_w2_f[i], w0_b=w0_b[i], w_w2_b=w_w2_b[i],
                  a0_f=a0_f[i], w_a2_f=w_a2_f[i], a0_b=a0_b[i], w_a2_b=w_a2_b[i],
                  w_g2=w_g2[i], k_k=k_k[i], k_a=k_a[i], r_k=r_k[i],
                  ln_x_w=ln_x_w[i], ln_x_b=ln_x_b[i], conv_w=conv_w[i], conv_gain=conv_gain[i],
                  w_out=w_out[i], norm2_g=norm2_g[i], w_ff1=w_ff1[i], w_ff2=w_ff2[i])
        mod_ctx = jnp.broadcast_to(jax.nn.silu(c_ctx) @ w_mod[i] + b_mod[i], (b_ctx, 6 * D_MODEL))
        x_ctx, s_f, s_b = _layer(x_ctx, mod_ctx, zero_state, zero_state, None, **lw)
        new_f.append(s_f)
        new_b.append(s_b)
        mod_lat = jax.nn.silu(c) @ w_mod[i] + b_mod[i]
        x_lat, _, _ = _layer(x_lat, mod_lat, state_rwkv_fwd[:, i], state_rwkv_bwd[:, i], rows, **lw)
    y_prompt = _rmsnorm(x_ctx, final_g)
    y_sample = _rmsnorm(x_lat, final_g)
    new_state_rwkv_fwd = jnp.stack(new_f, axis=1)
    new_state_rwkv_bwd = jnp.stack(new_b, axis=1)
    return (y_prompt, y_sample, new_state_rwkv_fwd, new_state_rwkv_bwd)
```

```python
import numpy as np
from contextlib import ExitStack
import concourse.bass as bass
import concourse.mybir as mybir
from concourse.ap import AP
from concourse.bass_utils import run_bass_kernel_spmd

F32 = mybir.dt.float32
BF16 = mybir.dt.bfloat16
AF = mybir.ActivationFunctionType
ALU = mybir.AluOpType
AX = mybir.AxisListType

ENG_NAMES = ("pe", "act", "dve", "pool", "sp")
NCORES = 8
NTOK = 5120
NTILE = 40
NSAMP_T = 32
D = 1024
KAPPA = 0.5 * float(np.exp(-0.5))
NORM_EPS = 1e-6
GN_EPS = 64 * 1e-5


class _Op:
    __slots__ = ("eng", "fn", "deps", "dma", "milestone", "need", "known_after")

    def __init__(self, eng, fn, dma):
        self.eng = eng
        self.fn = fn
        self.deps = []
        self.dma = dma
        self.milestone = None
        self.need = False
        self.known_after = None


class Prog:
    def __init__(self):
        self.ops = {e: [] for e in ENG_NAMES}
        self.last_writer = {}
        self.readers = {}

    cutoff = False

    def op(self, eng, fn, reads=(), writes=(), dma=None):
        if self.cutoff:
            return None
        pbr = [r for r in reads if isinstance(r, str) and r.startswith("pb")]
        if pbr:
            reads = [r for r in reads if r not in pbr]
            writes = list(writes) + [r for r in pbr if r not in writes]
        o = _Op(eng, fn, dma)
        seen = set()

        def add(d, raw):
            if d is None or id(d) in seen:
                return
            if d.dma is None and d.eng == eng and dma is None:
                if eng == "pe" or not raw:
                    return
            seen.add(id(d))
            o.deps.append(d)

        for r in reads:
            add(self.last_writer.get(r), True)
        for r in writes:
            add(self.last_writer.get(r), False)
            for rd in self.readers.get(r, ()):
                add(rd, False)
        self.ops[eng].append(o)
        for r in writes:
            self.last_writer[r] = o
            self.readers[r] = []
        for r in reads:
            if r not in writes:
                self.readers.setdefault(r, []).append(o)
        return o


def build_and_emit(nc, prog, bar):
    with ExitStack() as es:
        eng_sem = {e: es.enter_context(nc.semaphore("s_" + e)) for e in ENG_NAMES}
        for e in ENG_NAMES:
            for o in prog.ops[e]:
                for d in o.deps:
                    d.need = True
        dma_keys, seen = [], set()
        for e in ENG_NAMES:
            for o in prog.ops[e]:
                if o.dma is not None and o.dma not in seen:
                    seen.add(o.dma)
                    dma_keys.append(o.dma)
        dma_sem = {k: es.enter_context(nc.semaphore("d_%d" % i)) for i, k in enumerate(dma_keys)}
        dma_cnt = {k: 0 for k in dma_keys}
        final = {}
        for e in ENG_NAMES:
            m = 0
            for o in prog.ops[e]:
                if o.dma is not None:
                    dma_cnt[o.dma] += 1
                    o.milestone = 16 * dma_cnt[o.dma]
                elif o.need:
                    m += 1
                    o.milestone = m
            final[e] = m + 1
        pending = {e: 0 for e in ENG_NAMES}
        known = {e: {} for e in ENG_NAMES}
        waits = {}
        progress = True
        while progress:
            progress = False
            for e in ENG_NAMES:
                lst = prog.ops[e]
                kn = known[e]
                while pending[e] < len(lst):
                    o = lst[pending[e]]
                    if any(d.known_after is None for d in o.deps):
                        break
                    w = []
                    for d in o.deps:
                        key = ("dma", d.dma) if d.dma is not None else ("eng", d.eng)
                        if kn.get(key, 0) >= d.milestone:
                            continue
                        w.append(d)
                        for k2, v2 in d.known_after.items():
                            if kn.get(k2, 0) < v2:
                                kn[k2] = v2
                        if kn.get(key, 0) < d.milestone:
                            kn[key] = d.milestone
                    waits[id(o)] = w
                    ka = dict(kn)
                    if o.dma is None and o.milestone is not None:
                        ka[("eng", e)] = o.milestone
                    o.known_after = ka
                    pending[e] += 1
                    progress = True
        for e in ENG_NAMES:
            assert pending[e] == len(prog.ops[e]), "dependency cycle on " + e

        def run_engine(e, engobj):
            for o in prog.ops[e]:
                for d in waits[id(o)]:
                    s = dma_sem[d.dma] if d.dma is not None else eng_sem[d.eng]
                    engobj.wait_ge(s, d.milestone)
                ins = o.fn(engobj)
                if o.dma is not None:
                    ins.then_inc(dma_sem[o.dma], 16)
                elif o.milestone is not None:
                    ins.then_inc(eng_sem[e], 1)
            if e != "sp":
                bar[e](engobj).then_inc(eng_sem[e], 1)
            for k in dma_keys:
                engobj.wait_ge(dma_sem[k], 16 * dma_cnt[k])
            for e2 in ENG_NAMES:
                if e2 != "sp" and e2 != e:
                    engobj.wait_ge(eng_sem[e2], final[e2])

        with nc.Block() as block:
            @block.tensor
            def _(eng):
                run_engine("pe", eng)

            @block.scalar
            def _(eng):
                run_engine("act", eng)

            @block.vector
            def _(eng):
                run_engine("dve", eng)

            @block.gpsimd
            def _(eng):
                run_engine("pool", eng)

            @block.sync
            def _(eng):
                run_engine("sp", eng)


VR = {}
_r = 0
for _n, _c in (("mu", 15), ("w0f", 4), ("w0b", 4), ("a0f", 4), ("a0b", 4), ("kk", 4), ("ka", 4),
               ("rk", 4), ("cw0", 4), ("cw1", 4), ("cw2", 4), ("cg", 4), ("g1", 8), ("g2", 8),
               ("bsha", 8), ("bsca", 8), ("bshf", 8), ("bscf", 8)):
    VR[_n] = _r
    _r += _c
NVR = _r


def bcast(ap, shape):
    return ap.broadcast_to(list(shape))


def build_program(dbg=False, stop_after=9, p1_groups=20, p1_stage=99, p0_stage=99):
    nc = bass.Bass("TRN2", target_bir_lowering=False)

    def din(name, shape, dt=F32):
        return nc.dram_tensor(name, list(shape), dt, kind="ExternalInput").ap()

    x_d = din("x", [NTOK, D])
    c2_d = din("c2", [2, D])
    st_d = din("st", [2, 8, 64, 64])
    vecs_d = din("vecs", [128, 128])
    wmod_d = din("w_mod", [D, 6 * D])
    bmod_d = din("b_mod", [1, 6 * D])
    win_d = din("w_in", [D, 3456])
    ww2_d = din("w_w2", [128, 512])
    wa2_d = din("w_a2", [128, 512])
    wg2_d = din("w_g2", [128, 512])
    lnw_d = din("ln_w", [1, 512])
    lnb_d = din("ln_b", [1, 512])
    fg_d = din("final_g", [1, D])
    wout_d = din("w_out", [D, D])
    wff1_d = din("w_ff1", [D, 4 * D])
    wff2_d = din("w_ff2", [4 * D, D])
    y_d = nc.dram_tensor("y", [NTOK, D], F32, kind="ExternalOutput").ap()
    ns_d = nc.dram_tensor("ns", [2, 4, 8, 64, 64], F32, kind="ExternalOutput").ap()

    def dscr(name, shape, dt):
        return nc.dram_tensor(name, list(shape), dt, kind="Internal").ap()

    mtS = dscr("mtS", [NTILE, 128, 1024], BF16)
    nnS = dscr("nnS", [NTILE, 128, 512], F32)
    qtS = dscr("qtS", [NTILE, 128, 1024], BF16)
    ysS = dscr("ysS", [NTILE, 128, 512], F32)
    gS = dscr("gS", [NTILE, 128, 512], BF16)
    bvS = dscr("bvS", [NTILE, 128, 512], BF16)
    ocS = dscr("ocS", [NTILE, 128, 512], BF16)
    gatesS = dscr("gatesS", [128, 4 * D], F32)
    wcvS = dscr("wcvS", [12, 128, 8, 128], BF16)

    with ExitStack() as es0:
        def sb(es, name, shape, dt):
            return es.enter_context(nc.sbuf_tensor(name, list(shape), dt))

        def pst(es, name, shape, dt):
            return es.enter_context(nc.psum_tensor(name, list(shape), dt))

        identb = sb(es0, "identb", [128, 128], BF16)
        identf = sb(es0, "identf", [128, 128], F32)
        ind8 = sb(es0, "ind8", [128, 4, 8], BF16)
        onescol = sb(es0, "onescol", [128, 1], BF16)
        nhalf = sb(es0, "nhalf", [128, 8], F32)
        vc = sb(es0, "vc", [128, 128], F32)
        dc = sb(es0, "dc", [128, 64], F32)
        modc = sb(es0, "modc", [128, 2, 4, 8], F32)
        T32 = [sb(es0, "T32_%d" % d, [128, 4, 64], F32) for d in range(2)]
        Tbf = [sb(es0, "Tbf_%d" % d, [128, 4, 64], BF16) for d in range(2)]
        T32i = sb(es0, "T32i", [128, 4, 64], F32)
        gC = [None, sb(es0, "gC_1", [128, 4, 80], F32)]
        rcall = sb(es0, "rcall", [128, NTILE], F32)
        bart = sb(es0, "bart", [128, 8], F32)
        barb = sb(es0, "barb", [128, 8], BF16)
        PB = [pst(es0, "pb%d" % i, [128, 512], F32) for i in range(8)]

        def pbf(i):
            return PB[i][:].bitcast(BF16)

        bar = {
            "pe": lambda e: e.matmul(PB[7][0:8, 0:8], lhsT=barb[0:8, 0:8], rhs=barb[0:8, 0:8], start=True, stop=True),
            "act": lambda e: e.activation(out=bart[:, 0:1], in_=bart[:, 1:2], func=AF.Copy),
            "dve": lambda e: e.tensor_copy(out=bart[:, 2:3], in_=bart[:, 3:4]),
            "pool": lambda e: e.tensor_copy(out=bart[:, 4:5], in_=bart[:, 5:6]),
        }

        with ExitStack() as es:
            P = Prog()
            op = P.op
            vraw = sb(es, "vraw", [128, 128], F32)
            c2T = sb(es, "c2T", [128, 8, 2], F32)
            c2raw = sb(es, "c2raw", [2, D], F32)
            th0 = sb(es, "th0", [128, 8, 2], F32)
            silu = sb(es, "silu", [128, 8, 2], F32)
            srep = sb(es, "srep", [128, 2, 8, 128], F32)
            wms = [sb(es, "wms%d" % i, [128, 8, 512], F32) for i in range(2)]
            bmb = sb(es, "bmb", [128, 2, D], F32)
            straw = sb(es, "straw", [64, 2, 4, 128], F32)
            modp = sb(es, "modp", [128, 32, 2], F32)
            gates = sb(es, "gates", [128, 2, 2, D], F32)

            op("pool", lambda e: e.memset(bart[:], 0.0), writes=["bart"])
            op("pool", lambda e: e.memset(barb[:], 0.0), writes=["barb"])
            op("pool", lambda e: e.memset(identf[:], 1.0), writes=["identf"])
            op("pool", lambda e: e.affine_select(out=identf[:], in_=identf[:], pattern=[[-1, 128]],
                                                 compare_op=ALU.is_equal, fill=0.0, base=0, channel_multiplier=1),
               reads=["identf"], writes=["identf"])
            op("pool", lambda e: e.tensor_copy(out=identb[:], in_=identf[:]), reads=["identf"], writes=["identb"])
            op("pool", lambda e: e.memset(ind8[:], 0.0), writes=["ind8"])
            for c in range(4):
                for p in range(2):
                    op("pool", lambda e, c=c, p=p: e.memset(ind8[64 * p:64 * p + 64, c, 2 * c + p:2 * c + p + 1], 1.0),
                       reads=["ind8"], writes=["ind8"])
            op("pool", lambda e: e.memset(onescol[:], 1.0), writes=["onescol"])
            op("pool", lambda e: e.memset(nhalf[:], -0.5), writes=["nhalf"])
            op("pool", lambda e: e.memset(gC[1][:], 1.0), writes=["gC1"])
            op("pool", lambda e: e.memset(rcall[:], 1.0), writes=["rcall"])

            P.cutoff = p0_stage < 1
            op("sp", lambda e: e.dma_start(out=vraw[:], in_=vecs_d[:, :]), writes=["vraw"], dma="l0")
            op("sp", lambda e: e.dma_start(out=c2raw[:], in_=c2_d[:, :]), writes=["c2raw"], dma="l1")
            op("sp", lambda e: e.dma_start(out=bmb[:, 0, :], in_=bcast(bmod_d[0:1, 2 * D:3 * D], [128, D])),
               writes=["bmb0"], dma="l5")
            op("sp", lambda e: e.dma_start(out=bmb[:, 1, :], in_=bcast(bmod_d[0:1, 5 * D:6 * D], [128, D])),
               writes=["bmb1"], dma="l6")
            for d in range(2):
                op("sp", lambda e, d=d: e.dma_start(
                    out=straw[:, d, :, :].rearrange("i c (p j) -> i c p j", p=2),
                    in_=st_d[d].rearrange("(c p) i j -> i c p j", p=2)), writes=["straw%d" % d], dma="l7%d" % d)

            P.cutoff = p0_stage < 2
            op("pe", lambda e: e.transpose(out=PB[0][:, 0:128], in_=vraw[:], identity=identf[:]),
               reads=["vraw", "identf"], writes=["pb0"])
            op("dve", lambda e: e.tensor_copy(out=vc[:], in_=PB[0][:, 0:128]), reads=["pb0"], writes=["vc"])
            op("dve", lambda e: e.tensor_scalar(out=dc[:, 0:8], in0=vc[:, VR["w0f"]:VR["w0f"] + 8], scalar1=0.5,
                                                scalar2=None, op0=ALU.mult), reads=["vc"], writes=["dc"])
            op("dve", lambda e: e.tensor_scalar(out=dc[:, 8:16], in0=vc[:, VR["a0f"]:VR["a0f"] + 8], scalar1=0.5,
                                                scalar2=None, op0=ALU.mult), reads=["vc"], writes=["dc"])
            op("dve", lambda e: e.tensor_scalar(out=dc[:, 16:20], in0=vc[:, VR["ka"]:VR["ka"] + 4], scalar1=0.5,
                                                scalar2=None, op0=ALU.mult), reads=["vc"], writes=["dc"])
            op("dve", lambda e: e.tensor_scalar(out=dc[:, 20:24], in0=vc[:, VR["ka"]:VR["ka"] + 4], scalar1=-0.5,
                                                scalar2=1.0, op0=ALU.mult, op1=ALU.add), reads=["vc"], writes=["dc"])
            op("dve", lambda e: e.tensor_scalar(out=dc[:, 32:47], in0=vc[:, VR["mu"]:VR["mu"] + 15], scalar1=0.5,
                                                scalar2=None, op0=ALU.mult), reads=["vc"], writes=["dc"])
            op("dve", lambda e: e.tensor_scalar(out=dc[:, 48:63], in0=vc[:, VR["mu"]:VR["mu"] + 15], scalar1=-1.0,
                                                scalar2=1.0, op0=ALU.mult, op1=ALU.add), reads=["vc"], writes=["dc"])
            P.cutoff = p0_stage < 3
            for d in range(2):
                for c in range(4):
                    op("pe", lambda e, d=d, c=c: e.transpose(out=PB[1][:, c * 64:c * 64 + 64], in_=straw[:, d, c, :],
                                                             identity=identf[0:64, 0:64]),
                       reads=["straw%d" % d, "identf"], writes=["pb1"])
                op("dve", lambda e, d=d: e.tensor_copy(out=T32[d][:], in_=PB[1][:, 0:256].rearrange("p (c i) -> p c i", c=4)),
                   reads=["pb1"], writes=["T32_%d" % d])
                op("act", lambda e, d=d: e.activation(out=Tbf[d][:], in_=T32[d][:], func=AF.Copy),
                   reads=["T32_%d" % d], writes=["Tbf_%d" % d])
                if d == 1:
                    op("dve", lambda e: e.tensor_copy(out=T32i[:], in_=T32[1][:]), reads=["T32_1"], writes=["T32i"])
            P.cutoff = p0_stage < 4
            for kc in range(8):
                op("pe", lambda e, kc=kc: e.transpose(out=PB[5][:, kc * 2:kc * 2 + 2], in_=c2raw[0:2, kc * 128:kc * 128 + 128],
                                                      identity=identf[0:2, 0:2]), reads=["c2raw", "identf"], writes=["pb5"])
            op("dve", lambda e: e.tensor_copy(out=c2T[:], in_=PB[5][:, 0:16].rearrange("p (k m) -> p k m", m=2)),
               reads=["pb5"], writes=["c2T"])
            op("act", lambda e: e.activation(out=th0[:], in_=c2T[:], func=AF.Tanh, scale=0.5), reads=["c2T"], writes=["th0"])
            op("dve", lambda e: e.tensor_scalar(out=th0[:], in0=th0[:], scalar1=0.5, scalar2=0.5, op0=ALU.mult, op1=ALU.add),
               reads=["th0"], writes=["th0"])
            op("dve", lambda e: e.tensor_tensor(out=silu[:], in0=th0[:], in1=c2T[:], op=ALU.mult),
               reads=["th0", "c2T"], writes=["silu"])
            for m in range(2):
                op("dve", lambda e, m=m: e.tensor_copy(out=srep[:, m, :, :], in_=bcast(silu[:, :, m:m + 1], [128, 8, 128])),
                   reads=["silu"], writes=["srep%d" % m])
            P.cutoff = p0_stage < 5
            for b in range(12):
                q, half = b // 2, b % 2
                w = wms[b % 2]
                wn = "wms%d" % (b % 2)
                op("sp", lambda e, b=b, w=w: e.dma_start(out=w[:], in_=wmod_d[:, 512 * b:512 * b + 512].rearrange(
                    "(kc p) n -> p kc n", p=128)), writes=[wn], dma="wm%d" % (b % 2))
                if q in (2, 5):
                    gi = 0 if q == 2 else 1
                    for m in range(2):
                        bank = 2 + m
                        for kc in range(8):
                            op("pe", lambda e, m=m, kc=kc, w=w, bank=bank: e.matmul(
                                PB[bank][:, :], lhsT=srep[:, m, kc, :], rhs=w[:, kc, :], start=(kc == 0), stop=(kc == 7)),
                               reads=[wn, "srep%d" % m], writes=["pb%d" % bank])
                        op("dve", lambda e, m=m, gi=gi, half=half, bank=bank: e.tensor_tensor(
                            out=gates[:, m, gi, 512 * half:512 * half + 512], in0=PB[bank][:, :],
                            in1=bmb[:, gi, 512 * half:512 * half + 512], op=ALU.add),
                           reads=["pb%d" % bank, "bmb%d" % gi], writes=["gates"])
                else:
                    qi = {0: 0, 1: 1, 3: 2, 4: 3}[q]
                    for cc in range(4):
                        col = qi * 8 + half * 4 + cc
                        for kc in range(8):
                            op("pe", lambda e, kc=kc, cc=cc, col=col, w=w: e.matmul(
                                PB[4][:, 2 * col:2 * col + 2], lhsT=w[:, kc, cc * 128:cc * 128 + 128], rhs=silu[:, kc, :],
                                start=(kc == 0), stop=(kc == 7)), reads=[wn, "silu"], writes=["pb4"])
            P.cutoff = p0_stage < 6
            op("dve", lambda e: e.tensor_copy(out=modp[:], in_=PB[4][:, 0:64].rearrange("p (c m) -> p c m", m=2)),
               reads=["pb4"], writes=["modp"])
            for m in range(2):
                for (dst, qsc, bname, gname) in ((0, 1, "bsca", "g1"), (2, 3, "bscf", "g2")):
                    op("dve", lambda e, m=m, dst=dst, qsc=qsc, bname=bname: e.scalar_tensor_tensor(
                        out=modc[:, m, dst, :], in0=modp[:, qsc * 8:qsc * 8 + 8, m], scalar=1.0,
                        in1=vc[:, VR[bname]:VR[bname] + 8], op0=ALU.add, op1=ALU.add),
                       reads=["modp", "vc"], writes=["modc"])
                    op("dve", lambda e, m=m, dst=dst, gname=gname: e.tensor_tensor(
                        out=modc[:, m, dst, :], in0=modc[:, m, dst, :], in1=vc[:, VR[gname]:VR[gname] + 8], op=ALU.mult),
                       reads=["modc", "vc"], writes=["modc"])
                for (dst, qsh, bname) in ((1, 0, "bsha"), (3, 2, "bshf")):
                    op("dve", lambda e, m=m, dst=dst, qsh=qsh, bname=bname: e.tensor_tensor(
                        out=modc[:, m, dst, :], in0=modp[:, qsh * 8:qsh * 8 + 8, m], in1=vc[:, VR[bname]:VR[bname] + 8],
                        op=ALU.add), reads=["modp", "vc"], writes=["modc"])
            P.cutoff = p0_stage < 7
            op("sp", lambda e: e.dma_start(out=gatesS[:, :], in_=gates[:].rearrange("p a b n -> p (a b n)")),
               reads=["gates"], writes=["gatesS"], dma="sg")
            build_and_emit(nc, P, bar)

        G = 256
        if stop_after < 1:
            return nc
        with ExitStack() as es:
            P = Prog()
            op = P.op
            winb = sb(es, "winb", [128, 8, 1920], BF16)
            wtmp = sb(es, "wtmp", [128, 1536], BF16)
            wcv = [sb(es, "wcv%d" % i, [128, 8, 128], BF16) for i in range(4)]
            ww2 = sb(es, "ww2", [128, 512], BF16)
            wa2 = sb(es, "wa2", [128, 512], BF16)
            wg2 = sb(es, "wg2", [128, 512], BF16)
            op("pool", lambda e: e.dma_start(out=ww2[:], in_=ww2_d[:, :]), writes=["ww2"], dma="g0")
            op("pool", lambda e: e.dma_start(out=wa2[:], in_=wa2_d[:, :]), writes=["wa2"], dma="g1")
            op("pool", lambda e: e.dma_start(out=wg2[:], in_=wg2_d[:, :]), writes=["wg2"], dma="g2")
            gC[0] = sb(es, "gC_0", [128, 4, 80], F32)
            ones64 = sb(es, "ones64", [128, 64], F32)
            op("pool", lambda e: e.memset(gC[0][:], 1.0), writes=["gC0"])
            op("pool", lambda e: e.memset(ones64[:], 1.0), writes=["ones64"])
            mSL = sb(es, "mSL", [128, 128], F32)
            mSU = sb(es, "mSU", [128, 128], F32)
            mLI = sb(es, "mLI", [128, 128], F32)
            mUI = sb(es, "mUI", [128, 128], F32)
            for (mt_, pat, cm, cmp_, zr) in ((mSL, -1, 1, ALU.is_gt, (64, 0)), (mLI, -1, 1, ALU.is_ge, (64, 0)),
                                            (mSU, 1, -1, ALU.is_gt, (0, 64)), (mUI, 1, -1, ALU.is_ge, (0, 64))):
                nm = mt_.name
                op("pool", lambda e, t=mt_: e.memset(t[:], 1.0), writes=[nm])
                op("pool", lambda e, t=mt_, pat=pat, cm=cm, cmp_=cmp_: e.affine_select(
                    out=t[:], in_=t[:], pattern=[[pat, 128]], compare_op=cmp_, fill=0.0, base=0,
                    channel_multiplier=cm), reads=[nm], writes=[nm])
                op("pool", lambda e, t=mt_, zr=zr: e.memset(t[zr[0]:zr[0] + 64, zr[1]:zr[1] + 64], 0.0),
                   reads=[nm], writes=[nm])
            xt0_ = sb(es, "xt0", [128, D], F32)
            xt = [xt0_, xt0_]
            xs = [sb(es, "xs%d" % i, [128, D], BF16) for i in range(2)]
            st1 = sb(es, "st1", [128, 8], F32)
            hT = sb(es, "hT", [128, 8, G], BF16)
            zpS = [sb(es, "zpS%d" % i, [128, 4, 66], F32) for i in range(2)]
            zpP = [sb(es, "zpP%d" % i, [128, 1, 258], F32) for i in range(2)]
            upS = sb(es, "upS", [128, 4, 66], F32)
            upP = sb(es, "upP", [128, 1, 258], F32)
            sT = sb(es, "sT", [128, G], F32)
            uT = sb(es, "uT", [128, G], F32)
            rk = sb(es, "rk", [128, 8, G], F32)
            vbf = sb(es, "vbf", [128, 4, G], BF16)
            lor = sb(es, "lor", [128, 3, G], BF16)
            tgl = sb(es, "tgl", [128, G], F32)
            gcs = sb(es, "gcs", [128, G], F32)
            gbs = sb(es, "gbs", [128, G], F32)
            ct = [sb(es, "ct%d" % i, [128, G], F32) for i in range(3)]
            sqc = sb(es, "sqc", [128, 4, G], BF16)
            ocb = sb(es, "ocb", [128, 4, G], BF16)
            kr = sb(es, "kr", [128, 4, G], F32)
            kkt = kr
            sqk = sqc
            rnx = sb(es, "rnx", [128, 512], F32)
            st8 = sb(es, "st8", [128, 4, 8], F32)
            pr = sb(es, "pr", [128, 4, G], BF16)
            prd = sb(es, "prd", [128, G], BF16)
            tw = sb(es, "tw", [128, G], F32)
            Lc = sb(es, "Lc", [128, G], F32)
            nLx = sb(es, "nLx", [128, G], F32)
            Ein = sb(es, "Ein", [128, G], F32)
            Einv = sb(es, "Einv", [128, G], F32)
            Eex = sb(es, "Eex", [128, G], F32)
            ta = sb(es, "ta", [128, G], F32)
            qq = sb(es, "qq", [128, G], F32)
            kd = sb(es, "kd", [128, G], F32)
            bq = sb(es, "bq", [128, G], F32)
            Rt = [sb(es, "Rt%d" % d, [128, 4, G], BF16) for d in range(2)]
            Kt = [sb(es, "Kt%d" % d, [128, 4, G], BF16) for d in range(2)]
            Bt = [sb(es, "Bt%d" % d, [128, 4, G], BF16) for d in range(2)]
            At = [sb(es, "At%d" % d, [128, 4, G], BF16) for d in range(2)]
            Atm = [sb(es, "Atm%d" % d, [128, 512], BF16) for d in range(2)]
            Btm = [sb(es, "Btm%d" % d, [128, 512], BF16) for d in range(2)]
            Ktm = [sb(es, "Ktm%d" % d, [128, 512], BF16) for d in range(2)]
            Vtm = sb(es, "Vtm", [128, 512], BF16)
            bon = sb(es, "bon", [128, 16], F32)
            bvt = sb(es, "bvt", [128, 512], BF16)
            gtm = sb(es, "gtm", [128, 512], BF16)
            ysum = sb(es, "ysum", [128, 512], F32)
            UB = {}
            for d in range(2):
                for p in range(2):
                    u = {}
                    for nm in ("ab", "abT", "akT", "rbT", "rkT", "P1", "PT0", "PT1", "X0", "X1"):
                        u[nm] = sb(es, "u%d%d_%s" % (d, p, nm), [128, 4, 128], BF16)
                    u["P0"] = u["akT"]
                    UB[(d, p)] = u
            mtT = [sb(es, "mtT%d" % d, [128, 2, 4, 128], BF16) for d in range(2)]
            nnT = [sb(es, "nnT%d" % d, [128, 2, 4, 64], F32) for d in range(2)]
            qtT = [sb(es, "qtT%d" % d, [128, 2, 4, 128], BF16) for d in range(2)]
            ctmp = sb(es, "ctmp", [128, 4, 64], F32)
            otr = sb(es, "otr", [64, 128], F32)

            for d in range(2):
                op("pool", lambda e, d=d: e.memset(qtT[d][:], 0.0), writes=["qtT%d" % d])
                op("pool", lambda e, d=d: e.memset(mtT[d][:], 0.0), writes=["mtT%d" % d])
            for t_ in zpS + zpP + [upS, upP]:
                op("pool", lambda e, t_=t_: e.memset(t_[:], 0.0), writes=[t_.name])
            for kc in range(8):
                op("pool", lambda e, kc=kc: e.dma_start(out=winb[:, kc, :], in_=win_d[kc * 128:kc * 128 + 128, 0:1920]),
                   writes=["winb%d" % kc], dma="wi%d" % kc)
                op("pool", lambda e, kc=kc: e.dma_start(out=wtmp[:], in_=win_d[kc * 128:kc * 128 + 128, 1920:3456]),
                   writes=["wtmp"], dma="wt")
                op("sp", lambda e, kc=kc: e.dma_start(out=wcvS[:, :, kc, :].rearrange("o p n -> p o n"),
                                                      in_=wtmp[:].rearrange("p (o n) -> p o n", o=12)),
                   reads=["wtmp"], writes=["wcvS"], dma="wts")
            WINB = ["winb%d" % kc for kc in range(8)]

            zrot = [0]
            ubk = [0]
            UBANKS = [4, 5, 6, 7, 0, 1]
            CONV_ORDER = [19, 23, 15, 20, 24, 16, 21, 25, 17, 22, 26, 18]

            def load_wcv(pos):
                oc = CONV_ORDER[pos]
                wsl = pos % 4
                op("sp", lambda e, oc=oc, wsl=wsl: e.dma_start(out=wcv[wsl][:].rearrange("p k n -> p (k n)"),
                                                               in_=wcvS[oc - 15].rearrange("p k n -> p (k n)")),
                   reads=["wcvS"], writes=["wcv%d" % wsl], dma="wcv%d" % wsl)
            P.cutoff = p1_stage < 1

            def next_ubank():
                b = UBANKS[ubk[0] % len(UBANKS)]
                ubk[0] += 1
                return b

            def do_group(gi):
                isS = gi < 16
                m = 0 if isS else 1
                tok0 = gi * G
                rows, rl = (4, 64) if isS else (1, 256)
                zp = zpS if isS else zpP
                up_ = upS if isS else upP

                def v3(ap2):
                    return ap2.rearrange("p (r l) -> p r l", r=rows)

                for pos_ in range(4):
                    load_wcv(pos_)
                for tl in range(2):
                    ti = gi * 2 + tl
                    X, XS = xt[tl], xs[tl]
                    xn, xsn = "xt0", "xs%d" % tl
                    op("sp", lambda e, X=X, ti=ti: e.dma_start(out=X[:], in_=x_d[ti * 128:ti * 128 + 128, :]),
                       writes=[xn], dma="x0")
                    op("act", lambda e, X=X, XS=XS, tl=tl: e.activation(out=XS[:], in_=X[:], func=AF.Square,
                                                                        accum_out=st1[:, tl:tl + 1]),
                       reads=[xn], writes=[xsn, "st1a%d" % tl])
                    op("dve", lambda e, tl=tl: e.tensor_scalar(out=st1[:, 2 + tl:3 + tl], in0=st1[:, tl:tl + 1],
                                                               scalar1=1.0 / D, scalar2=NORM_EPS, op0=ALU.mult, op1=ALU.add),
                       reads=["st1a%d" % tl], writes=["st1b%d" % tl])
                    op("pool", lambda e, tl=tl: e.tensor_tensor(out=st1[:, 4 + tl:5 + tl], in0=st1[:, 2 + tl:3 + tl],
                                                                in1=nhalf[:, 0:1], op=ALU.pow),
                       reads=["st1b%d" % tl, "nhalf"], writes=["st1c%d" % tl])
                    op("act", lambda e, X=X, XS=XS, tl=tl: e.activation(out=XS[:], in_=X[:], func=AF.Copy,
                                                                        scale=st1[:, 4 + tl:5 + tl]),
                       reads=[xn, "st1c%d" % tl], writes=[xsn])
                    for kc in range(8):
                        op("pe", lambda e, XS=XS, kc=kc, tl=tl: e.transpose(
                            out=pbf(0)[:, kc * 128:kc * 128 + 128] if tl == 0 else pbf(1)[:, kc * 128:kc * 128 + 128],
                            in_=XS[:, kc * 128:kc * 128 + 128], identity=identb[:]),
                           reads=[xsn, "identb"], writes=["pb%d" % tl])
                    for kc in range(8):
                        op("act", lambda e, kc=kc, tl=tl, m=m: e.activation(
                            out=hT[:, kc, tl * 128:tl * 128 + 128], in_=pbf(tl)[:, kc * 128:kc * 128 + 128],
                            func=AF.Identity, scale=modc[:, m, 0, kc:kc + 1], bias=modc[:, m, 1, kc:kc + 1]),
                           reads=["pb%d" % tl, "modc"], writes=["hT%d" % kc])
                HT = ["hT%d" % kc for kc in range(8)]

                def zmm(oc):
                    slot = zrot[0] % 2
                    zrot[0] += 1
                    bank = 2 + slot
                    apz = PB[bank][:, 0:256]
                    rn_ = "pb%d" % bank
                    if oc < 15:
                        for kc in range(8):
                            op("pe", lambda e, kc=kc, oc=oc, apz=apz: e.matmul(
                                apz, lhsT=winb[:, kc, oc * 128:oc * 128 + 128], rhs=hT[:, kc, :], start=(kc == 0), stop=(kc == 7)),
                               reads=[WINB[kc], HT[kc]], writes=[rn_])
                    else:
                        pos = CONV_ORDER.index(oc)
                        wsl = pos % 4
                        for kc in range(8):
                            op("pe", lambda e, kc=kc, wsl=wsl, apz=apz: e.matmul(
                                apz, lhsT=wcv[wsl][:, kc, :], rhs=hT[:, kc, :], start=(kc == 0), stop=(kc == 7)),
                               reads=["wcv%d" % wsl, HT[kc]], writes=[rn_])
                        if pos + 4 < 12:
                            load_wcv(pos + 4)
                    return apz, rn_

                P.cutoff = p1_stage < 2
                import os
                A2N = int(os.environ.get("A2N", "15"))
                A2K = int(os.environ.get("A2K", "99"))
                for oc in range(A2N):
                    apz, rn_ = zmm(oc)
                    zb = zp[oc % 2]
                    zbn = zb.name
                    if A2K < 1:
                        continue
                    op("act", lambda e, apz=apz, zb=zb: e.activation(out=zb[:, :, 1:rl + 1], in_=v3(apz), func=AF.Copy),
                       reads=[rn_], writes=[zbn])
                    if A2K < 2:
                        continue
                    op("act", lambda e, apz=apz, oc=oc: e.activation(out=v3(sT[:]), in_=v3(apz), func=AF.Copy,
                                                                     scale=dc[:, 48 + oc:49 + oc]),
                       reads=[rn_, "dc"], writes=["sT"])
                    op("dve", lambda e, zb=zb, oc=oc: e.scalar_tensor_tensor(out=v3(uT[:]), in0=zb[:, :, 0:rl], scalar=dc[:, 32 + oc:33 + oc],
                                                                             in1=v3(sT[:]), op0=ALU.mult, op1=ALU.add),
                       reads=["sT", zbn, "dc"], writes=["uT"])
                    if oc < 8:
                        dst, dn = rk[:, oc, :], "rk%d" % oc
                    elif oc < 12:
                        dst, dn = vbf[:, oc - 8, :], "vbf%d" % (oc - 8)
                    else:
                        dst, dn = (tgl[:] if oc == 14 else uT[:]), ("tgl" if oc == 14 else "uT")
                    if oc < 12 or oc == 14:
                        op("dve", lambda e, zb=zb, dst=dst, oc=oc: e.scalar_tensor_tensor(
                            out=v3(dst), in0=zb[:, :, 2:rl + 2], scalar=dc[:, 32 + oc:33 + oc],
                            in1=v3(uT[:]), op0=ALU.mult, op1=ALU.add), reads=["uT", zbn, "dc"], writes=[dn])
                    else:
                        op("dve", lambda e, zb=zb, oc=oc: e.scalar_tensor_tensor(
                            out=v3(sT[:]), in0=zb[:, :, 2:rl + 2], scalar=dc[:, 32 + oc:33 + oc],
                            in1=v3(uT[:]), op0=ALU.mult, op1=ALU.add), reads=["uT", zbn, "dc"], writes=["sT"])
                        if oc == 12:
                            op("act", lambda e: e.activation(out=lor[:, 0, :], in_=sT[:], func=AF.Tanh),
                               reads=["sT"], writes=["lor0"])
                        else:
                            op("act", lambda e: e.activation(out=lor[:, 1, :], in_=sT[:], func=AF.Copy),
                               reads=["sT"], writes=["lor1"])
                    if oc == 14:
                        op("act", lambda e: e.activation(out=tgl[:], in_=tgl[:], func=AF.Tanh, scale=0.5),
                           reads=["tgl"], writes=["tgl"])
                        op("dve", lambda e: e.tensor_scalar(out=lor[:, 2, :], in0=tgl[:], scalar1=0.5, scalar2=0.5,
                                                            op0=ALU.mult, op1=ALU.add), reads=["tgl"], writes=["lor2"])

                P.cutoff = p1_stage < 3
                for j in range(4):
                    apz, rn_ = zmm(19 + j)
                    op("act", lambda e, apz=apz: e.activation(out=gcs[:], in_=apz, func=AF.Copy), reads=[rn_], writes=["gcs"])
                    apz2, rn2 = zmm(23 + j)
                    op("dve", lambda e, apz2=apz2: e.tensor_tensor(out=up_[:, :, 1:rl + 1], in0=v3(apz2), in1=v3(gcs[:]),
                                                                   op=ALU.mult), reads=[rn2, "gcs"], writes=["upad"])
                    apz3, rn3 = zmm(15 + j)
                    op("act", lambda e, apz3=apz3: e.activation(out=gbs[:], in_=apz3, func=AF.Copy), reads=[rn3], writes=["gbs"])
                    op("act", lambda e, j=j: e.activation(out=v3(ct[0][:]), in_=up_[:, :, 0:rl], func=AF.Copy,
                                                          scale=vc[:, VR["cw0"] + j:VR["cw0"] + j + 1]),
                       reads=["upad", "vc"], writes=["ct0"])
                    op("dve", lambda e, j=j: e.scalar_tensor_tensor(out=v3(ct[1][:]), in0=up_[:, :, 1:rl + 1],
                                                                    scalar=vc[:, VR["cw1"] + j:VR["cw1"] + j + 1],
                                                                    in1=v3(ct[0][:]), op0=ALU.mult, op1=ALU.add),
                       reads=["upad", "ct0", "vc"], writes=["ct1"])
                    op("dve", lambda e, j=j: e.scalar_tensor_tensor(out=v3(ct[2][:]), in0=up_[:, :, 2:rl + 2],
                                                                    scalar=vc[:, VR["cw2"] + j:VR["cw2"] + j + 1],
                                                                    in1=v3(ct[1][:]), op0=ALU.mult, op1=ALU.add),
                       reads=["upad", "ct1", "vc"], writes=["ct2"])
                    op("dve", lambda e: e.tensor_tensor(out=ct[0][:], in0=ct[2][:], in1=gbs[:], op=ALU.mult),
                       reads=["ct2", "gbs"], writes=["ct0"])
                    op("act", lambda e, j=j: e.activation(out=sqc[:, j, :], in_=ct[0][:], func=AF.Square),
                       reads=["ct0"], writes=["sqc%d" % j])
                    op("dve", lambda e, j=j: e.tensor_scalar(out=ocb[:, j, :], in0=ct[0][:],
                                                             scalar1=vc[:, VR["cg"] + j:VR["cg"] + j + 1], scalar2=None,
                                                             op0=ALU.mult), reads=["ct0", "vc"], writes=["ocb%d" % j])
                for tl in range(2):
                    for j in range(4):
                        op("pe", lambda e, tl=tl, j=j: e.matmul(PB[1][:, 16 + tl:17 + tl], lhsT=sqc[:, j, tl * 128:tl * 128 + 128],
                                                                rhs=onescol[:, 0:1], start=(j == 0), stop=(j == 3)),
                           reads=["sqc%d" % j, "onescol"], writes=["pb1"])
                for tl in range(2):
                    ti = gi * 2 + tl
                    op("dve", lambda e, tl=tl: e.tensor_scalar(out=st1[:, 6 + tl:7 + tl], in0=PB[1][:, 16 + tl:17 + tl],
                                                               scalar1=1.0 / 512, scalar2=NORM_EPS, op0=ALU.mult, op1=ALU.add),
                       reads=["pb1"], writes=["st1d%d" % tl])
                    op("pool", lambda e, tl=tl, ti=ti: e.tensor_tensor(out=rcall[:, ti:ti + 1], in0=st1[:, 6 + tl:7 + tl],
                                                                       in1=nhalf[:, 0:1], op=ALU.pow),
                       reads=["st1d%d" % tl, "nhalf"], writes=["rcall"])
                    op("sp", lambda e, tl=tl, ti=ti: e.dma_start(
                        out=ocS[ti].rearrange("p (c t) -> p c t", c=4), in_=ocb[:, :, tl * 128:tl * 128 + 128]),
                       reads=["ocb%d" % j for j in range(4)], writes=["ocS"], dma="s_oc")

                P.cutoff = p1_stage < 4
                for c in range(4):
                    op("dve", lambda e, c=c: e.tensor_scalar(out=kr[:, c, :], in0=rk[:, 4 + c, :],
                                                             scalar1=vc[:, VR["kk"] + c:VR["kk"] + c + 1], scalar2=None,
                                                             op0=ALU.mult), reads=["rk%d" % (4 + c), "vc"], writes=["kr%d" % c])
                    op("act", lambda e, c=c: e.activation(out=sqk[:, c, :], in_=kr[:, c, :], func=AF.Square),
                       reads=["kr%d" % c], writes=["sqc%d" % c])
                    op("dve", lambda e, c=c: e.tensor_scalar(out=pr[:, c, :], in0=rk[:, c, :],
                                                             scalar1=vc[:, VR["rk"] + c:VR["rk"] + c + 1], scalar2=None,
                                                             op0=ALU.mult), reads=["rk%d" % c, "vc"], writes=["pr%d" % c])
                for tl in range(2):
                    for c in range(4):
                        op("pe", lambda e, tl=tl, c=c: e.matmul(PB[1][:, 32 + tl * 8:40 + tl * 8], lhsT=sqk[:, c, tl * 128:tl * 128 + 128],
                                                                rhs=ind8[:, c, :], start=(c == 0), stop=(c == 3)),
                           reads=["sqc%d" % c, "ind8"], writes=["pb1"])
                for tl in range(2):
                    op("dve", lambda e, tl=tl: e.tensor_scalar(out=st8[:, 0, :], in0=PB[1][:, 32 + tl * 8:40 + tl * 8],
                                                               scalar1=1e-12, scalar2=None, op0=ALU.add),
                       reads=["pb1"], writes=["st8a"])
                    op("pool", lambda e: e.tensor_tensor(out=st8[:, 1, :], in0=st8[:, 0, :], in1=nhalf[:, 0:8], op=ALU.pow),
                       reads=["st8a", "nhalf"], writes=["st8b"])
                    op("dve", lambda e: e.tensor_copy(out=rnx[:].rearrange("p (h j) -> p h j", h=8),
                                                      in_=bcast(st8[:, 1, :].unsqueeze(2), [128, 8, 64])),
                       reads=["st8b"], writes=["rnx"])
                    for c in range(4):
                        op("pe", lambda e, c=c: e.transpose(out=PB[0][:, c * 128:c * 128 + 128], in_=rnx[:, c * 128:c * 128 + 128],
                                                            identity=identf[:]), reads=["rnx", "identf"], writes=["pb0"])
                    op("dve", lambda e, tl=tl: e.tensor_tensor(out=kkt[:, :, tl * 128:tl * 128 + 128],
                                                               in0=PB[0][:, :].rearrange("p (c t) -> p c t", c=4),
                                                               in1=kr[:, :, tl * 128:tl * 128 + 128], op=ALU.mult),
                       reads=["pb0"] + ["kr%d" % c for c in range(4)], writes=["kkt%d" % tl])
                KKT = ["kkt0", "kkt1"]

                P.cutoff = p1_stage < 5
                op("dve", lambda e: e.memset(PB[1][:, 64:80], 0.0), writes=["pb1"])
                def a3b(d):
                    dr = slice(64 * d, 64 * d + 64)
                    for c in range(4):
                        lp = PB[0][:, 256:512]
                        op("pe", lambda e, c=c, dr=dr, lp=lp: e.matmul(lp, lhsT=ww2[dr, c * 128:c * 128 + 128], rhs=lor[dr, 0, :],
                                                                       start=True, stop=True),
                           reads=["ww2", "lor0"], writes=["pb0"])
                        op("act", lambda e, c=c, d=d, lp=lp: e.activation(out=tw[:], in_=lp, func=AF.Tanh, scale=0.5,
                                                                          bias=dc[:, d * 4 + c:d * 4 + c + 1]),
                           reads=["pb0", "dc"], writes=["tw"])
                        for ch in range(4):
                            if d == 0:
                                sl = slice(ch * 64, ch * 64 + 64)
                            else:
                                sl = slice(ch * 64 + 63, ch * 64 - 1 if ch > 0 else None, -1)
                            op("dve", lambda e, sl=sl: e.tensor_tensor_scan(out=Lc[:, sl], data0=tw[:, sl], data1=ones64[:],
                                                                            initial=0.0, op0=ALU.add, op1=ALU.add),
                               reads=["tw", "ones64"], writes=["Lc"])
                        op("dve", lambda e: e.scalar_tensor_tensor(out=nLx[:], in0=tw[:], scalar=1.0, in1=Lc[:],
                                                                   op0=ALU.add, op1=ALU.subtract), reads=["tw", "Lc"], writes=["nLx"])
                        op("act", lambda e: e.activation(out=Ein[:], in_=Lc[:], func=AF.Exp, scale=-KAPPA), reads=["Lc"], writes=["Ein"])
                        op("act", lambda e: e.activation(out=Einv[:], in_=Lc[:], func=AF.Exp, scale=KAPPA), reads=["Lc"], writes=["Einv"])
                        op("act", lambda e: e.activation(out=Eex[:], in_=nLx[:], func=AF.Exp, scale=KAPPA), reads=["nLx"], writes=["Eex"])
                        cs = 63 if d == 0 else 0
                        op("pool", lambda e, d=d, c=c, cs=cs, gi=gi: e.tensor_copy(
                            out=gC[d][:, c, gi * 4:gi * 4 + 4], in_=Ein[:].rearrange("p (ch t) -> p ch t", t=64)[:, :, cs]),
                           reads=["Ein"], writes=["gC%d" % d])
                        ap_ = PB[0][:, 0:256]
                        op("pe", lambda e, c=c, dr=dr, ap_=ap_: e.matmul(ap_, lhsT=wa2[dr, c * 128:c * 128 + 128], rhs=lor[dr, 1, :],
                                                                         start=True, stop=True),
                           reads=["wa2", "lor1"], writes=["pb0"])
                        op("act", lambda e, c=c, d=d, ap_=ap_: e.activation(out=ta[:], in_=ap_, func=AF.Tanh, scale=0.5,
                                                                            bias=dc[:, 8 + d * 4 + c:8 + d * 4 + c + 1]),
                           reads=["pb0", "dc"], writes=["ta"])
                        op("dve", lambda e, c=c: e.tensor_scalar(out=qq[:], in0=ta[:], scalar1=dc[:, 16 + c:17 + c],
                                                                 scalar2=dc[:, 20 + c:21 + c], op0=ALU.mult, op1=ALU.add),
                           reads=["ta", "dc"], writes=["qq"])
                        op("dve", lambda e, c=c: e.tensor_tensor(out=kd[:], in0=qq[:], in1=rk[:, 4 + c, :], op=ALU.mult),
                           reads=["qq", "rk%d" % (4 + c)], writes=["kd"])
                        op("pool", lambda e, c=c: e.tensor_tensor(out=prd[:], in0=kd[:], in1=pr[:, c, :], op=ALU.mult),
                           reads=["kd", "pr%d" % c], writes=["prd"])
                        for tl in range(2):
                            op("pe", lambda e, tl=tl, c=c, d=d: e.matmul(PB[1][:, 64 + tl * 8:72 + tl * 8],
                                                                         lhsT=prd[:, tl * 128:tl * 128 + 128], rhs=ind8[:, c, :],
                                                                         start=False, stop=(c == 3 and d == 1), skip_group_check=True),
                               reads=["prd", "ind8"], writes=["pb1"])
                        op("dve", lambda e, c=c: e.scalar_tensor_tensor(out=bq[:], in0=ta[:], scalar=1.0, in1=kkt[:, c, :],
                                                                        op0=ALU.add, op1=ALU.mult),
                           reads=["ta"] + KKT, writes=["bq"])
                        op("pool", lambda e, c=c, d=d: e.tensor_tensor(out=Rt[d][:, c, :], in0=rk[:, c, :], in1=Ein[:], op=ALU.mult),
                           reads=["rk%d" % c, "Ein"], writes=["Rt%d_%d" % (d, c)])
                        op("pool", lambda e, c=c, d=d: e.tensor_tensor(out=Kt[d][:, c, :], in0=kd[:], in1=Einv[:], op=ALU.mult),
                           reads=["kd", "Einv"], writes=["Kt%d_%d" % (d, c)])
                        op("dve", lambda e, c=c, d=d: e.scalar_tensor_tensor(out=Bt[d][:, c, :], in0=bq[:], scalar=0.5, in1=Einv[:],
                                                                             op0=ALU.mult, op1=ALU.mult),
                           reads=["bq", "Einv"], writes=["Bt%d_%d" % (d, c)])
                        op("dve", lambda e, c=c, d=d: e.scalar_tensor_tensor(out=At[d][:, c, :], in0=kkt[:, c, :], scalar=-1.0,
                                                                             in1=Eex[:], op0=ALU.mult, op1=ALU.mult),
                           reads=KKT + ["Eex"], writes=["At%d_%d" % (d, c)])
                        yield
                for _ in a3b(0):
                    pass

                P.cutoff = p1_stage < 6
                def fmres(nm, d):
                    return ["%s%d_%d" % (nm, d, c) for c in range(4)]

                def do_tile(tl):
                    ti = gi * 2 + tl
                    cols = slice(tl * 128, tl * 128 + 128)
                    def tm_entries(dirs, with_v):
                        ent = []
                        for d in dirs:
                            ent += [(At[d], Atm[d], fmres("At", d), "Atm%d" % d), (Bt[d], Btm[d], fmres("Bt", d), "Btm%d" % d),
                                    (Kt[d], Ktm[d], fmres("Kt", d), "Ktm%d" % d)]
                        if with_v:
                            ent.append((vbf, Vtm, ["vbf%d" % c for c in range(4)], "Vtm"))
                        return ent

                    def tm_transposes(entries):
                        for qi_, (src, dstt, srcn, dstn) in enumerate(entries):
                            half = qi_ % 2
                            for c in range(4):
                                op("pe", lambda e, src=src, c=c, half=half: e.transpose(
                                    out=pbf(0)[:, half * 512 + c * 128:half * 512 + c * 128 + 128], in_=src[:, c, cols], identity=identb[:]),
                                   reads=srcn + ["identb"], writes=["pb0"])
                            if qi_ % 2 == 0:
                                op("act", lambda e, dstt=dstt, half=half: e.activation(out=dstt[:], in_=pbf(0)[:, half * 512:half * 512 + 512],
                                                                                       func=AF.Copy), reads=["pb0"], writes=[dstn])
                            else:
                                op("dve", lambda e, dstt=dstt, half=half: e.tensor_copy(out=dstt[:], in_=pbf(0)[:, half * 512:half * 512 + 512]),
                                   reads=["pb0"], writes=[dstn])

                    def bv_part():
                        op("pool", lambda e: e.tensor_tensor(out=bvt[:].rearrange("p (h j) -> p h j", h=8),
                                                             in0=Vtm[:].rearrange("p (h j) -> p h j", h=8),
                                                             in1=bcast(bon[:, tl * 8:tl * 8 + 8].unsqueeze(2), [128, 8, 64]), op=ALU.mult),
                           reads=["Vtm", "bon"], writes=["bvt"])
                        op("sp", lambda e: e.dma_start(out=bvS[ti], in_=bvt[:]), reads=["bvt"], writes=["bvS"], dma="s_bv")

                    def g_part():
                        op("pe", lambda e: e.matmul(PB[3][:, :], lhsT=lor[:, 2, cols], rhs=wg2[:], start=True, stop=True),
                           reads=["lor2", "wg2"], writes=["pb3"])
                        op("act", lambda e: e.activation(out=gtm[:], in_=PB[3][:, :], func=AF.Copy), reads=["pb3"], writes=["gtm"])
                        op("sp", lambda e: e.dma_start(out=gS[ti], in_=gtm[:]), reads=["gtm"], writes=["gS"], dma="s_g")
                        op("dve", lambda e: e.memset(PB[2][:, :], 0.0), writes=["pb2"])
                        op("dve", lambda e: e.memset(PB[3][:, :], 0.0), writes=["pb3"])

                    first_y = {}
                    def do_unit(d, p):
                        if d == 0:
                            masks = {"ab": mSL, "abT": mSU, "akT": mSU, "rbT": mUI, "rkT": mUI}
                        else:
                            masks = {"ab": mSU, "abT": mSL, "akT": mSL, "rbT": mLI, "rkT": mLI}
                        if True:
                            pb_ = 64 * p
                            ypsn = "pb%d" % (2 + p)
                            ub = UB[(d, p)]
                            un = lambda s, p=p, d=d: "u%d%d_%s" % (d, p, "akT" if s == "P0" else s)
                            wcol, ucol = pb_, 64 - pb_
                            prs = slice(pb_, pb_ + 64)
                            P.cutoff = p1_stage < 7.05
                            prods = (("ab", At, Bt, "At", "Bt"), ("abT", Bt, At, "Bt", "At"), ("akT", Kt, At, "Kt", "At"),
                                     ("rbT", Bt, Rt, "Bt", "Rt"), ("rkT", Kt, Rt, "Kt", "Rt"))
                            for (nm, Lm, Rm, ln_, rn__) in prods:
                                bk = next_ubank()
                                for c in range(4):
                                    op("pe", lambda e, bk=bk, c=c, Lm=Lm, Rm=Rm, d=d, prs=prs: e.matmul(
                                        PB[bk][:, c * 128:c * 128 + 128], lhsT=Lm[d][prs, c, cols], rhs=Rm[d][prs, c, cols],
                                        start=True, stop=True), reads=fmres(ln_, d) + fmres(rn__, d), writes=["pb%d" % bk])
                                mk = masks[nm]
                                op("dve", lambda e, bk=bk, nm=nm, ub=ub, mk=mk: e.tensor_tensor(
                                    out=ub[nm][:], in0=PB[bk][:, :].rearrange("p (c t) -> p c t", c=4),
                                    in1=bcast(mk[:].unsqueeze(1), [128, 4, 128]), op=ALU.mult),
                                   reads=["pb%d" % bk, mk.name], writes=[un(nm)])
                                yield
                            P.cutoff = p1_stage < 7.15
                            bk = next_ubank()
                            for c in range(4):
                                h = 2 * c + p
                                op("pe", lambda e, bk=bk, c=c, h=h, ub=ub: e.matmul(
                                    PB[bk][:, c * 64:c * 64 + 64], lhsT=ub["akT"][:, c, :], rhs=Vtm[:, h * 64:h * 64 + 64],
                                    start=True, stop=True), reads=[un("akT"), "Vtm"], writes=["pb%d" % bk])
                            op("act", lambda e, bk=bk, ub=ub, ucol=ucol: e.activation(
                                out=ub["X0"][:, :, ucol:ucol + 64], in_=PB[bk][:, 0:256].rearrange("p (c i) -> p c i", c=4),
                                func=AF.Copy), reads=["pb%d" % bk], writes=[un("X0")])
                            op("act", lambda e, ub=ub, wcol=wcol, d=d, p=p: e.activation(
                                out=ub["X0"][:, :, wcol:wcol + 64],
                                in_=Atm[d][:].rearrange("p (c q j) -> p c q j", c=4, q=2)[:, :, p, :], func=AF.Copy),
                               reads=["Atm%d" % d], writes=[un("X0")])
                            yield
                            P.cutoff = p1_stage < 7.25
                            Pk = [ub["ab"], ub["P1"], ub["P0"]]
                            PTk = [ub["abT"], ub["PT1"], ub["PT0"]]
                            Pn = [un("ab"), un("P1"), un("P0")]
                            PTn = [un("abT"), un("PT1"), un("PT0")]

                            def lvl(k):
                                return 0 if k == 0 else 1 + ((k - 1) % 2)
                            Xs = [ub["X0"], ub["X1"]]
                            Xn = [un("X0"), un("X1")]
                            for k in range(6):
                                a_, b_ = lvl(k), lvl(k + 1)
                                bk = next_ubank()
                                for c in range(4):
                                    op("pe", lambda e, bk=bk, c=c, a_=a_, k=k, PTk=PTk, Xs=Xs: e.matmul(
                                        PB[bk][:, c * 128:c * 128 + 128], lhsT=PTk[a_][:, c, :], rhs=Xs[k % 2][:, c, :],
                                        start=True, stop=True), reads=[PTn[a_], Xn[k % 2]], writes=["pb%d" % bk])
                                op("dve", lambda e, bk=bk, k=k, Xs=Xs: e.tensor_tensor(
                                    out=Xs[(k + 1) % 2][:], in0=PB[bk][:, :].rearrange("p (c t) -> p c t", c=4), in1=Xs[k % 2][:],
                                    op=ALU.add), reads=["pb%d" % bk, Xn[k % 2]], writes=[Xn[(k + 1) % 2]])
                                if k < 5:
                                    bk = next_ubank()
                                    for c in range(4):
                                        op("pe", lambda e, bk=bk, c=c, a_=a_, Pk=Pk, PTk=PTk: e.matmul(
                                            PB[bk][:, c * 128:c * 128 + 128], lhsT=Pk[a_][:, c, :], rhs=PTk[a_][:, c, :],
                                            start=True, stop=True), reads=[Pn[a_], PTn[a_]], writes=["pb%d" % bk])
                                    bk2 = None
                                    if k < 4:
                                        bk2 = next_ubank()
                                        for c in range(4):
                                            op("pe", lambda e, bk2=bk2, c=c, a_=a_, Pk=Pk, PTk=PTk: e.matmul(
                                                PB[bk2][:, c * 128:c * 128 + 128], lhsT=PTk[a_][:, c, :], rhs=Pk[a_][:, c, :],
                                                start=True, stop=True), reads=[Pn[a_], PTn[a_]], writes=["pb%d" % bk2])
                                    op("act", lambda e, bk=bk, b_=b_, PTk=PTk: e.activation(
                                        out=PTk[b_][:], in_=PB[bk][:, :].rearrange("p (c t) -> p c t", c=4), func=AF.Copy),
                                       reads=["pb%d" % bk], writes=[PTn[b_]])
                                    if bk2 is not None:
                                        op("act", lambda e, bk2=bk2, b_=b_, Pk=Pk: e.activation(
                                            out=Pk[b_][:], in_=PB[bk2][:, :].rearrange("p (c t) -> p c t", c=4), func=AF.Copy),
                                           reads=["pb%d" % bk2], writes=[Pn[b_]])
                                yield
                            X6, X6n = Xs[0], Xn[0]
                            P.cutoff = p1_stage < 7.35
                            for ch in range(2):
                                bk = next_ubank()
                                chr_ = slice(ch * 64, ch * 64 + 64)
                                for c in range(4):
                                    h = 2 * c + p
                                    op("pe", lambda e, bk=bk, ch=ch, c=c, h=h, chr_=chr_, X6=X6, d=d: e.matmul(
                                        PB[bk][:, c * 64:c * 64 + 64], lhsT=X6[chr_, c, :],
                                        rhs=Btm[d][chr_, h * 64:h * 64 + 64], start=True, stop=True),
                                       reads=[X6n, "Btm%d" % d], writes=["pb%d" % bk])
                                op("act", lambda e, bk=bk, d=d, prs=prs, pb_=pb_, ch=ch: e.activation(
                                    out=mtT[d][prs, ch, :, pb_:pb_ + 64],
                                    in_=PB[bk][prs, 0:256].rearrange("p (c j) -> p c j", c=4), func=AF.Copy),
                                   reads=["pb%d" % bk], writes=["mtT%d_%d" % (d, p)])
                                yield
                            P.cutoff = p1_stage < 7.45
                            for ch in range(2):
                                bk = next_ubank()
                                chr_ = slice(ch * 64, ch * 64 + 64)
                                for c in range(4):
                                    h = 2 * c + p
                                    o_ = PB[bk][:, c * 64:c * 64 + 64]
                                    op("pe", lambda e, o_=o_, c=c, chr_=chr_, X6=X6, d=d, ucol=ucol: e.matmul(
                                        o_, lhsT=Btm[d][chr_, c * 128:c * 128 + 128], rhs=X6[chr_, c, ucol:ucol + 64],
                                        start=True, stop=False), reads=[X6n, "Btm%d" % d], writes=["pb%d" % bk])
                                    op("pe", lambda e, o_=o_, c=c, h=h, chr_=chr_, d=d: e.matmul(
                                        o_, lhsT=Ktm[d][chr_, c * 128:c * 128 + 128], rhs=Vtm[chr_, h * 64:h * 64 + 64],
                                        start=False, stop=True), reads=["Ktm%d" % d, "Vtm"], writes=["pb%d" % bk])
                                op("act", lambda e, bk=bk, d=d, prs=prs, ch=ch: e.activation(
                                    out=nnT[d][prs, ch, :, :], in_=PB[bk][prs, 0:256].rearrange("p (c i) -> p c i", c=4), func=AF.Copy),
                                   reads=["pb%d" % bk], writes=["nnT%d_%d" % (d, p)])
                                yield
                            P.cutoff = p1_stage < 7.55
                            bk = next_ubank()
                            for c in range(4):
                                op("pe", lambda e, bk=bk, c=c, X6=X6, ub=ub: e.matmul(
                                    PB[bk][:, c * 128:c * 128 + 128], lhsT=X6[:, c, :], rhs=ub["rbT"][:, c, :], start=True, stop=True),
                                   reads=[X6n, un("rbT")], writes=["pb%d" % bk])
                            for ch in range(2):
                                op("dve", lambda e, bk=bk, ch=ch, d=d, prs=prs: e.tensor_tensor(
                                    out=qtT[d][prs, ch, :, ch * 64:ch * 64 + 64],
                                    in0=PB[bk][prs, :].rearrange("p (c t) -> p c t", c=4)[:, :, ch * 64:ch * 64 + 64],
                                    in1=Rt[d][prs, :, tl * 128 + ch * 64:tl * 128 + ch * 64 + 64], op=ALU.add),
                                   reads=["pb%d" % bk] + fmres("Rt", d), writes=["qtT%d_%d" % (d, p)])
                            yield
                            P.cutoff = p1_stage < 7.65
                            for c in range(4):
                                h = 2 * c + p
                                yo = PB[2 + p][:, h * 64:h * 64 + 64]
                                st_ = (d == 0)
                                op("pe", lambda e, yo=yo, c=c, X6=X6, ub=ub, ucol=ucol, st_=st_: e.matmul(
                                    yo, lhsT=ub["rbT"][:, c, :], rhs=X6[:, c, ucol:ucol + 64], start=False, stop=False, skip_group_check=True),
                                   reads=[un("rbT"), X6n], writes=[ypsn])
                                op("pe", lambda e, yo=yo, c=c, h=h, ub=ub, d=d: e.matmul(
                                    yo, lhsT=ub["rkT"][:, c, :], rhs=Vtm[:, h * 64:h * 64 + 64], start=False, stop=(d == 1), skip_group_check=True),
                                   reads=[un("rkT"), "Vtm"], writes=[ypsn])
                            P.cutoff = p1_stage < 7.75
                            if d == 0:
                                for ch in range(2):
                                    gch = ti * 2 + ch
                                    bk = next_ubank()
                                    for c in range(4):
                                        h = 2 * c + p
                                        op("pe", lambda e, c=c, h=h, ch=ch, prs=prs, p=p: e.matmul(
                                            PB[2 + p][:, h * 64:h * 64 + 64], lhsT=qtT[0][prs, ch, c, :], rhs=Tbf[0][prs, c, :],
                                            start=False, stop=False, skip_group_check=True), reads=["qtT0_%d" % p, "Tbf0_%d" % p], writes=[ypsn])
                                        op("pe", lambda e, bk=bk, c=c, ch=ch, prs=prs: e.matmul(
                                            PB[bk][:, c * 64:c * 64 + 64], lhsT=mtT[0][prs, ch, c, :], rhs=Tbf[0][prs, c, :],
                                            start=True, stop=True), reads=["mtT0_%d" % p, "Tbf0_%d" % p], writes=["pb%d" % bk])
                                    op("dve", lambda e, bk=bk, ch=ch, prs=prs: e.tensor_tensor(
                                        out=ctmp[prs, :, :], in0=PB[bk][prs, 0:256].rearrange("p (c i) -> p c i", c=4),
                                        in1=nnT[0][prs, ch, :, :], op=ALU.add), reads=["pb%d" % bk, "nnT0_%d" % p], writes=["ctmp%d" % p])
                                    op("dve", lambda e, prs=prs: e.tensor_tensor(out=ctmp[prs, :, :], in0=ctmp[prs, :, :],
                                                                                 in1=T32[0][prs, :, :], op=ALU.add),
                                       reads=["ctmp%d" % p, "T32_0_%d" % p], writes=["ctmp%d" % p])
                                    op("dve", lambda e, prs=prs, gch=gch: e.tensor_tensor(
                                        out=Tbf[0][prs, :, :], in0=ctmp[prs, :, :],
                                        in1=bcast(gC[0][prs, :, gch:gch + 1], [64, 4, 64]), op=ALU.mult),
                                       reads=["ctmp%d" % p, "gC0"], writes=["Tbf0_%d" % p])
                                    op("pool", lambda e, prs=prs, gch=gch: e.tensor_tensor(
                                        out=T32[0][prs, :, :], in0=ctmp[prs, :, :],
                                        in1=bcast(gC[0][prs, :, gch:gch + 1], [64, 4, 64]), op=ALU.mult),
                                       reads=["ctmp%d" % p, "gC0"], writes=["T32_0_%d" % p])
                                    yield
                    def run(gens, extra=None):
                        while gens or extra is not None:
                            for g_ in list(gens):
                                try:
                                    next(g_)
                                except StopIteration:
                                    gens.remove(g_)
                            if extra is not None:
                                try:
                                    next(extra)
                                except StopIteration:
                                    extra = None

                    if tl == 0:
                        tm_transposes(tm_entries([0], True))
                        g_part()
                        UBANKS[:] = [4, 5, 6, 7]
                        run([do_unit(0, 0), do_unit(0, 1)], extra=a3b(1))
                        UBANKS[:] = [4, 5, 6, 7, 0, 1]
                        op("dve", lambda e: e.tensor_copy(out=bon[:], in_=PB[1][:, 64:80]), reads=["pb1"], writes=["bon"])
                        tm_transposes(tm_entries([1], False))
                        bv_part()
                        run([do_unit(1, 0), do_unit(1, 1)])
                    else:
                        tm_transposes(tm_entries([0, 1], True))
                        bv_part()
                        g_part()
                        run([do_unit(0, 0), do_unit(0, 1), do_unit(1, 0), do_unit(1, 1)])
                    P.cutoff = p1_stage < 8
                    for p in range(2):
                        op("act", lambda e, p=p: e.activation(
                            out=ysum[:].rearrange("p (c q j) -> p c q j", c=4, q=2)[:, :, p, :],
                            in_=PB[2 + p][:, :].rearrange("p (c q j) -> p c q j", c=4, q=2)[:, :, p, :], func=AF.Copy),
                           reads=["pb%d" % (2 + p)], writes=["ysum"])
                    op("sp", lambda e, ti=ti: e.dma_start(out=ysS[ti], in_=ysum[:]), reads=["ysum"], writes=["ysS"], dma="s_ys")
                    op("sp", lambda e, ti=ti: e.dma_start(out=mtS[ti], in_=mtT[1][:].rearrange("p a c j -> p (a c j)")),
                       reads=["mtT1_0", "mtT1_1"], writes=["mtS"], dma="s_mt")
                    op("sp", lambda e, ti=ti: e.dma_start(out=nnS[ti], in_=nnT[1][:].rearrange("p a c j -> p (a c j)")),
                       reads=["nnT1_0", "nnT1_1"], writes=["nnS"], dma="s_nn")
                    op("sp", lambda e, ti=ti: e.dma_start(out=qtS[ti], in_=qtT[1][:].rearrange("p a c j -> p (a c j)")),
                       reads=["qtT1_0", "qtT1_1"], writes=["qtS"], dma="s_qt")

                for tl in range(2):
                    do_tile(tl)
                if not isS:
                    sq_ = gi - 16
                    for c in range(4):
                        op("pe", lambda e, c=c: e.transpose(out=PB[0][0:64, c * 128:c * 128 + 128], in_=T32[0][:, c, :], identity=identf[:]),
                           reads=["T32_0_0", "T32_0_1", "identf"], writes=["pb0"])
                    for c in range(4):
                        op("dve", lambda e, c=c: e.tensor_copy(out=otr[:, :], in_=PB[0][0:64, c * 128:c * 128 + 128]),
                           reads=["pb0"], writes=["otr"])
                        op("sp", lambda e, c=c, sq_=sq_: e.dma_start(
                            out=ns_d[0, sq_, 2 * c:2 * c + 2].rearrange("q i j -> i q j"),
                            in_=otr[:, :].rearrange("i (q j) -> i q j", q=2)), reads=["otr"], writes=["ns"], dma="s_ns")
                if gi >= 15:
                    op("pool", lambda e: e.memset(T32[0][:], 0.0), writes=["T32_0_0", "T32_0_1"])
                    op("pool", lambda e: e.memset(Tbf[0][:], 0.0), writes=["Tbf0_0", "Tbf0_1"])
            for gi in range(p1_groups):
                do_group(gi)
            build_and_emit(nc, P, bar)

        if stop_after < 2:
            return nc
        with ExitStack() as es:
            P = Prog()
            op = P.op
            woutb = sb(es, "woutb", [128, 8, D], BF16)
            wff1b = sb(es, "wff1b", [128, 8, 4 * D], BF16)
            wff2b = sb(es, "wff2b", [128, 32, D], BF16)
            gateA = sb(es, "gateA", [128, D], F32)
            gateF = sb(es, "gateF", [128, D], F32)
            lnwb = sb(es, "lnwb", [128, 512], F32)
            lnbb = sb(es, "lnbb", [128, 512], F32)
            fgb = sb(es, "fgb", [128, D], F32)
            op("sp", lambda e: e.dma_start(out=lnwb[:], in_=bcast(lnw_d[0:1, :], [128, 512])), writes=["lnwb"], dma="l2")
            op("sp", lambda e: e.dma_start(out=lnbb[:], in_=bcast(lnb_d[0:1, :], [128, 512])), writes=["lnbb"], dma="l3")
            op("sp", lambda e: e.dma_start(out=fgb[:], in_=bcast(fg_d[0:1, :], [128, D])), writes=["fgb"], dma="l4")
            xt2 = sb(es, "xt2", [128, D], F32)
            x1b = [sb(es, "x1b%d" % i, [128, D], F32) for i in range(2)]
            xs2 = sb(es, "xs2", [128, D], BF16)
            junk2 = xs2
            h2Tb = [sb(es, "h2Tb%d" % i, [128, 8, 128], BF16) for i in range(2)]
            afm = sb(es, "afm", [128, 16, 128], BF16)
            rl_ = [sb(es, "rl%d" % i, [128, 512], BF16) for i in range(2)]
            mtL = sb(es, "mtL", [128, 2, 4, 128], BF16)
            nnL = sb(es, "nnL", [128, 2, 4, 64], F32)
            qtL = sb(es, "qtL", [128, 2, 4, 128], BF16)
            ysL = sb(es, "ysL", [128, 512], F32)
            gL = sb(es, "gL", [128, 512], BF16)
            bvL = sb(es, "bvL", [128, 512], BF16)
            ocL = sb(es, "ocL", [128, 4, 128], BF16)
            yn = sb(es, "yn", [128, 512], F32)
            yy = ysL
            ysq = yn
            otm = sb(es, "otm", [128, 512], BF16)
            ofm = sb(es, "ofm", [128, 4, 128], BF16)
            s8 = sb(es, "s8", [128, 6, 8], F32)
            st2 = sb(es, "st2", [128, 8], F32)
            ctmp2 = sb(es, "ctmp2", [128, 4, 64], F32)
            otr2 = ctmp2[0:64, 0:2, :].rearrange("p a i -> p (a i)")

            for kc in range(8):
                op("pool", lambda e, kc=kc: e.dma_start(out=woutb[:, kc, :], in_=wout_d[kc * 128:kc * 128 + 128, :]),
                   writes=["woutb"], dma="wo")
                op("pool", lambda e, kc=kc: e.dma_start(out=wff1b[:, kc, :], in_=wff1_d[kc * 128:kc * 128 + 128, :]),
                   writes=["wff1b"], dma="w1")
            for kc in range(32):
                op("pool", lambda e, kc=kc: e.dma_start(out=wff2b[:, kc, :], in_=wff2_d[kc * 128:kc * 128 + 128, :]),
                   writes=["wff2b"], dma="w2")

            def front(ti):
                isS = ti < NSAMP_T
                m = 0 if isS else 1
                if (not isS) and (ti % 2 == 1):
                    op("pool", lambda e: e.memset(T32[1][:], 0.0), writes=["T32_1_0", "T32_1_1"])
                    op("pool", lambda e: e.memset(Tbf[1][:], 0.0), writes=["Tbf1_0", "Tbf1_1"])
                if ti == NSAMP_T - 1:
                    op("dve", lambda e: e.tensor_copy(out=T32[1][:], in_=T32i[:]), writes=["T32_1_0", "T32_1_1"])
                    op("act", lambda e: e.activation(out=Tbf[1][:], in_=T32i[:], func=AF.Copy), writes=["Tbf1_0", "Tbf1_1"])
                op("sp", lambda e, ti=ti: e.dma_start(out=mtL[:].rearrange("p a c j -> p (a c j)"), in_=mtS[ti]), writes=["mtL"], dma="r_mt")
                op("sp", lambda e, ti=ti: e.dma_start(out=nnL[:].rearrange("p a c j -> p (a c j)"), in_=nnS[ti]), writes=["nnL"], dma="r_nn")
                op("sp", lambda e, ti=ti: e.dma_start(out=qtL[:].rearrange("p a c j -> p (a c j)"), in_=qtS[ti]), writes=["qtL"], dma="r_qt")
                op("sp", lambda e, ti=ti: e.dma_start(out=ysL[:], in_=ysS[ti]), writes=["ysL"], dma="r_ys")
                op("sp", lambda e, ti=ti: e.dma_start(out=gL[:], in_=gS[ti]), writes=["gL"], dma="r_g")
                op("sp", lambda e, ti=ti: e.dma_start(out=bvL[:], in_=bvS[ti]), writes=["bvL"], dma="r_bv")
                op("sp", lambda e, ti=ti: e.dma_start(out=ocL[:].rearrange("p c t -> p (c t)"), in_=ocS[ti]), writes=["ocL"], dma="r_oc")
                op("dve", lambda e: e.memset(PB[0][:, :], 0.0), writes=["pb0"])
                op("dve", lambda e: e.memset(PB[3][:, :], 0.0), writes=["pb3"])
                for p in range(2):
                    prs = slice(64 * p, 64 * p + 64)
                    for ch in (1, 0):
                        gch = ti * 2 + ch
                        for c in range(4):
                            h = 2 * c + p
                            op("pe", lambda e, c=c, h=h, ch=ch, prs=prs, p=p: e.matmul(
                                PB[3 * p][:, h * 64:h * 64 + 64], lhsT=qtL[prs, ch, c, :], rhs=Tbf[1][prs, c, :],
                                start=False, stop=(ch == 0), skip_group_check=True), reads=["qtL", "Tbf1_%d" % p], writes=["pb%d" % (3 * p)])
                            op("pe", lambda e, c=c, ch=ch, prs=prs, p=p: e.matmul(
                                PB[1][:, c * 64:c * 64 + 64], lhsT=mtL[prs, ch, c, :], rhs=Tbf[1][prs, c, :],
                                start=True, stop=True), reads=["mtL", "Tbf1_%d" % p], writes=["pb1"])
                        op("dve", lambda e, ch=ch, prs=prs, p=p: e.tensor_tensor(
                            out=ctmp2[prs, :, :], in0=PB[1][prs, 0:256].rearrange("p (c i) -> p c i", c=4),
                            in1=nnL[prs, ch, :, :], op=ALU.add), reads=["pb1", "nnL"], writes=["ctmp2_%d" % p])
                        op("dve", lambda e, prs=prs: e.tensor_tensor(out=ctmp2[prs, :, :], in0=ctmp2[prs, :, :],
                                                                     in1=T32[1][prs, :, :], op=ALU.add),
                           reads=["ctmp2_%d" % p, "T32_1_%d" % p], writes=["ctmp2_%d" % p])
                        op("dve", lambda e, prs=prs, gch=gch: e.tensor_tensor(
                            out=Tbf[1][prs, :, :], in0=ctmp2[prs, :, :], in1=bcast(gC[1][prs, :, gch:gch + 1], [64, 4, 64]),
                            op=ALU.mult), reads=["ctmp2_%d" % p], writes=["Tbf1_%d" % p])
                        op("pool", lambda e, prs=prs, gch=gch: e.tensor_tensor(
                            out=T32[1][prs, :, :], in0=ctmp2[prs, :, :], in1=bcast(gC[1][prs, :, gch:gch + 1], [64, 4, 64]),
                            op=ALU.mult), reads=["ctmp2_%d" % p], writes=["T32_1_%d" % p])
                        yield
                if (not isS) and (ti % 2 == 0):
                    sq_ = (ti - NSAMP_T) // 2
                    for c in range(4):
                        op("pe", lambda e, c=c: e.transpose(out=PB[1][0:64, c * 128:c * 128 + 128], in_=T32[1][:, c, :], identity=identf[:]),
                           reads=["T32_1_0", "T32_1_1", "identf"], writes=["pb1"])
                    for c in range(4):
                        op("dve", lambda e, c=c: e.tensor_copy(out=otr2[:, :], in_=PB[1][0:64, c * 128:c * 128 + 128]),
                           reads=["pb1"], writes=["otr2", "ctmp2_0", "ctmp2_1"])
                        op("sp", lambda e, c=c, sq_=sq_: e.dma_start(
                            out=ns_d[1, sq_, 2 * c:2 * c + 2].rearrange("q i j -> i q j"),
                            in_=otr2[:, :].rearrange("i (q j) -> i q j", q=2)), reads=["otr2", "ctmp2_0", "ctmp2_1"], writes=["ns"], dma="s_ns")
                for p in range(2):
                    op("dve", lambda e, p=p: e.tensor_tensor(
                        out=yy[:].rearrange("p (c q j) -> p c q j", c=4, q=2)[:, :, p, :],
                        in0=PB[3 * p][:, :].rearrange("p (c q j) -> p c q j", c=4, q=2)[:, :, p, :],
                        in1=ysL[:].rearrange("p (c q j) -> p c q j", c=4, q=2)[:, :, p, :], op=ALU.add),
                       reads=["pb%d" % (3 * p), "ysL"], writes=["ysL"])
                yield
                y3 = lambda t: t[:].rearrange("p (h j) -> p h j", h=8)
                op("dve", lambda e: e.tensor_reduce(out=s8[:, 0, :], in_=y3(yy), axis=AX.X, op=ALU.add), reads=["ysL"], writes=["s8a"])
                op("dve", lambda e: e.tensor_tensor(out=ysq[:], in0=yy[:], in1=yy[:], op=ALU.mult), reads=["ysL"], writes=["yn"])
                op("dve", lambda e: e.tensor_reduce(out=s8[:, 1, :], in_=y3(ysq), axis=AX.X, op=ALU.add), reads=["yn"], writes=["s8b"])
                op("dve", lambda e: e.tensor_scalar(out=s8[:, 2, :], in0=s8[:, 0, :], scalar1=1.0 / 64, scalar2=None, op0=ALU.mult),
                   reads=["s8a"], writes=["s8c"])
                op("dve", lambda e: e.tensor_tensor(out=s8[:, 3, :], in0=s8[:, 2, :], in1=s8[:, 2, :], op=ALU.mult),
                   reads=["s8c"], writes=["s8d"])
                op("dve", lambda e: e.scalar_tensor_tensor(out=s8[:, 4, :], in0=s8[:, 1, :], scalar=1.0 / 64, in1=s8[:, 3, :],
                                                           op0=ALU.mult, op1=ALU.subtract), reads=["s8b", "s8d"], writes=["s8e"])
                op("dve", lambda e: e.tensor_scalar(out=s8[:, 4, :], in0=s8[:, 4, :], scalar1=GN_EPS, scalar2=None, op0=ALU.add),
                   reads=["s8e"], writes=["s8e"])
                op("pool", lambda e: e.tensor_tensor(out=s8[:, 5, :], in0=s8[:, 4, :], in1=nhalf[:, 0:8], op=ALU.pow),
                   reads=["s8e", "nhalf"], writes=["s8f"])
                yield
                op("dve", lambda e: e.tensor_tensor(out=y3(yn), in0=y3(yy), in1=bcast(s8[:, 2, :].unsqueeze(2), [128, 8, 64]),
                                                    op=ALU.subtract), reads=["ysL", "s8c"], writes=["yn"])
                op("dve", lambda e: e.tensor_tensor(out=y3(yn), in0=y3(yn), in1=bcast(s8[:, 5, :].unsqueeze(2), [128, 8, 64]),
                                                     op=ALU.mult), reads=["yn", "s8f"], writes=["yn"])
                op("dve", lambda e: e.tensor_tensor(out=yn[:], in0=yn[:], in1=lnwb[:], op=ALU.mult), reads=["yn", "lnwb"], writes=["yn"])
                op("dve", lambda e: e.tensor_tensor(out=yn[:], in0=yn[:], in1=lnbb[:], op=ALU.add), reads=["yn", "lnbb"], writes=["yn"])
                op("dve", lambda e: e.tensor_tensor(out=yn[:], in0=yn[:], in1=bvL[:], op=ALU.add), reads=["yn", "bvL"], writes=["yn"])
                op("dve", lambda e: e.tensor_tensor(out=otm[:], in0=yn[:], in1=gL[:], op=ALU.mult), reads=["yn", "gL"], writes=["otm"])
                for c in range(4):
                    op("pe", lambda e, c=c: e.transpose(out=pbf(3)[:, c * 128:c * 128 + 128], in_=otm[:, c * 128:c * 128 + 128],
                                                        identity=identb[:]), reads=["otm", "identb"], writes=["pb3"])
                op("act", lambda e: e.activation(out=ofm[:], in_=pbf(3)[:, 0:512].rearrange("p (c t) -> p c t", c=4), func=AF.Copy),
                   reads=["pb3"], writes=["ofm"])
                yield
                if ti == NTILE - 1 or ti == NSAMP_T - 1:
                    op("sp", lambda e, m=m: e.dma_start(out=gateA[:], in_=gatesS[:, m * 2 * D:m * 2 * D + D]),
                       writes=["gateA"], dma="l5")
                op("sp", lambda e, ti=ti: e.dma_start(out=xt2[:], in_=x_d[ti * 128:ti * 128 + 128, :]), writes=["xt2"], dma="r_x")
                x1 = x1b[ti % 2]
                x1n = "x1b%d" % (ti % 2)
                h2T = h2Tb[ti % 2]
                h2n = "h2Tb%d" % (ti % 2)
                for half in range(2):
                    for kc in range(4):
                        op("pe", lambda e, half=half, kc=kc: e.matmul(PB[0][:, :], lhsT=ofm[:, kc, :],
                                                                      rhs=woutb[:, kc, half * 512:half * 512 + 512],
                                                                      start=(kc == 0), stop=(kc == 3)),
                           reads=["ofm", "woutb"], writes=["pb0"])
                    for kc in range(4):
                        op("pe", lambda e, half=half, kc=kc: e.matmul(PB[3][:, :], lhsT=ocL[:, kc, :],
                                                                      rhs=woutb[:, 4 + kc, half * 512:half * 512 + 512],
                                                                      start=(kc == 0), stop=(kc == 3)),
                           reads=["ocL", "woutb"], writes=["pb3"])
                    hs = slice(half * 512, half * 512 + 512)
                    op("act", lambda e, hs=hs, x1=x1: e.activation(out=x1[:, hs], in_=PB[0][:, :], func=AF.Copy),
                       reads=["pb0"], writes=[x1n])
                    op("dve", lambda e, hs=hs, ti=ti, x1=x1: e.scalar_tensor_tensor(
                        out=x1[:, hs], in0=PB[3][:, :], scalar=rcall[:, ti:ti + 1], in1=x1[:, hs],
                        op0=ALU.mult, op1=ALU.add), reads=["pb3", x1n], writes=[x1n])
                    op("dve", lambda e, hs=hs, x1=x1: e.tensor_tensor(out=x1[:, hs], in0=x1[:, hs], in1=gateA[:, hs], op=ALU.mult),
                       reads=[x1n, "gateA"], writes=[x1n])
                    op("dve", lambda e, hs=hs, x1=x1: e.tensor_tensor(out=x1[:, hs], in0=x1[:, hs], in1=xt2[:, hs], op=ALU.add),
                       reads=[x1n, "xt2"], writes=[x1n])
                    yield
                op("act", lambda e, x1=x1: e.activation(out=xs2[:], in_=x1[:], func=AF.Square, accum_out=st2[:, 0:1]),
                   reads=[x1n], writes=["xs2", "st2a"])
                op("dve", lambda e: e.tensor_scalar(out=st2[:, 1:2], in0=st2[:, 0:1], scalar1=1.0 / D, scalar2=NORM_EPS,
                                                    op0=ALU.mult, op1=ALU.add), reads=["st2a"], writes=["st2b"])
                op("pool", lambda e: e.tensor_tensor(out=st2[:, 2:3], in0=st2[:, 1:2], in1=nhalf[:, 0:1], op=ALU.pow),
                   reads=["st2b", "nhalf"], writes=["st2c"])
                op("act", lambda e, x1=x1: e.activation(out=xs2[:], in_=x1[:], func=AF.Copy, scale=st2[:, 2:3]),
                   reads=[x1n, "st2c"], writes=["xs2"])
                yield
                for kc in range(8):
                    op("pe", lambda e, kc=kc: e.transpose(out=pbf(1)[:, kc * 128:kc * 128 + 128], in_=xs2[:, kc * 128:kc * 128 + 128],
                                                          identity=identb[:]), reads=["xs2", "identb"], writes=["pb1"])
                for kc in range(8):
                    op("act" if kc % 2 else "dve", (lambda e, kc=kc, m=m, h2T=h2T: e.activation(
                        out=h2T[:, kc, :], in_=pbf(1)[:, kc * 128:kc * 128 + 128], func=AF.Identity,
                        scale=modc[:, m, 2, kc:kc + 1], bias=modc[:, m, 3, kc:kc + 1])) if kc % 2 else
                       (lambda e, kc=kc, m=m, h2T=h2T: e.tensor_scalar(
                           out=h2T[:, kc, :], in0=pbf(1)[:, kc * 128:kc * 128 + 128], scalar1=modc[:, m, 2, kc:kc + 1],
                           scalar2=modc[:, m, 3, kc:kc + 1], op0=ALU.mult, op1=ALU.add)),
                       reads=["pb1", "modc"], writes=[h2n])
                yield

            def back(ti):
                isS = ti < NSAMP_T
                m = 0 if isS else 1
                x1 = x1b[ti % 2]
                x1n = "x1b%d" % (ti % 2)
                h2T = h2Tb[ti % 2]
                h2n = "h2Tb%d" % (ti % 2)
                if ti == NTILE - 1 or ti == NSAMP_T - 1:
                    op("sp", lambda e, m=m: e.dma_start(out=gateF[:], in_=gatesS[:, m * 2 * D + D:(m + 1) * 2 * D]),
                       writes=["gateF"], dma="l6")
                def tr_block(cb):
                    c4 = cb % 4
                    r_ = rl_[cb % 2]
                    for i4 in range(4):
                        op("pe", lambda e, r_=r_, i4=i4: e.transpose(out=pbf(2)[:, i4 * 128:i4 * 128 + 128],
                                                                     in_=r_[:, i4 * 128:i4 * 128 + 128], identity=identb[:]),
                           reads=[r_.name, "identb"], writes=["pb2"])
                    if cb % 2:
                        op("act", lambda e, c4=c4: e.activation(out=afm[:, c4 * 4:c4 * 4 + 4, :],
                                                                in_=pbf(2)[:, 0:512].rearrange("p (c t) -> p c t", c=4), func=AF.Copy),
                           reads=["pb2"], writes=["afm%d" % c4])
                    else:
                        op("dve", lambda e, c4=c4: e.tensor_copy(out=afm[:, c4 * 4:c4 * 4 + 4, :],
                                                                 in_=pbf(2)[:, 0:512].rearrange("p (c t) -> p c t", c=4)),
                           reads=["pb2"], writes=["afm%d" % c4])

                for hf in range(2):
                    for c4 in range(4):
                        cb = hf * 4 + c4
                        bank = (6, 7)[cb % 2]
                        for kc in range(8):
                            op("pe", lambda e, kc=kc, cb=cb, bank=bank, h2T=h2T: e.matmul(
                                PB[bank][:, :], lhsT=h2T[:, kc, :], rhs=wff1b[:, kc, cb * 512:cb * 512 + 512],
                                start=(kc == 0), stop=(kc == 7)), reads=["wff1b", h2n], writes=["pb%d" % bank])
                        r_ = rl_[cb % 2]
                        op("act", lambda e, bank=bank, r_=r_: e.activation(out=r_[:], in_=PB[bank][:, :], func=AF.Relu),
                           reads=["pb%d" % bank], writes=[r_.name])
                        op("dve", lambda e, r_=r_: e.tensor_tensor(out=r_[:], in0=r_[:], in1=r_[:], op=ALU.mult),
                           reads=[r_.name], writes=[r_.name])
                        if c4 > 0:
                            tr_block(cb - 1)
                        yield
                    tr_block(hf * 4 + 3)
                    for half in range(2):
                        for o16 in range(16):
                            kc = hf * 16 + o16
                            op("pe", lambda e, half=half, kc=kc, o16=o16: e.matmul(PB[4 + half][:, :], lhsT=afm[:, o16, :],
                                                                                   rhs=wff2b[:, kc, half * 512:half * 512 + 512],
                                                                                   start=(kc == 0), stop=(kc == 31)),
                               reads=["afm%d" % (o16 // 4), "wff2b"], writes=["pb%d" % (4 + half)])
                        yield
                for half in range(2):
                    hs = slice(half * 512, half * 512 + 512)
                    op("dve", lambda e, half=half, hs=hs: e.tensor_tensor(out=PB[4 + half][:, :], in0=PB[4 + half][:, :],
                                                                          in1=gateF[:, hs], op=ALU.mult),
                       reads=["pb%d" % (4 + half), "gateF"], writes=["pb%d" % (4 + half)])
                    op("dve", lambda e, half=half, hs=hs, x1=x1: e.tensor_tensor(out=x1[:, hs], in0=PB[4 + half][:, :], in1=x1[:, hs], op=ALU.add),
                       reads=["pb%d" % (4 + half), x1n], writes=[x1n])
                op("act", lambda e, x1=x1: e.activation(out=afm[:].rearrange("p a t -> p (a t)")[:, 0:D], in_=x1[:], func=AF.Square,
                                                        accum_out=st2[:, 3:4]),
                   reads=[x1n], writes=["afm0", "afm1", "st2d"])
                op("dve", lambda e: e.tensor_scalar(out=st2[:, 4:5], in0=st2[:, 3:4], scalar1=1.0 / D, scalar2=NORM_EPS,
                                                    op0=ALU.mult, op1=ALU.add), reads=["st2d"], writes=["st2e"])
                op("pool", lambda e: e.tensor_tensor(out=st2[:, 5:6], in0=st2[:, 4:5], in1=nhalf[:, 0:1], op=ALU.pow),
                   reads=["st2e", "nhalf"], writes=["st2f"])
                op("dve", lambda e, x1=x1: e.scalar_tensor_tensor(out=x1[:], in0=x1[:], scalar=st2[:, 5:6], in1=fgb[:],
                                                                  op0=ALU.mult, op1=ALU.mult), reads=[x1n, "st2f", "fgb"], writes=[x1n])
                op("sp", lambda e, ti=ti, x1=x1: e.dma_start(out=y_d[ti * 128:ti * 128 + 128, :], in_=x1[:]), reads=[x1n], writes=["y"], dma="s_y")
                yield
            gF = front(NTILE - 1)
            for _ in gF:
                pass
            for ti in range(NTILE - 1, -1, -1):
                gB = back(ti)
                gF = front(ti - 1) if ti > 0 else None
                nb = 0
                while gB is not None or gF is not None:
                    if gB is not None:
                        try:
                            next(gB)
                            nb += 1
                        except StopIteration:
                            gB = None
                    if gF is not None and (gB is None or nb % 1 == 0):
                        try:
                            next(gF)
                        except StopIteration:
                            gF = None
            build_and_emit(nc, P, bar)
    return nc


_NC_CACHE = {}


def kernel(x_prompt, x_sample, c, state_rwkv_fwd, state_rwkv_bwd, c_ctx, w_mod, b_mod,
           norm1_g, w_in, mu_shift, w0_f, w_w2_f, w0_b, w_w2_b, a0_f, w_a2_f, a0_b, w_a2_b,
           w_g2, k_k, k_a, r_k, ln_x_w, ln_x_b, conv_w, conv_gain, w_out, norm2_g,
           w_ff1, w_ff2, final_g):
    f = lambda a: np.ascontiguousarray(np.asarray(a, dtype=np.float32))
    x_prompt, x_sample, c, c_ctx = f(x_prompt), f(x_sample), f(c), f(c_ctx)
    sf, sbk = f(state_rwkv_fwd), f(state_rwkv_bwd)
    bm = f(b_mod)[0]
    rows = []

    def addv(v):
        v = f(v).reshape(-1)
        rows.append(v.reshape(-1, 128))
    addv(mu_shift[0])
    for v in (w0_f, w0_b, a0_f, a0_b, k_k, k_a, r_k):
        addv(v[0])
    cw = f(conv_w)[0]
    addv(cw[0]); addv(cw[1]); addv(cw[2])
    addv(conv_gain[0]); addv(norm1_g[0]); addv(norm2_g[0])
    addv(bm[0:D]); addv(bm[D:2 * D]); addv(bm[3 * D:4 * D]); addv(bm[4 * D:5 * D])
    vr = np.concatenate(rows, axis=0)
    assert vr.shape[0] == NVR
    vecs = np.zeros((128, 128), np.float32)
    vecs[:NVR] = vr
    common = {
        "vecs": vecs, "w_mod": f(w_mod)[0], "b_mod": f(b_mod), "w_in": f(w_in)[0],
        "w_w2": np.concatenate([f(w_w2_f)[0], f(w_w2_b)[0]], axis=0),
        "w_a2": np.concatenate([f(w_a2_f)[0], f(w_a2_b)[0]], axis=0),
        "w_g2": f(w_g2)[0], "ln_w": f(ln_x_w), "ln_b": f(ln_x_b), "final_g": f(final_g).reshape(1, D),
        "w_out": f(w_out)[0], "w_ff1": f(w_ff1)[0], "w_ff2": f(w_ff2)[0],
    }
    in_maps = []
    for i in range(NCORES):
        xm = np.concatenate([x_sample[i], x_prompt[4 * i:4 * i + 4].reshape(1024, D)], axis=0)
        mm = dict(common)
        mm["x"] = np.ascontiguousarray(xm)
        mm["c2"] = np.ascontiguousarray(np.stack([c[i], c_ctx], axis=0))
        mm["st"] = np.ascontiguousarray(np.stack([sf[i, 0], sbk[i, 0]], axis=0))
        in_maps.append(mm)
    if "nc" not in _NC_CACHE:
        _NC_CACHE["nc"] = build_program()
    nc = _NC_CACHE["nc"]
    res = run_bass_kernel_spmd(nc, in_maps, core_ids=list(range(NCORES)))
    y_prompt = np.zeros((32, 256, D), np.float32)
    y_sample = np.zeros((8, 4096, D), np.float32)
    nsf = np.zeros((32, 1, 8, 64, 64), np.float32)
    nsb = np.zeros((32, 1, 8, 64, 64), np.float32)
    for i in range(NCORES):
        r = res.results[i]
        y_sample[i] = r["y"][:4096]
        y_prompt[4 * i:4 * i + 4] = r["y"][4096:].reshape(4, 256, D)
        nsf[4 * i:4 * i + 4, 0] = r["ns"][0]
        nsb[4 * i:4 * i + 4, 0] = r["ns"][1]
    return (y_prompt, y_sample, nsf, nsb)
```

```python
import numpy as np
from contextlib import ExitStack
import concourse.bass as bass
import concourse.mybir as mybir
from concourse.ap import AP
from concourse.bass_utils import run_bass_kernel_spmd

F32 = mybir.dt.float32
BF16 = mybir.dt.bfloat16
AF = mybir.ActivationFunctionType
ALU = mybir.AluOpType
AX = mybir.AxisListType

ENG_NAMES = ("pe", "act", "dve", "pool", "sp")
NCORES = 8
NTOK = 5120
NTILE = 40
NSAMP_T = 32
D = 1024
KAPPA = 0.5 * float(np.exp(-0.5))
NORM_EPS = 1e-6
GN_EPS = 64 * 1e-5


class _Op:
    __slots__ = ("eng", "fn", "deps", "dma", "milestone", "need", "known_after")

    def __init__(self, eng, fn, dma):
        self.eng = eng
        self.fn = fn
        self.deps = []
        self.dma = dma
        self.milestone = None
        self.need = False
        self.known_after = None


class Prog:
    def __init__(self):
        self.ops = {e: [] for e in ENG_NAMES}
        self.last_writer = {}
        self.readers = {}

    cutoff = False

    def op(self, eng, fn, reads=(), writes=(), dma=None):
        if self.cutoff:
            return None
        pbr = [r for r in reads if isinstance(r, str) and r.startswith("pb")]
        if pbr:
            reads = [r for r in reads if r not in pbr]
            writes = list(writes) + [r for r in pbr if r not in writes]
        o = _Op(eng, fn, dma)
        seen = set()

        def add(d, raw):
            if d is None or id(d) in seen:
                return
            if d.dma is None and d.eng == eng and dma is None:
                if eng == "pe" or not raw:
                    return
            seen.add(id(d))
            o.deps.append(d)

        for r in reads:
            add(self.last_writer.get(r), True)
        for r in writes:
            add(self.last_writer.get(r), False)
            for rd in self.readers.get(r, ()):
                add(rd, False)
        self.ops[eng].append(o)
        for r in writes:
            self.last_writer[r] = o
            self.readers[r] = []
        for r in reads:
            if r not in writes:
                self.readers.setdefault(r, []).append(o)
        return o


def build_and_emit(nc, prog, bar):
    with ExitStack() as es:
        eng_sem = {e: es.enter_context(nc.semaphore("s_" + e)) for e in ENG_NAMES}
        for e in ENG_NAMES:
            for o in prog.ops[e]:
                for d in o.deps:
                    d.need = True
        dma_keys, seen = [], set()
        for e in ENG_NAMES:
            for o in prog.ops[e]:
                if o.dma is not None and o.dma not in seen:
                    seen.add(o.dma)
                    dma_keys.append(o.dma)
        dma_sem = {k: es.enter_context(nc.semaphore("d_%d" % i)) for i, k in enumerate(dma_keys)}
        dma_cnt = {k: 0 for k in dma_keys}
        final = {}
        for e in ENG_NAMES:
            m = 0
            for o in prog.ops[e]:
                if o.dma is not None:
                    dma_cnt[o.dma] += 1
                    o.milestone = 16 * dma_cnt[o.dma]
                elif o.need:
                    m += 1
                    o.milestone = m
            final[e] = m + 1
        pending = {e: 0 for e in ENG_NAMES}
        known = {e: {} for e in ENG_NAMES}
        waits = {}
        progress = True
        while progress:
            progress = False
            for e in ENG_NAMES:
                lst = prog.ops[e]
                kn = known[e]
                while pending[e] < len(lst):
                    o = lst[pending[e]]
                    if any(d.known_after is None for d in o.deps):
                        break
                    w = []
                    for d in o.deps:
                        key = ("dma", d.dma) if d.dma is not None else ("eng", d.eng)
                        if kn.get(key, 0) >= d.milestone:
                            continue
                        w.append(d)
                        for k2, v2 in d.known_after.items():
                            if kn.get(k2, 0) < v2:
                                kn[k2] = v2
                        if kn.get(key, 0) < d.milestone:
                            kn[key] = d.milestone
                    waits[id(o)] = w
                    ka = dict(kn)
                    if o.dma is None and o.milestone is not None:
                        ka[("eng", e)] = o.milestone
                    o.known_after = ka
                    pending[e] += 1
                    progress = True
        for e in ENG_NAMES:
            assert pending[e] == len(prog.ops[e]), "dependency cycle on " + e

        def run_engine(e, engobj):
            for o in prog.ops[e]:
                for d in waits[id(o)]:
                    s = dma_sem[d.dma] if d.dma is not None else eng_sem[d.eng]
                    engobj.wait_ge(s, d.milestone)
                ins = o.fn(engobj)
                if o.dma is not None:
                    ins.then_inc(dma_sem[o.dma], 16)
                elif o.milestone is not None:
                    ins.then_inc(eng_sem[e], 1)
            if e != "sp":
                bar[e](engobj).then_inc(eng_sem[e], 1)
            for k in dma_keys:
                engobj.wait_ge(dma_sem[k], 16 * dma_cnt[k])
            for e2 in ENG_NAMES:
                if e2 != "sp" and e2 != e:
                    engobj.wait_ge(eng_sem[e2], final[e2])

        with nc.Block() as block:
            @block.tensor
            def _(eng):
                run_engine("pe", eng)

            @block.scalar
            def _(eng):
                run_engine("act", eng)

            @block.vector
            def _(eng):
                run_engine("dve", eng)

            @block.gpsimd
            def _(eng):
                run_engine("pool", eng)

            @block.sync
            def _(eng):
                run_engine("sp", eng)


VR = {}
_r = 0
for _n, _c in (("mu", 15), ("w0f", 4), ("w0b", 4), ("a0f", 4), ("a0b", 4), ("kk", 4), ("ka", 4),
               ("rk", 4), ("cw0", 4), ("cw1", 4), ("cw2", 4), ("cg", 4), ("g1", 8), ("g2", 8),
               ("bsha", 8), ("bsca", 8), ("bshf", 8), ("bscf", 8)):
    VR[_n] = _r
    _r += _c
NVR = _r


def bcast(ap, shape):
    return ap.broadcast_to(list(shape))


def build_program(dbg=False, stop_after=9, p1_groups=20, p1_stage=99, p0_stage=99):
    nc = bass.Bass("TRN2", target_bir_lowering=False)

    def din(name, shape, dt=F32):
        return nc.dram_tensor(name, list(shape), dt, kind="ExternalInput").ap()

    x_d = din("x", [NTOK, D])
    c2_d = din("c2", [2, D])
    st_d = din("st", [2, 8, 64, 64])
    vecs_d = din("vecs", [128, 128])
    wmod_d = din("w_mod", [D, 6 * D])
    bmod_d = din("b_mod", [1, 6 * D])
    win_d = din("w_in", [D, 3456])
    ww2_d = din("w_w2", [128, 512])
    wa2_d = din("w_a2", [128, 512])
    wg2_d = din("w_g2", [128, 512])
    lnw_d = din("ln_w", [1, 512])
    lnb_d = din("ln_b", [1, 512])
    fg_d = din("final_g", [1, D])
    wout_d = din("w_out", [D, D])
    wff1_d = din("w_ff1", [D, 4 * D])
    wff2_d = din("w_ff2", [4 * D, D])
    y_d = nc.dram_tensor("y", [NTOK, D], F32, kind="ExternalOutput").ap()
    ns_d = nc.dram_tensor("ns", [2, 4, 8, 64, 64], F32, kind="ExternalOutput").ap()

    def dscr(name, shape, dt):
        return nc.dram_tensor(name, list(shape), dt, kind="Internal").ap()

    mtS = dscr("mtS", [NTILE, 128, 1024], BF16)
    nnS = dscr("nnS", [NTILE, 128, 512], F32)
    qtS = dscr("qtS", [NTILE, 128, 1024], BF16)
    ysS = dscr("ysS", [NTILE, 128, 512], F32)
    gS = dscr("gS", [NTILE, 128, 512], BF16)
    bvS = dscr("bvS", [NTILE, 128, 512], BF16)
    ocS = dscr("ocS", [NTILE, 128, 512], BF16)
    gatesS = dscr("gatesS", [128, 4 * D], F32)
    wcvS = dscr("wcvS", [12, 128, 8, 128], BF16)

    with ExitStack() as es0:
        def sb(es, name, shape, dt):
            return es.enter_context(nc.sbuf_tensor(name, list(shape), dt))

        def pst(es, name, shape, dt):
            return es.enter_context(nc.psum_tensor(name, list(shape), dt))

        identb = sb(es0, "identb", [128, 128], BF16)
        identf = sb(es0, "identf", [128, 128], F32)
        ind8 = sb(es0, "ind8", [128, 4, 8], BF16)
        onescol = sb(es0, "onescol", [128, 1], BF16)
        nhalf = sb(es0, "nhalf", [128, 8], F32)
        vc = sb(es0, "vc", [128, 128], F32)
        dc = sb(es0, "dc", [128, 64], F32)
        modc = sb(es0, "modc", [128, 2, 4, 8], F32)
        T32 = [sb(es0, "T32_%d" % d, [128, 4, 64], F32) for d in range(2)]
        Tbf = [sb(es0, "Tbf_%d" % d, [128, 4, 64], BF16) for d in range(2)]
        T32i = sb(es0, "T32i", [128, 4, 64], F32)
        gC = [None, sb(es0, "gC_1", [128, 4, 80], F32)]
        rcall = sb(es0, "rcall", [128, NTILE], F32)
        bart = sb(es0, "bart", [128, 8], F32)
        barb = sb(es0, "barb", [128, 8], BF16)
        PB = [pst(es0, "pb%d" % i, [128, 512], F32) for i in range(8)]

        def pbf(i):
            return PB[i][:].bitcast(BF16)

        bar = {
            "pe": lambda e: e.matmul(PB[7][0:8, 0:8], lhsT=barb[0:8, 0:8], rhs=barb[0:8, 0:8], start=True, stop=True),
            "act": lambda e: e.activation(out=bart[:, 0:1], in_=bart[:, 1:2], func=AF.Copy),
            "dve": lambda e: e.tensor_copy(out=bart[:, 2:3], in_=bart[:, 3:4]),
            "pool": lambda e: e.tensor_copy(out=bart[:, 4:5], in_=bart[:, 5:6]),
        }

        with ExitStack() as es:
            P = Prog()
            op = P.op
            vraw = sb(es, "vraw", [128, 128], F32)
            c2T = sb(es, "c2T", [128, 8, 2], F32)
            c2raw = sb(es, "c2raw", [2, D], F32)
            th0 = sb(es, "th0", [128, 8, 2], F32)
            silu = sb(es, "silu", [128, 8, 2], F32)
            srep = sb(es, "srep", [128, 2, 8, 128], F32)
            wms = [sb(es, "wms%d" % i, [128, 8, 512], F32) for i in range(2)]
            bmb = sb(es, "bmb", [128, 2, D], F32)
            straw = sb(es, "straw", [64, 2, 4, 128], F32)
            modp = sb(es, "modp", [128, 32, 2], F32)
            gates = sb(es, "gates", [128, 2, 2, D], F32)
            wtmp0 = [sb(es, "wtmp0_%d" % i, [128, 1536], BF16) for i in range(8)]
            for kc in range(8):
                op("pool", lambda e, kc=kc: e.dma_start(out=wtmp0[kc][:], in_=win_d[kc * 128:kc * 128 + 128, 1920:3456]),
                   writes=["wtmp0_%d" % kc], dma="wt%d" % kc)

            op("pool", lambda e: e.memset(bart[:], 0.0), writes=["bart"])
            op("pool", lambda e: e.memset(barb[:], 0.0), writes=["barb"])
            op("pool", lambda e: e.memset(identf[:], 1.0), writes=["identf"])
            op("pool", lambda e: e.affine_select(out=identf[:], in_=identf[:], pattern=[[-1, 128]],
                                                 compare_op=ALU.is_equal, fill=0.0, base=0, channel_multiplier=1),
               reads=["identf"], writes=["identf"])
            op("pool", lambda e: e.tensor_copy(out=identb[:], in_=identf[:]), reads=["identf"], writes=["identb"])
            op("pool", lambda e: e.memset(ind8[:], 0.0), writes=["ind8"])
            for c in range(4):
                for p in range(2):
                    op("pool", lambda e, c=c, p=p: e.memset(ind8[64 * p:64 * p + 64, c, 2 * c + p:2 * c + p + 1], 1.0),
                       reads=["ind8"], writes=["ind8"])
            op("pool", lambda e: e.memset(onescol[:], 1.0), writes=["onescol"])
            op("pool", lambda e: e.memset(nhalf[:], -0.5), writes=["nhalf"])
            op("pool", lambda e: e.memset(gC[1][:], 1.0), writes=["gC1"])
            op("pool", lambda e: e.memset(rcall[:], 1.0), writes=["rcall"])

            P.cutoff = p0_stage < 1
            op("sp", lambda e: e.dma_start(out=vraw[:], in_=vecs_d[:, :]), writes=["vraw"], dma="l0")
            op("sp", lambda e: e.dma_start(out=c2raw[:], in_=c2_d[:, :]), writes=["c2raw"], dma="l1")
            op("sp", lambda e: e.dma_start(out=bmb[:, 0, :], in_=bcast(bmod_d[0:1, 2 * D:3 * D], [128, D])),
               writes=["bmb0"], dma="l5")
            op("sp", lambda e: e.dma_start(out=bmb[:, 1, :], in_=bcast(bmod_d[0:1, 5 * D:6 * D], [128, D])),
               writes=["bmb1"], dma="l6")
            for d in range(2):
                op("sp", lambda e, d=d: e.dma_start(
                    out=straw[:, d, :, :].rearrange("i c (p j) -> i c p j", p=2),
                    in_=st_d[d].rearrange("(c p) i j -> i c p j", p=2)), writes=["straw%d" % d], dma="l7%d" % d)

            P.cutoff = p0_stage < 2
            op("pe", lambda e: e.transpose(out=PB[0][:, 0:128], in_=vraw[:], identity=identf[:]),
               reads=["vraw", "identf"], writes=["pb0"])
            op("dve", lambda e: e.tensor_copy(out=vc[:], in_=PB[0][:, 0:128]), reads=["pb0"], writes=["vc"])
            op("dve", lambda e: e.tensor_scalar(out=dc[:, 0:8], in0=vc[:, VR["w0f"]:VR["w0f"] + 8], scalar1=0.5,
                                                scalar2=None, op0=ALU.mult), reads=["vc"], writes=["dc"])
            op("dve", lambda e: e.tensor_scalar(out=dc[:, 8:16], in0=vc[:, VR["a0f"]:VR["a0f"] + 8], scalar1=0.5,
                                                scalar2=None, op0=ALU.mult), reads=["vc"], writes=["dc"])
            op("dve", lambda e: e.tensor_scalar(out=dc[:, 16:20], in0=vc[:, VR["ka"]:VR["ka"] + 4], scalar1=0.5,
                                                scalar2=None, op0=ALU.mult), reads=["vc"], writes=["dc"])
            op("dve", lambda e: e.tensor_scalar(out=dc[:, 20:24], in0=vc[:, VR["ka"]:VR["ka"] + 4], scalar1=-0.5,
                                                scalar2=1.0, op0=ALU.mult, op1=ALU.add), reads=["vc"], writes=["dc"])
            op("dve", lambda e: e.tensor_scalar(out=dc[:, 32:47], in0=vc[:, VR["mu"]:VR["mu"] + 15], scalar1=0.5,
                                                scalar2=None, op0=ALU.mult), reads=["vc"], writes=["dc"])
            op("dve", lambda e: e.tensor_scalar(out=dc[:, 48:63], in0=vc[:, VR["mu"]:VR["mu"] + 15], scalar1=-1.0,
                                                scalar2=1.0, op0=ALU.mult, op1=ALU.add), reads=["vc"], writes=["dc"])
            P.cutoff = p0_stage < 3
            for d in range(2):
                for c in range(4):
                    op("pe", lambda e, d=d, c=c: e.transpose(out=PB[1][:, c * 64:c * 64 + 64], in_=straw[:, d, c, :],
                                                             identity=identf[0:64, 0:64]),
                       reads=["straw%d" % d, "identf"], writes=["pb1"])
                op("dve", lambda e, d=d: e.tensor_copy(out=T32[d][:], in_=PB[1][:, 0:256].rearrange("p (c i) -> p c i", c=4)),
                   reads=["pb1"], writes=["T32_%d" % d])
                op("act", lambda e, d=d: e.activation(out=Tbf[d][:], in_=T32[d][:], func=AF.Copy),
                   reads=["T32_%d" % d], writes=["Tbf_%d" % d])
                if d == 1:
                    op("dve", lambda e: e.tensor_copy(out=T32i[:], in_=T32[1][:]), reads=["T32_1"], writes=["T32i"])
            P.cutoff = p0_stage < 4
            for kc in range(8):
                op("pe", lambda e, kc=kc: e.transpose(out=PB[5][:, kc * 2:kc * 2 + 2], in_=c2raw[0:2, kc * 128:kc * 128 + 128],
                                                      identity=identf[0:2, 0:2]), reads=["c2raw", "identf"], writes=["pb5"])
            op("dve", lambda e: e.tensor_copy(out=c2T[:], in_=PB[5][:, 0:16].rearrange("p (k m) -> p k m", m=2)),
               reads=["pb5"], writes=["c2T"])
            op("act", lambda e: e.activation(out=th0[:], in_=c2T[:], func=AF.Tanh, scale=0.5), reads=["c2T"], writes=["th0"])
            op("dve", lambda e: e.tensor_scalar(out=th0[:], in0=th0[:], scalar1=0.5, scalar2=0.5, op0=ALU.mult, op1=ALU.add),
               reads=["th0"], writes=["th0"])
            op("dve", lambda e: e.tensor_tensor(out=silu[:], in0=th0[:], in1=c2T[:], op=ALU.mult),
               reads=["th0", "c2T"], writes=["silu"])
            for m in range(2):
                op("dve", lambda e, m=m: e.tensor_copy(out=srep[:, m, :, :], in_=bcast(silu[:, :, m:m + 1], [128, 8, 128])),
                   reads=["silu"], writes=["srep%d" % m])
            P.cutoff = p0_stage < 5
            for b in range(12):
                q, half = b // 2, b % 2
                w = wms[b % 2]
                wn = "wms%d" % (b % 2)
                op("sp", lambda e, b=b, w=w: e.dma_start(out=w[:], in_=wmod_d[:, 512 * b:512 * b + 512].rearrange(
                    "(kc p) n -> p kc n", p=128)), writes=[wn], dma="wm%d" % (b % 2))
                if 2 <= b < 10:
                    kcw = b - 2
                    op("sp", lambda e, kcw=kcw: e.dma_start(out=wcvS[:, :, kcw, :].rearrange("o p n -> p o n"),
                                                            in_=wtmp0[kcw][:].rearrange("p (o n) -> p o n", o=12)),
                       reads=["wtmp0_%d" % kcw], writes=["wcvS"], dma="wts")
                if q in (2, 5):
                    gi = 0 if q == 2 else 1
                    for m in range(2):
                        bank = 2 + m
                        for kc in range(8):
                            op("pe", lambda e, m=m, kc=kc, w=w, bank=bank: e.matmul(
                                PB[bank][:, :], lhsT=srep[:, m, kc, :], rhs=w[:, kc, :], start=(kc == 0), stop=(kc == 7)),
                               reads=[wn, "srep%d" % m], writes=["pb%d" % bank])
                        op("dve", lambda e, m=m, gi=gi, half=half, bank=bank: e.tensor_tensor(
                            out=gates[:, m, gi, 512 * half:512 * half + 512], in0=PB[bank][:, :],
                            in1=bmb[:, gi, 512 * half:512 * half + 512], op=ALU.add),
                           reads=["pb%d" % bank, "bmb%d" % gi], writes=["gates"])
                else:
                    qi = {0: 0, 1: 1, 3: 2, 4: 3}[q]
                    for cc in range(4):
                        col = qi * 8 + half * 4 + cc
                        for kc in range(8):
                            op("pe", lambda e, kc=kc, cc=cc, col=col, w=w: e.matmul(
                                PB[4][:, 2 * col:2 * col + 2], lhsT=w[:, kc, cc * 128:cc * 128 + 128], rhs=silu[:, kc, :],
                                start=(kc == 0), stop=(kc == 7)), reads=[wn, "silu"], writes=["pb4"])
            P.cutoff = p0_stage < 6
            op("dve", lambda e: e.tensor_copy(out=modp[:], in_=PB[4][:, 0:64].rearrange("p (c m) -> p c m", m=2)),
               reads=["pb4"], writes=["modp"])
            for m in range(2):
                for (dst, qsc, bname, gname) in ((0, 1, "bsca", "g1"), (2, 3, "bscf", "g2")):
                    op("dve", lambda e, m=m, dst=dst, qsc=qsc, bname=bname: e.scalar_tensor_tensor(
                        out=modc[:, m, dst, :], in0=modp[:, qsc * 8:qsc * 8 + 8, m], scalar=1.0,
                        in1=vc[:, VR[bname]:VR[bname] + 8], op0=ALU.add, op1=ALU.add),
                       reads=["modp", "vc"], writes=["modc"])
                    op("dve", lambda e, m=m, dst=dst, gname=gname: e.tensor_tensor(
                        out=modc[:, m, dst, :], in0=modc[:, m, dst, :], in1=vc[:, VR[gname]:VR[gname] + 8], op=ALU.mult),
                       reads=["modc", "vc"], writes=["modc"])
                for (dst, qsh, bname) in ((1, 0, "bsha"), (3, 2, "bshf")):
                    op("dve", lambda e, m=m, dst=dst, qsh=qsh, bname=bname: e.tensor_tensor(
                        out=modc[:, m, dst, :], in0=modp[:, qsh * 8:qsh * 8 + 8, m], in1=vc[:, VR[bname]:VR[bname] + 8],
                        op=ALU.add), reads=["modp", "vc"], writes=["modc"])
            P.cutoff = p0_stage < 7
            op("sp", lambda e: e.dma_start(out=gatesS[:, :], in_=gates[:].rearrange("p a b n -> p (a b n)")),
               reads=["gates"], writes=["gatesS"], dma="sg")
            build_and_emit(nc, P, bar)

        G = 256
        if stop_after < 1:
            return nc
        with ExitStack() as es:
            P = Prog()
            op = P.op
            winb = sb(es, "winb", [128, 8, 1920], BF16)
            wcv = [sb(es, "wcv%d" % i, [128, 8, 128], BF16) for i in range(4)]
            ww2 = sb(es, "ww2", [128, 512], BF16)
            wa2 = sb(es, "wa2", [128, 512], BF16)
            wg2 = sb(es, "wg2", [128, 512], BF16)
            op("pool", lambda e: e.dma_start(out=ww2[:], in_=ww2_d[:, :]), writes=["ww2"], dma="g0")
            op("pool", lambda e: e.dma_start(out=wa2[:], in_=wa2_d[:, :]), writes=["wa2"], dma="g1")
            op("pool", lambda e: e.dma_start(out=wg2[:], in_=wg2_d[:, :]), writes=["wg2"], dma="g2")
            gC[0] = sb(es, "gC_0", [128, 4, 80], F32)
            ones64 = sb(es, "ones64", [128, 64], F32)
            op("pool", lambda e: e.memset(gC[0][:], 1.0), writes=["gC0"])
            op("pool", lambda e: e.memset(ones64[:], 1.0), writes=["ones64"])
            mSL = sb(es, "mSL", [128, 128], F32)
            mSU = sb(es, "mSU", [128, 128], F32)
            mLI = sb(es, "mLI", [128, 128], F32)
            mUI = sb(es, "mUI", [128, 128], F32)
            for (mt_, pat, cm, cmp_, zr) in ((mSL, -1, 1, ALU.is_gt, (64, 0)), (mLI, -1, 1, ALU.is_ge, (64, 0)),
                                            (mSU, 1, -1, ALU.is_gt, (0, 64)), (mUI, 1, -1, ALU.is_ge, (0, 64))):
                nm = mt_.name
                op("pool", lambda e, t=mt_: e.memset(t[:], 1.0), writes=[nm])
                op("pool", lambda e, t=mt_, pat=pat, cm=cm, cmp_=cmp_: e.affine_select(
                    out=t[:], in_=t[:], pattern=[[pat, 128]], compare_op=cmp_, fill=0.0, base=0,
                    channel_multiplier=cm), reads=[nm], writes=[nm])
                op("pool", lambda e, t=mt_, zr=zr: e.memset(t[zr[0]:zr[0] + 64, zr[1]:zr[1] + 64], 0.0),
                   reads=[nm], writes=[nm])
            xt0_ = sb(es, "xt0", [128, D], F32)
            xt = [xt0_, xt0_]
            xs = [sb(es, "xs%d" % i, [128, D], BF16) for i in range(2)]
            st1 = sb(es, "st1", [128, 8], F32)
            hT = sb(es, "hT", [128, 8, G], BF16)
            zpS = [sb(es, "zpS%d" % i, [128, 4, 66], F32) for i in range(2)]
            zpP = [sb(es, "zpP%d" % i, [128, 1, 258], F32) for i in range(2)]
            upS = sb(es, "upS", [128, 4, 66], F32)
            upP = sb(es, "upP", [128, 1, 258], F32)
            sT = sb(es, "sT", [128, G], F32)
            uT = sb(es, "uT", [128, G], F32)
            rk = sb(es, "rk", [128, 8, G], F32)
            vbf = sb(es, "vbf", [128, 4, G], BF16)
            lor = sb(es, "lor", [128, 3, G], BF16)
            tgl = sb(es, "tgl", [128, G], F32)
            gcs = sb(es, "gcs", [128, G], F32)
            gbs = sb(es, "gbs", [128, G], F32)
            ct = [sb(es, "ct%d" % i, [128, G], F32) for i in range(3)]
            sqc = sb(es, "sqc", [128, 4, G], BF16)
            ocb = sb(es, "ocb", [128, 4, G], BF16)
            kr = sb(es, "kr", [128, 4, G], F32)
            kkt = kr
            sqk = sqc
            rnx = sb(es, "rnx", [128, 512], F32)
            st8 = sb(es, "st8", [128, 4, 8], F32)
            pr = sb(es, "pr", [128, 4, G], BF16)
            prd = sb(es, "prd", [128, G], BF16)
            tw = sb(es, "tw", [128, G], F32)
            Lc = sb(es, "Lc", [128, G], F32)
            nLx = sb(es, "nLx", [128, G], F32)
            Ein = sb(es, "Ein", [128, G], F32)
            Einv = sb(es, "Einv", [128, G], F32)
            Eex = sb(es, "Eex", [128, G], F32)
            ta = sb(es, "ta", [128, G], F32)
            qq = sb(es, "qq", [128, G], F32)
            kd = sb(es, "kd", [128, G], F32)
            bq = sb(es, "bq", [128, G], F32)
            Rt = [sb(es, "Rt%d" % d, [128, 4, G], BF16) for d in range(2)]
            Kt = [sb(es, "Kt%d" % d, [128, 4, G], BF16) for d in range(2)]
            Bt = [sb(es, "Bt%d" % d, [128, 4, G], BF16) for d in range(2)]
            At = [sb(es, "At%d" % d, [128, 4, G], BF16) for d in range(2)]
            Atm = [sb(es, "Atm%d" % d, [128, 512], BF16) for d in range(2)]
            Btm = [sb(es, "Btm%d" % d, [128, 512], BF16) for d in range(2)]
            Ktm = [sb(es, "Ktm%d" % d, [128, 512], BF16) for d in range(2)]
            Vtm = sb(es, "Vtm", [128, 512], BF16)
            bon = sb(es, "bon", [128, 16], F32)
            bvt = sb(es, "bvt", [128, 512], BF16)
            gtm = sb(es, "gtm", [128, 512], BF16)
            ysum = sb(es, "ysum", [128, 512], F32)
            UB = {}
            for d in range(2):
                for p in range(2):
                    u = {}
                    for nm in ("ab", "abT", "akT", "rbT", "rkT", "P1", "PT0", "PT1", "X0", "X1"):
                        u[nm] = sb(es, "u%d%d_%s" % (d, p, nm), [128, 4, 128], BF16)
                    u["P0"] = u["akT"]
                    UB[(d, p)] = u
            mtT = [sb(es, "mtT%d" % d, [128, 2, 4, 128], BF16) for d in range(2)]
            nnT = [sb(es, "nnT%d" % d, [128, 2, 4, 64], F32) for d in range(2)]
            qtT = [sb(es, "qtT%d" % d, [128, 2, 4, 128], BF16) for d in range(2)]
            ctmp = sb(es, "ctmp", [128, 4, 64], F32)
            otr = sb(es, "otr", [64, 128], F32)

            for d in range(2):
                op("pool", lambda e, d=d: e.memset(qtT[d][:], 0.0), writes=["qtT%d" % d])
                op("pool", lambda e, d=d: e.memset(mtT[d][:], 0.0), writes=["mtT%d" % d])
            for t_ in zpS + zpP + [upS, upP]:
                op("pool", lambda e, t_=t_: e.memset(t_[:], 0.0), writes=[t_.name])
            for kc in range(8):
                op("pool", lambda e, kc=kc: e.dma_start(out=winb[:, kc, :], in_=win_d[kc * 128:kc * 128 + 128, 0:1920]),
                   writes=["winb%d" % kc], dma="wi%d" % kc)

            WINB = ["winb%d" % kc for kc in range(8)]

            zrot = [0]
            ubk = [0]
            UBANKS = [4, 5, 6, 7, 0, 1]
            CONV_ORDER = [19, 23, 15, 20, 24, 16, 21, 25, 17, 22, 26, 18]

            def load_wcv(pos):
                oc = CONV_ORDER[pos]
                wsl = pos % 4
                op("sp", lambda e, oc=oc, wsl=wsl: e.dma_start(out=wcv[wsl][:].rearrange("p k n -> p (k n)"),
                                                               in_=wcvS[oc - 15].rearrange("p k n -> p (k n)")),
                   reads=["wcvS"], writes=["wcv%d" % wsl], dma="wcv%d" % wsl)
            P.cutoff = p1_stage < 1

            def next_ubank():
                b = UBANKS[ubk[0] % len(UBANKS)]
                ubk[0] += 1
                return b

            def do_group(gi):
                isS = gi < 16
                m = 0 if isS else 1
                tok0 = gi * G
                rows, rl = (4, 64) if isS else (1, 256)
                zp = zpS if isS else zpP
                up_ = upS if isS else upP

                def v3(ap2):
                    return ap2.rearrange("p (r l) -> p r l", r=rows)

                for pos_ in range(4):
                    load_wcv(pos_)
                for tl in range(2):
                    ti = gi * 2 + tl
                    X, XS = xt[tl], xs[tl]
                    xn, xsn = "xt0", "xs%d" % tl
                    op("sp", lambda e, X=X, ti=ti: e.dma_start(out=X[:], in_=x_d[ti * 128:ti * 128 + 128, :]),
                       writes=[xn], dma="x0")
                    op("act", lambda e, X=X, XS=XS, tl=tl: e.activation(out=XS[:], in_=X[:], func=AF.Square,
                                                                        accum_out=st1[:, tl:tl + 1]),
                       reads=[xn], writes=[xsn, "st1a%d" % tl])
                    op("dve", lambda e, tl=tl: e.tensor_scalar(out=st1[:, 2 + tl:3 + tl], in0=st1[:, tl:tl + 1],
                                                               scalar1=1.0 / D, scalar2=NORM_EPS, op0=ALU.mult, op1=ALU.add),
                       reads=["st1a%d" % tl], writes=["st1b%d" % tl])
                    op("pool", lambda e, tl=tl: e.tensor_tensor(out=st1[:, 4 + tl:5 + tl], in0=st1[:, 2 + tl:3 + tl],
                                                                in1=nhalf[:, 0:1], op=ALU.pow),
                       reads=["st1b%d" % tl, "nhalf"], writes=["st1c%d" % tl])
                    op("dve", lambda e, X=X, XS=XS, tl=tl: e.tensor_scalar(out=XS[:], in0=X[:], scalar1=st1[:, 4 + tl:5 + tl],
                                                                           scalar2=None, op0=ALU.mult),
                       reads=[xn, "st1c%d" % tl], writes=[xsn])
                    for kc in range(8):
                        op("pe", lambda e, XS=XS, kc=kc, tl=tl: e.transpose(
                            out=pbf(0)[:, kc * 128:kc * 128 + 128] if tl == 0 else pbf(1)[:, kc * 128:kc * 128 + 128],
                            in_=XS[:, kc * 128:kc * 128 + 128], identity=identb[:]),
                           reads=[xsn, "identb"], writes=["pb%d" % tl])
                    for kc in range(8):
                        op("act", lambda e, kc=kc, tl=tl, m=m: e.activation(
                            out=hT[:, kc, tl * 128:tl * 128 + 128], in_=pbf(tl)[:, kc * 128:kc * 128 + 128],
                            func=AF.Identity, scale=modc[:, m, 0, kc:kc + 1], bias=modc[:, m, 1, kc:kc + 1]),
                           reads=["pb%d" % tl, "modc"], writes=["hT%d" % kc])
                HT = ["hT%d" % kc for kc in range(8)]

                def zmm(oc):
                    slot = zrot[0] % 2
                    zrot[0] += 1
                    bank = 2 + slot
                    apz = PB[bank][:, 0:256]
                    rn_ = "pb%d" % bank
                    if oc < 15:
                        for kc in range(8):
                            op("pe", lambda e, kc=kc, oc=oc, apz=apz: e.matmul(
                                apz, lhsT=winb[:, kc, oc * 128:oc * 128 + 128], rhs=hT[:, kc, :], start=(kc == 0), stop=(kc == 7)),
                               reads=[WINB[kc], HT[kc]], writes=[rn_])
                    else:
                        pos = CONV_ORDER.index(oc)
                        wsl = pos % 4
                        for kc in range(8):
                            op("pe", lambda e, kc=kc, wsl=wsl, apz=apz: e.matmul(
                                apz, lhsT=wcv[wsl][:, kc, :], rhs=hT[:, kc, :], start=(kc == 0), stop=(kc == 7)),
                               reads=["wcv%d" % wsl, HT[kc]], writes=[rn_])
                        if pos + 4 < 12:
                            load_wcv(pos + 4)
                    return apz, rn_

                P.cutoff = p1_stage < 2
                import os
                A2N = int(os.environ.get("A2N", "15"))
                A2K = int(os.environ.get("A2K", "99"))
                for oc in range(A2N):
                    apz, rn_ = zmm(oc)
                    zb = zp[oc % 2]
                    zbn = zb.name
                    if A2K < 1:
                        continue
                    op("act", lambda e, apz=apz, zb=zb: e.activation(out=zb[:, :, 1:rl + 1], in_=v3(apz), func=AF.Copy),
                       reads=[rn_], writes=[zbn])
                    if A2K < 2:
                        continue
                    op("act", lambda e, apz=apz, oc=oc: e.activation(out=v3(sT[:]), in_=v3(apz), func=AF.Copy,
                                                                     scale=dc[:, 48 + oc:49 + oc]),
                       reads=[rn_, "dc"], writes=["sT"])
                    op("dve", lambda e, zb=zb, oc=oc: e.scalar_tensor_tensor(out=v3(uT[:]), in0=zb[:, :, 0:rl], scalar=dc[:, 32 + oc:33 + oc],
                                                                             in1=v3(sT[:]), op0=ALU.mult, op1=ALU.add),
                       reads=["sT", zbn, "dc"], writes=["uT"])
                    if oc < 8:
                        dst, dn = rk[:, oc, :], "rk%d" % oc
                    elif oc < 12:
                        dst, dn = vbf[:, oc - 8, :], "vbf%d" % (oc - 8)
                    else:
                        dst, dn = (tgl[:] if oc == 14 else uT[:]), ("tgl" if oc == 14 else "uT")
                    if oc < 12 or oc == 14:
                        op("dve", lambda e, zb=zb, dst=dst, oc=oc: e.scalar_tensor_tensor(
                            out=v3(dst), in0=zb[:, :, 2:rl + 2], scalar=dc[:, 32 + oc:33 + oc],
                            in1=v3(uT[:]), op0=ALU.mult, op1=ALU.add), reads=["uT", zbn, "dc"], writes=[dn])
                    else:
                        op("dve", lambda e, zb=zb, oc=oc: e.scalar_tensor_tensor(
                            out=v3(sT[:]), in0=zb[:, :, 2:rl + 2], scalar=dc[:, 32 + oc:33 + oc],
                            in1=v3(uT[:]), op0=ALU.mult, op1=ALU.add), reads=["uT", zbn, "dc"], writes=["sT"])
                        if oc == 12:
                            op("act", lambda e: e.activation(out=lor[:, 0, :], in_=sT[:], func=AF.Tanh),
                               reads=["sT"], writes=["lor0"])
                        else:
                            op("act", lambda e: e.activation(out=lor[:, 1, :], in_=sT[:], func=AF.Copy),
                               reads=["sT"], writes=["lor1"])
                    if oc == 14:
                        op("act", lambda e: e.activation(out=tgl[:], in_=tgl[:], func=AF.Tanh, scale=0.5),
                           reads=["tgl"], writes=["tgl"])
                        op("dve", lambda e: e.tensor_scalar(out=lor[:, 2, :], in0=tgl[:], scalar1=0.5, scalar2=0.5,
                                                            op0=ALU.mult, op1=ALU.add), reads=["tgl"], writes=["lor2"])

                P.cutoff = p1_stage < 3
                for j in range(4):
                    apz, rn_ = zmm(19 + j)
                    op("act", lambda e, apz=apz: e.activation(out=gcs[:], in_=apz, func=AF.Copy), reads=[rn_], writes=["gcs"])
                    apz2, rn2 = zmm(23 + j)
                    op("dve", lambda e, apz2=apz2: e.tensor_tensor(out=up_[:, :, 1:rl + 1], in0=v3(apz2), in1=v3(gcs[:]),
                                                                   op=ALU.mult), reads=[rn2, "gcs"], writes=["upad"])
                    apz3, rn3 = zmm(15 + j)
                    op("act", lambda e, apz3=apz3: e.activation(out=gbs[:], in_=apz3, func=AF.Copy), reads=[rn3], writes=["gbs"])
                    op("act", lambda e, j=j: e.activation(out=v3(ct[0][:]), in_=up_[:, :, 0:rl], func=AF.Copy,
                                                          scale=vc[:, VR["cw0"] + j:VR["cw0"] + j + 1]),
                       reads=["upad", "vc"], writes=["ct0"])
                    op("dve", lambda e, j=j: e.scalar_tensor_tensor(out=v3(ct[1][:]), in0=up_[:, :, 1:rl + 1],
                                                                    scalar=vc[:, VR["cw1"] + j:VR["cw1"] + j + 1],
                                                                    in1=v3(ct[0][:]), op0=ALU.mult, op1=ALU.add),
                       reads=["upad", "ct0", "vc"], writes=["ct1"])
                    op("dve", lambda e, j=j: e.scalar_tensor_tensor(out=v3(ct[2][:]), in0=up_[:, :, 2:rl + 2],
                                                                    scalar=vc[:, VR["cw2"] + j:VR["cw2"] + j + 1],
                                                                    in1=v3(ct[1][:]), op0=ALU.mult, op1=ALU.add),
                       reads=["upad", "ct1", "vc"], writes=["ct2"])
                    op("dve", lambda e: e.tensor_tensor(out=ct[0][:], in0=ct[2][:], in1=gbs[:], op=ALU.mult),
                       reads=["ct2", "gbs"], writes=["ct0"])
                    op("act", lambda e, j=j: e.activation(out=sqc[:, j, :], in_=ct[0][:], func=AF.Square),
                       reads=["ct0"], writes=["sqc%d" % j])
                    op("dve", lambda e, j=j: e.tensor_scalar(out=ocb[:, j, :], in0=ct[0][:],
                                                             scalar1=vc[:, VR["cg"] + j:VR["cg"] + j + 1], scalar2=None,
                                                             op0=ALU.mult), reads=["ct0", "vc"], writes=["ocb%d" % j])
                for tl in range(2):
                    for j in range(4):
                        op("pe", lambda e, tl=tl, j=j: e.matmul(PB[1][:, 16 + tl:17 + tl], lhsT=sqc[:, j, tl * 128:tl * 128 + 128],
                                                                rhs=onescol[:, 0:1], start=(j == 0), stop=(j == 3)),
                           reads=["sqc%d" % j, "onescol"], writes=["pb1"])
                for tl in range(2):
                    ti = gi * 2 + tl
                    op("dve", lambda e, tl=tl: e.tensor_scalar(out=st1[:, 6 + tl:7 + tl], in0=PB[1][:, 16 + tl:17 + tl],
                                                               scalar1=1.0 / 512, scalar2=NORM_EPS, op0=ALU.mult, op1=ALU.add),
                       reads=["pb1"], writes=["st1d%d" % tl])
                    op("pool", lambda e, tl=tl, ti=ti: e.tensor_tensor(out=rcall[:, ti:ti + 1], in0=st1[:, 6 + tl:7 + tl],
                                                                       in1=nhalf[:, 0:1], op=ALU.pow),
                       reads=["st1d%d" % tl, "nhalf"], writes=["rcall"])
                    op("sp", lambda e, tl=tl, ti=ti: e.dma_start(
                        out=ocS[ti].rearrange("p (c t) -> p c t", c=4), in_=ocb[:, :, tl * 128:tl * 128 + 128]),
                       reads=["ocb%d" % j for j in range(4)], writes=["ocS"], dma="s_oc")

                P.cutoff = p1_stage < 4
                for c in range(4):
                    op("dve", lambda e, c=c: e.tensor_scalar(out=kr[:, c, :], in0=rk[:, 4 + c, :],
                                                             scalar1=vc[:, VR["kk"] + c:VR["kk"] + c + 1], scalar2=None,
                                                             op0=ALU.mult), reads=["rk%d" % (4 + c), "vc"], writes=["kr%d" % c])
                    op("act", lambda e, c=c: e.activation(out=sqk[:, c, :], in_=kr[:, c, :], func=AF.Square),
                       reads=["kr%d" % c], writes=["sqc%d" % c])
                    op("dve", lambda e, c=c: e.tensor_scalar(out=pr[:, c, :], in0=rk[:, c, :],
                                                             scalar1=vc[:, VR["rk"] + c:VR["rk"] + c + 1], scalar2=None,
                                                             op0=ALU.mult), reads=["rk%d" % c, "vc"], writes=["pr%d" % c])
                for tl in range(2):
                    for c in range(4):
                        op("pe", lambda e, tl=tl, c=c: e.matmul(PB[1][:, 32 + tl * 8:40 + tl * 8], lhsT=sqk[:, c, tl * 128:tl * 128 + 128],
                                                                rhs=ind8[:, c, :], start=(c == 0), stop=(c == 3)),
                           reads=["sqc%d" % c, "ind8"], writes=["pb1"])
                for tl in range(2):
                    op("dve", lambda e, tl=tl: e.tensor_scalar(out=st8[:, 0, :], in0=PB[1][:, 32 + tl * 8:40 + tl * 8],
                                                               scalar1=1e-12, scalar2=None, op0=ALU.add),
                       reads=["pb1"], writes=["st8a"])
                    op("pool", lambda e: e.tensor_tensor(out=st8[:, 1, :], in0=st8[:, 0, :], in1=nhalf[:, 0:8], op=ALU.pow),
                       reads=["st8a", "nhalf"], writes=["st8b"])
                    op("dve", lambda e: e.tensor_copy(out=rnx[:].rearrange("p (h j) -> p h j", h=8),
                                                      in_=bcast(st8[:, 1, :].unsqueeze(2), [128, 8, 64])),
                       reads=["st8b"], writes=["rnx"])
                    for c in range(4):
                        op("pe", lambda e, c=c: e.transpose(out=PB[0][:, c * 128:c * 128 + 128], in_=rnx[:, c * 128:c * 128 + 128],
                                                            identity=identf[:]), reads=["rnx", "identf"], writes=["pb0"])
                    op("dve", lambda e, tl=tl: e.tensor_tensor(out=kkt[:, :, tl * 128:tl * 128 + 128],
                                                               in0=PB[0][:, :].rearrange("p (c t) -> p c t", c=4),
                                                               in1=kr[:, :, tl * 128:tl * 128 + 128], op=ALU.mult),
                       reads=["pb0"] + ["kr%d" % c for c in range(4)], writes=["kkt%d" % tl])
                KKT = ["kkt0", "kkt1"]

                P.cutoff = p1_stage < 5
                op("dve", lambda e: e.memset(PB[1][:, 64:80], 0.0), writes=["pb1"])
                def a3b(d):
                    dr = slice(64 * d, 64 * d + 64)
                    for c in range(4):
                        lp = PB[0][:, 256:512]
                        op("pe", lambda e, c=c, dr=dr, lp=lp: e.matmul(lp, lhsT=ww2[dr, c * 128:c * 128 + 128], rhs=lor[dr, 0, :],
                                                                       start=True, stop=True),
                           reads=["ww2", "lor0"], writes=["pb0"])
                        op("act", lambda e, c=c, d=d, lp=lp: e.activation(out=tw[:], in_=lp, func=AF.Tanh, scale=0.5,
                                                                          bias=dc[:, d * 4 + c:d * 4 + c + 1]),
                           reads=["pb0", "dc"], writes=["tw"])
                        for ch in range(4):
                            if d == 0:
                                sl = slice(ch * 64, ch * 64 + 64)
                            else:
                                sl = slice(ch * 64 + 63, ch * 64 - 1 if ch > 0 else None, -1)
                            op("dve", lambda e, sl=sl: e.tensor_tensor_scan(out=Lc[:, sl], data0=tw[:, sl], data1=ones64[:],
                                                                            initial=0.0, op0=ALU.add, op1=ALU.add),
                               reads=["tw", "ones64"], writes=["Lc"])
                        op("dve", lambda e: e.scalar_tensor_tensor(out=nLx[:], in0=tw[:], scalar=1.0, in1=Lc[:],
                                                                   op0=ALU.add, op1=ALU.subtract), reads=["tw", "Lc"], writes=["nLx"])
                        op("act", lambda e: e.activation(out=Ein[:], in_=Lc[:], func=AF.Exp, scale=-KAPPA), reads=["Lc"], writes=["Ein"])
                        op("act", lambda e: e.activation(out=Einv[:], in_=Lc[:], func=AF.Exp, scale=KAPPA), reads=["Lc"], writes=["Einv"])
                        op("act", lambda e: e.activation(out=Eex[:], in_=nLx[:], func=AF.Exp, scale=KAPPA), reads=["nLx"], writes=["Eex"])
                        cs = 63 if d == 0 else 0
                        op("pool", lambda e, d=d, c=c, cs=cs, gi=gi: e.tensor_copy(
                            out=gC[d][:, c, gi * 4:gi * 4 + 4], in_=Ein[:].rearrange("p (ch t) -> p ch t", t=64)[:, :, cs]),
                           reads=["Ein"], writes=["gC%d" % d])
                        ap_ = PB[0][:, 0:256]
                        op("pe", lambda e, c=c, dr=dr, ap_=ap_: e.matmul(ap_, lhsT=wa2[dr, c * 128:c * 128 + 128], rhs=lor[dr, 1, :],
                                                                         start=True, stop=True),
                           reads=["wa2", "lor1"], writes=["pb0"])
                        op("act", lambda e, c=c, d=d, ap_=ap_: e.activation(out=ta[:], in_=ap_, func=AF.Tanh, scale=0.5,
                                                                            bias=dc[:, 8 + d * 4 + c:8 + d * 4 + c + 1]),
                           reads=["pb0", "dc"], writes=["ta"])
                        op("dve", lambda e, c=c: e.tensor_scalar(out=qq[:], in0=ta[:], scalar1=dc[:, 16 + c:17 + c],
                                                                 scalar2=dc[:, 20 + c:21 + c], op0=ALU.mult, op1=ALU.add),
                           reads=["ta", "dc"], writes=["qq"])
                        op("dve", lambda e, c=c: e.tensor_tensor(out=kd[:], in0=qq[:], in1=rk[:, 4 + c, :], op=ALU.mult),
                           reads=["qq", "rk%d" % (4 + c)], writes=["kd"])
                        op("pool", lambda e, c=c: e.tensor_tensor(out=prd[:], in0=kd[:], in1=pr[:, c, :], op=ALU.mult),
                           reads=["kd", "pr%d" % c], writes=["prd"])
                        for tl in range(2):
                            op("pe", lambda e, tl=tl, c=c, d=d: e.matmul(PB[1][:, 64 + tl * 8:72 + tl * 8],
                                                                         lhsT=prd[:, tl * 128:tl * 128 + 128], rhs=ind8[:, c, :],
                                                                         start=False, stop=(c == 3 and d == 1), skip_group_check=True),
                               reads=["prd", "ind8"], writes=["pb1"])
                        op("dve", lambda e, c=c: e.scalar_tensor_tensor(out=bq[:], in0=ta[:], scalar=1.0, in1=kkt[:, c, :],
                                                                        op0=ALU.add, op1=ALU.mult),
                           reads=["ta"] + KKT, writes=["bq"])
                        op("pool", lambda e, c=c, d=d: e.tensor_tensor(out=Rt[d][:, c, :], in0=rk[:, c, :], in1=Ein[:], op=ALU.mult),
                           reads=["rk%d" % c, "Ein"], writes=["Rt%d_%d" % (d, c)])
                        op("pool", lambda e, c=c, d=d: e.tensor_tensor(out=Kt[d][:, c, :], in0=kd[:], in1=Einv[:], op=ALU.mult),
                           reads=["kd", "Einv"], writes=["Kt%d_%d" % (d, c)])
                        op("dve", lambda e, c=c, d=d: e.scalar_tensor_tensor(out=Bt[d][:, c, :], in0=bq[:], scalar=0.5, in1=Einv[:],
                                                                             op0=ALU.mult, op1=ALU.mult),
                           reads=["bq", "Einv"], writes=["Bt%d_%d" % (d, c)])
                        op("dve", lambda e, c=c, d=d: e.scalar_tensor_tensor(out=At[d][:, c, :], in0=kkt[:, c, :], scalar=-1.0,
                                                                             in1=Eex[:], op0=ALU.mult, op1=ALU.mult),
                           reads=KKT + ["Eex"], writes=["At%d_%d" % (d, c)])
                        yield
                for _ in a3b(0):
                    pass

                P.cutoff = p1_stage < 6
                def fmres(nm, d):
                    return ["%s%d_%d" % (nm, d, c) for c in range(4)]

                def do_tile(tl):
                    ti = gi * 2 + tl
                    cols = slice(tl * 128, tl * 128 + 128)
                    def tm_entries(dirs, with_v):
                        ent = []
                        for d in dirs:
                            ent += [(At[d], Atm[d], fmres("At", d), "Atm%d" % d), (Bt[d], Btm[d], fmres("Bt", d), "Btm%d" % d),
                                    (Kt[d], Ktm[d], fmres("Kt", d), "Ktm%d" % d)]
                        if with_v:
                            ent.append((vbf, Vtm, ["vbf%d" % c for c in range(4)], "Vtm"))
                        return ent

                    def tm_transposes(entries):
                        for qi_, (src, dstt, srcn, dstn) in enumerate(entries):
                            half = qi_ % 2
                            for c in range(4):
                                op("pe", lambda e, src=src, c=c, half=half: e.transpose(
                                    out=pbf(0)[:, half * 512 + c * 128:half * 512 + c * 128 + 128], in_=src[:, c, cols], identity=identb[:]),
                                   reads=srcn + ["identb"], writes=["pb0"])
                            if qi_ % 2 == 0:
                                op("act", lambda e, dstt=dstt, half=half: e.activation(out=dstt[:], in_=pbf(0)[:, half * 512:half * 512 + 512],
                                                                                       func=AF.Copy), reads=["pb0"], writes=[dstn])
                            else:
                                op("dve", lambda e, dstt=dstt, half=half: e.tensor_copy(out=dstt[:], in_=pbf(0)[:, half * 512:half * 512 + 512]),
                                   reads=["pb0"], writes=[dstn])

                    def bv_part():
                        op("pool", lambda e: e.tensor_tensor(out=bvt[:].rearrange("p (h j) -> p h j", h=8),
                                                             in0=Vtm[:].rearrange("p (h j) -> p h j", h=8),
                                                             in1=bcast(bon[:, tl * 8:tl * 8 + 8].unsqueeze(2), [128, 8, 64]), op=ALU.mult),
                           reads=["Vtm", "bon"], writes=["bvt"])
                        op("sp", lambda e: e.dma_start(out=bvS[ti], in_=bvt[:]), reads=["bvt"], writes=["bvS"], dma="s_bv")

                    def g_part():
                        op("pe", lambda e: e.matmul(PB[3][:, :], lhsT=lor[:, 2, cols], rhs=wg2[:], start=True, stop=True),
                           reads=["lor2", "wg2"], writes=["pb3"])
                        op("act", lambda e: e.activation(out=gtm[:], in_=PB[3][:, :], func=AF.Copy), reads=["pb3"], writes=["gtm"])
                        op("sp", lambda e: e.dma_start(out=gS[ti], in_=gtm[:]), reads=["gtm"], writes=["gS"], dma="s_g")
                        op("dve", lambda e: e.memset(PB[2][:, :], 0.0), writes=["pb2"])
                        op("dve", lambda e: e.memset(PB[3][:, :], 0.0), writes=["pb3"])

                    first_y = {}
                    def do_unit(d, p):
                        if d == 0:
                            masks = {"ab": mSL, "abT": mSU, "akT": mSU, "rbT": mUI, "rkT": mUI}
                        else:
                            masks = {"ab": mSU, "abT": mSL, "akT": mSL, "rbT": mLI, "rkT": mLI}
                        if True:
                            pb_ = 64 * p
                            ypsn = "pb%d" % (2 + p)
                            ub = UB[(d, p)]
                            un = lambda s, p=p, d=d: "u%d%d_%s" % (d, p, "akT" if s == "P0" else s)
                            wcol, ucol = pb_, 64 - pb_
                            prs = slice(pb_, pb_ + 64)
                            P.cutoff = p1_stage < 7.05
                            prods = (("ab", At, Bt, "At", "Bt"), ("abT", Bt, At, "Bt", "At"), ("akT", Kt, At, "Kt", "At"),
                                     ("rbT", Bt, Rt, "Bt", "Rt"), ("rkT", Kt, Rt, "Kt", "Rt"))
                            for (nm, Lm, Rm, ln_, rn__) in prods:
                                bk = next_ubank()
                                for c in range(4):
                                    op("pe", lambda e, bk=bk, c=c, Lm=Lm, Rm=Rm, d=d, prs=prs: e.matmul(
                                        PB[bk][:, c * 128:c * 128 + 128], lhsT=Lm[d][prs, c, cols], rhs=Rm[d][prs, c, cols],
                                        start=True, stop=True), reads=fmres(ln_, d) + fmres(rn__, d), writes=["pb%d" % bk])
                                mk = masks[nm]
                                op("dve", lambda e, bk=bk, nm=nm, ub=ub, mk=mk: e.tensor_tensor(
                                    out=ub[nm][:], in0=PB[bk][:, :].rearrange("p (c t) -> p c t", c=4),
                                    in1=bcast(mk[:].unsqueeze(1), [128, 4, 128]), op=ALU.mult),
                                   reads=["pb%d" % bk, mk.name], writes=[un(nm)])
                                yield
                            P.cutoff = p1_stage < 7.15
                            bk = next_ubank()
                            for c in range(4):
                                h = 2 * c + p
                                op("pe", lambda e, bk=bk, c=c, h=h, ub=ub: e.matmul(
                                    PB[bk][:, c * 64:c * 64 + 64], lhsT=ub["akT"][:, c, :], rhs=Vtm[:, h * 64:h * 64 + 64],
                                    start=True, stop=True), reads=[un("akT"), "Vtm"], writes=["pb%d" % bk])
                            op("act", lambda e, bk=bk, ub=ub, ucol=ucol: e.activation(
                                out=ub["X0"][:, :, ucol:ucol + 64], in_=PB[bk][:, 0:256].rearrange("p (c i) -> p c i", c=4),
                                func=AF.Copy), reads=["pb%d" % bk], writes=[un("X0")])
                            op("act", lambda e, ub=ub, wcol=wcol, d=d, p=p: e.activation(
                                out=ub["X0"][:, :, wcol:wcol + 64],
                                in_=Atm[d][:].rearrange("p (c q j) -> p c q j", c=4, q=2)[:, :, p, :], func=AF.Copy),
                               reads=["Atm%d" % d], writes=[un("X0")])
                            yield
                            P.cutoff = p1_stage < 7.25
                            Pk = [ub["ab"], ub["P1"], ub["P0"]]
                            PTk = [ub["abT"], ub["PT1"], ub["PT0"]]
                            Pn = [un("ab"), un("P1"), un("P0")]
                            PTn = [un("abT"), un("PT1"), un("PT0")]

                            def lvl(k):
                                return 0 if k == 0 else 1 + ((k - 1) % 2)
                            Xs = [ub["X0"], ub["X1"]]
                            Xn = [un("X0"), un("X1")]
                            for k in range(6):
                                a_, b_ = lvl(k), lvl(k + 1)
                                bk = next_ubank()
                                for c in range(4):
                                    op("pe", lambda e, bk=bk, c=c, a_=a_, k=k, PTk=PTk, Xs=Xs: e.matmul(
                                        PB[bk][:, c * 128:c * 128 + 128], lhsT=PTk[a_][:, c, :], rhs=Xs[k % 2][:, c, :],
                                        start=True, stop=True), reads=[PTn[a_], Xn[k % 2]], writes=["pb%d" % bk])
                                op("dve", lambda e, bk=bk, k=k, Xs=Xs: e.tensor_tensor(
                                    out=Xs[(k + 1) % 2][:], in0=PB[bk][:, :].rearrange("p (c t) -> p c t", c=4), in1=Xs[k % 2][:],
                                    op=ALU.add), reads=["pb%d" % bk, Xn[k % 2]], writes=[Xn[(k + 1) % 2]])
                                if k < 5:
                                    bk = next_ubank()
                                    for c in range(4):
                                        op("pe", lambda e, bk=bk, c=c, a_=a_, Pk=Pk, PTk=PTk: e.matmul(
                                            PB[bk][:, c * 128:c * 128 + 128], lhsT=Pk[a_][:, c, :], rhs=PTk[a_][:, c, :],
                                            start=True, stop=True), reads=[Pn[a_], PTn[a_]], writes=["pb%d" % bk])
                                    bk2 = None
                                    if k < 4:
                                        bk2 = next_ubank()
                                        for c in range(4):
                                            op("pe", lambda e, bk2=bk2, c=c, a_=a_, Pk=Pk, PTk=PTk: e.matmul(
                                                PB[bk2][:, c * 128:c * 128 + 128], lhsT=PTk[a_][:, c, :], rhs=Pk[a_][:, c, :],
                                                start=True, stop=True), reads=[Pn[a_], PTn[a_]], writes=["pb%d" % bk2])
                                    op("act", lambda e, bk=bk, b_=b_, PTk=PTk: e.activation(
                                        out=PTk[b_][:], in_=PB[bk][:, :].rearrange("p (c t) -> p c t", c=4), func=AF.Copy),
                                       reads=["pb%d" % bk], writes=[PTn[b_]])
                                    if bk2 is not None:
                                        op("act", lambda e, bk2=bk2, b_=b_, Pk=Pk: e.activation(
                                            out=Pk[b_][:], in_=PB[bk2][:, :].rearrange("p (c t) -> p c t", c=4), func=AF.Copy),
                                           reads=["pb%d" % bk2], writes=[Pn[b_]])
                                yield
                            X6, X6n = Xs[0], Xn[0]
                            P.cutoff = p1_stage < 7.35
                            for ch in range(2):
                                bk = next_ubank()
                                chr_ = slice(ch * 64, ch * 64 + 64)
                                for c in range(4):
                                    h = 2 * c + p
                                    op("pe", lambda e, bk=bk, ch=ch, c=c, h=h, chr_=chr_, X6=X6, d=d: e.matmul(
                                        PB[bk][:, c * 64:c * 64 + 64], lhsT=X6[chr_, c, :],
                                        rhs=Btm[d][chr_, h * 64:h * 64 + 64], start=True, stop=True),
                                       reads=[X6n, "Btm%d" % d], writes=["pb%d" % bk])
                                op("act", lambda e, bk=bk, d=d, prs=prs, pb_=pb_, ch=ch: e.activation(
                                    out=mtT[d][prs, ch, :, pb_:pb_ + 64],
                                    in_=PB[bk][prs, 0:256].rearrange("p (c j) -> p c j", c=4), func=AF.Copy),
                                   reads=["pb%d" % bk], writes=["mtT%d_%d" % (d, p)])
                                yield
                            P.cutoff = p1_stage < 7.45
                            for ch in range(2):
                                bk = next_ubank()
                                chr_ = slice(ch * 64, ch * 64 + 64)
                                for c in range(4):
                                    h = 2 * c + p
                                    o_ = PB[bk][:, c * 64:c * 64 + 64]
                                    op("pe", lambda e, o_=o_, c=c, chr_=chr_, X6=X6, d=d, ucol=ucol: e.matmul(
                                        o_, lhsT=Btm[d][chr_, c * 128:c * 128 + 128], rhs=X6[chr_, c, ucol:ucol + 64],
                                        start=True, stop=False), reads=[X6n, "Btm%d" % d], writes=["pb%d" % bk])
                                    op("pe", lambda e, o_=o_, c=c, h=h, chr_=chr_, d=d: e.matmul(
                                        o_, lhsT=Ktm[d][chr_, c * 128:c * 128 + 128], rhs=Vtm[chr_, h * 64:h * 64 + 64],
                                        start=False, stop=True), reads=["Ktm%d" % d, "Vtm"], writes=["pb%d" % bk])
                                op("act", lambda e, bk=bk, d=d, prs=prs, ch=ch: e.activation(
                                    out=nnT[d][prs, ch, :, :], in_=PB[bk][prs, 0:256].rearrange("p (c i) -> p c i", c=4), func=AF.Copy),
                                   reads=["pb%d" % bk], writes=["nnT%d_%d" % (d, p)])
                                yield
                            P.cutoff = p1_stage < 7.55
                            bk = next_ubank()
                            for c in range(4):
                                op("pe", lambda e, bk=bk, c=c, X6=X6, ub=ub: e.matmul(
                                    PB[bk][:, c * 128:c * 128 + 128], lhsT=X6[:, c, :], rhs=ub["rbT"][:, c, :], start=True, stop=True),
                                   reads=[X6n, un("rbT")], writes=["pb%d" % bk])
                            for ch in range(2):
                                op("dve", lambda e, bk=bk, ch=ch, d=d, prs=prs: e.tensor_tensor(
                                    out=qtT[d][prs, ch, :, ch * 64:ch * 64 + 64],
                                    in0=PB[bk][prs, :].rearrange("p (c t) -> p c t", c=4)[:, :, ch * 64:ch * 64 + 64],
                                    in1=Rt[d][prs, :, tl * 128 + ch * 64:tl * 128 + ch * 64 + 64], op=ALU.add),
                                   reads=["pb%d" % bk] + fmres("Rt", d), writes=["qtT%d_%d" % (d, p)])
                            yield
                            P.cutoff = p1_stage < 7.65
                            for c in range(4):
                                h = 2 * c + p
                                yo = PB[2 + p][:, h * 64:h * 64 + 64]
                                st_ = (d == 0)
                                op("pe", lambda e, yo=yo, c=c, X6=X6, ub=ub, ucol=ucol, st_=st_: e.matmul(
                                    yo, lhsT=ub["rbT"][:, c, :], rhs=X6[:, c, ucol:ucol + 64], start=False, stop=False, skip_group_check=True),
                                   reads=[un("rbT"), X6n], writes=[ypsn])
                                op("pe", lambda e, yo=yo, c=c, h=h, ub=ub, d=d: e.matmul(
                                    yo, lhsT=ub["rkT"][:, c, :], rhs=Vtm[:, h * 64:h * 64 + 64], start=False, stop=(d == 1), skip_group_check=True),
                                   reads=[un("rkT"), "Vtm"], writes=[ypsn])
                            P.cutoff = p1_stage < 7.75
                            if d == 0:
                                for ch in range(2):
                                    gch = ti * 2 + ch
                                    bk = next_ubank()
                                    for c in range(4):
                                        h = 2 * c + p
                                        op("pe", lambda e, c=c, h=h, ch=ch, prs=prs, p=p: e.matmul(
                                            PB[2 + p][:, h * 64:h * 64 + 64], lhsT=qtT[0][prs, ch, c, :], rhs=Tbf[0][prs, c, :],
                                            start=False, stop=False, skip_group_check=True), reads=["qtT0_%d" % p, "Tbf0_%d" % p], writes=[ypsn])
                                        op("pe", lambda e, bk=bk, c=c, ch=ch, prs=prs: e.matmul(
                                            PB[bk][:, c * 64:c * 64 + 64], lhsT=mtT[0][prs, ch, c, :], rhs=Tbf[0][prs, c, :],
                                            start=True, stop=True), reads=["mtT0_%d" % p, "Tbf0_%d" % p], writes=["pb%d" % bk])
                                    op("dve", lambda e, bk=bk, ch=ch, prs=prs: e.tensor_tensor(
                                        out=ctmp[prs, :, :], in0=PB[bk][prs, 0:256].rearrange("p (c i) -> p c i", c=4),
                                        in1=nnT[0][prs, ch, :, :], op=ALU.add), reads=["pb%d" % bk, "nnT0_%d" % p], writes=["ctmp%d" % p])
                                    op("dve", lambda e, prs=prs: e.tensor_tensor(out=ctmp[prs, :, :], in0=ctmp[prs, :, :],
                                                                                 in1=T32[0][prs, :, :], op=ALU.add),
                                       reads=["ctmp%d" % p, "T32_0_%d" % p], writes=["ctmp%d" % p])
                                    op("dve", lambda e, prs=prs, gch=gch: e.tensor_tensor(
                                        out=Tbf[0][prs, :, :], in0=ctmp[prs, :, :],
                                        in1=bcast(gC[0][prs, :, gch:gch + 1], [64, 4, 64]), op=ALU.mult),
                                       reads=["ctmp%d" % p, "gC0"], writes=["Tbf0_%d" % p])
                                    op("pool", lambda e, prs=prs, gch=gch: e.tensor_tensor(
                                        out=T32[0][prs, :, :], in0=ctmp[prs, :, :],
                                        in1=bcast(gC[0][prs, :, gch:gch + 1], [64, 4, 64]), op=ALU.mult),
                                       reads=["ctmp%d" % p, "gC0"], writes=["T32_0_%d" % p])
                                    yield
                    def run(gens, extra=None):
                        while gens or extra is not None:
                            for g_ in list(gens):
                                try:
                                    next(g_)
                                except StopIteration:
                                    gens.remove(g_)
                            if extra is not None:
                                try:
                                    next(extra)
                                except StopIteration:
                                    extra = None

                    if tl == 0:
                        tm_transposes(tm_entries([0], True))
                        g_part()
                        UBANKS[:] = [4, 5, 6, 7]
                        run([do_unit(0, 0), do_unit(0, 1)], extra=a3b(1))
                        UBANKS[:] = [4, 5, 6, 7, 0, 1]
                        op("dve", lambda e: e.tensor_copy(out=bon[:], in_=PB[1][:, 64:80]), reads=["pb1"], writes=["bon"])
                        tm_transposes(tm_entries([1], False))
                        bv_part()
                        run([do_unit(1, 0), do_unit(1, 1)])
                    else:
                        tm_transposes(tm_entries([0, 1], True))
                        bv_part()
                        g_part()
                        run([do_unit(0, 0), do_unit(0, 1), do_unit(1, 0), do_unit(1, 1)])
                    P.cutoff = p1_stage < 8
                    for p in range(2):
                        op("act", lambda e, p=p: e.activation(
                            out=ysum[:].rearrange("p (c q j) -> p c q j", c=4, q=2)[:, :, p, :],
                            in_=PB[2 + p][:, :].rearrange("p (c q j) -> p c q j", c=4, q=2)[:, :, p, :], func=AF.Copy),
                           reads=["pb%d" % (2 + p)], writes=["ysum"])
                    op("sp", lambda e, ti=ti: e.dma_start(out=ysS[ti], in_=ysum[:]), reads=["ysum"], writes=["ysS"], dma="s_ys")
                    op("sp", lambda e, ti=ti: e.dma_start(out=mtS[ti], in_=mtT[1][:].rearrange("p a c j -> p (a c j)")),
                       reads=["mtT1_0", "mtT1_1"], writes=["mtS"], dma="s_mt")
                    op("sp", lambda e, ti=ti: e.dma_start(out=nnS[ti], in_=nnT[1][:].rearrange("p a c j -> p (a c j)")),
                       reads=["nnT1_0", "nnT1_1"], writes=["nnS"], dma="s_nn")
                    op("sp", lambda e, ti=ti: e.dma_start(out=qtS[ti], in_=qtT[1][:].rearrange("p a c j -> p (a c j)")),
                       reads=["qtT1_0", "qtT1_1"], writes=["qtS"], dma="s_qt")

                for tl in range(2):
                    do_tile(tl)
                if not isS:
                    sq_ = gi - 16
                    for c in range(4):
                        op("pe", lambda e, c=c: e.transpose(out=PB[0][0:64, c * 128:c * 128 + 128], in_=T32[0][:, c, :], identity=identf[:]),
                           reads=["T32_0_0", "T32_0_1", "identf"], writes=["pb0"])
                    for c in range(4):
                        op("dve", lambda e, c=c: e.tensor_copy(out=otr[:, :], in_=PB[0][0:64, c * 128:c * 128 + 128]),
                           reads=["pb0"], writes=["otr"])
                        op("sp", lambda e, c=c, sq_=sq_: e.dma_start(
                            out=ns_d[0, sq_, 2 * c:2 * c + 2].rearrange("q i j -> i q j"),
                            in_=otr[:, :].rearrange("i (q j) -> i q j", q=2)), reads=["otr"], writes=["ns"], dma="s_ns")
                if gi >= 15:
                    op("pool", lambda e: e.memset(T32[0][:], 0.0), writes=["T32_0_0", "T32_0_1"])
                    op("pool", lambda e: e.memset(Tbf[0][:], 0.0), writes=["Tbf0_0", "Tbf0_1"])
            for gi in range(p1_groups):
                do_group(gi)
            build_and_emit(nc, P, bar)

        if stop_after < 2:
            return nc
        with ExitStack() as es:
            P = Prog()
            op = P.op
            woutb = sb(es, "woutb", [128, 8, D], BF16)
            wff1b = sb(es, "wff1b", [128, 8, 4 * D], BF16)
            wff2b = sb(es, "wff2b", [128, 32, D], BF16)
            gateA = sb(es, "gateA", [128, D], F32)
            gateF = sb(es, "gateF", [128, D], F32)
            lnwb = sb(es, "lnwb", [128, 512], F32)
            lnbb = sb(es, "lnbb", [128, 512], F32)
            fgb = sb(es, "fgb", [128, D], F32)
            op("sp", lambda e: e.dma_start(out=lnwb[:], in_=bcast(lnw_d[0:1, :], [128, 512])), writes=["lnwb"], dma="l2")
            op("sp", lambda e: e.dma_start(out=lnbb[:], in_=bcast(lnb_d[0:1, :], [128, 512])), writes=["lnbb"], dma="l3")
            op("sp", lambda e: e.dma_start(out=fgb[:], in_=bcast(fg_d[0:1, :], [128, D])), writes=["fgb"], dma="l4")
            xt2 = sb(es, "xt2", [128, D], F32)
            x1b = [sb(es, "x1b%d" % i, [128, D], F32) for i in range(2)]
            xs2 = sb(es, "xs2", [128, D], BF16)
            junk2 = xs2
            h2Tb = [sb(es, "h2Tb%d" % i, [128, 8, 128], BF16) for i in range(2)]
            afm = sb(es, "afm", [128, 16, 128], BF16)
            rl_ = [sb(es, "rl%d" % i, [128, 512], BF16) for i in range(2)]
            mtL = sb(es, "mtL", [128, 2, 4, 128], BF16)
            nnL = sb(es, "nnL", [128, 2, 4, 64], F32)
            qtL = sb(es, "qtL", [128, 2, 4, 128], BF16)
            ysL = sb(es, "ysL", [128, 512], F32)
            gL = sb(es, "gL", [128, 512], BF16)
            bvL = sb(es, "bvL", [128, 512], BF16)
            ocL = sb(es, "ocL", [128, 4, 128], BF16)
            yn = sb(es, "yn", [128, 512], F32)
            yy = ysL
            ysq = yn
            otm = sb(es, "otm", [128, 512], BF16)
            ofm = sb(es, "ofm", [128, 4, 128], BF16)
            s8 = sb(es, "s8", [128, 6, 8], F32)
            st2 = sb(es, "st2", [128, 8], F32)
            ctmp2 = sb(es, "ctmp2", [128, 4, 64], F32)
            otr2 = ctmp2[0:64, 0:2, :].rearrange("p a i -> p (a i)")

            for kc in range(8):
                op("pool", lambda e, kc=kc: e.dma_start(out=woutb[:, kc, :], in_=wout_d[kc * 128:kc * 128 + 128, :]),
                   writes=["woutb"], dma="wo")
                op("pool", lambda e, kc=kc: e.dma_start(out=wff1b[:, kc, :], in_=wff1_d[kc * 128:kc * 128 + 128, :]),
                   writes=["wff1b"], dma="w1")
            for kc in range(32):
                op("pool", lambda e, kc=kc: e.dma_start(out=wff2b[:, kc, :], in_=wff2_d[kc * 128:kc * 128 + 128, :]),
                   writes=["wff2b"], dma="w2")

            def front(ti):
                isS = ti < NSAMP_T
                m = 0 if isS else 1
                if (not isS) and (ti % 2 == 1):
                    op("pool", lambda e: e.memset(T32[1][:], 0.0), writes=["T32_1_0", "T32_1_1"])
                    op("pool", lambda e: e.memset(Tbf[1][:], 0.0), writes=["Tbf1_0", "Tbf1_1"])
                if ti == NSAMP_T - 1:
                    op("dve", lambda e: e.tensor_copy(out=T32[1][:], in_=T32i[:]), writes=["T32_1_0", "T32_1_1"])
                    op("act", lambda e: e.activation(out=Tbf[1][:], in_=T32i[:], func=AF.Copy), writes=["Tbf1_0", "Tbf1_1"])
                op("sp", lambda e, ti=ti: e.dma_start(out=mtL[:].rearrange("p a c j -> p (a c j)"), in_=mtS[ti]), writes=["mtL"], dma="r_mt")
                op("sp", lambda e, ti=ti: e.dma_start(out=nnL[:].rearrange("p a c j -> p (a c j)"), in_=nnS[ti]), writes=["nnL"], dma="r_nn")
                op("sp", lambda e, ti=ti: e.dma_start(out=qtL[:].rearrange("p a c j -> p (a c j)"), in_=qtS[ti]), writes=["qtL"], dma="r_qt")
                op("sp", lambda e, ti=ti: e.dma_start(out=ysL[:], in_=ysS[ti]), writes=["ysL"], dma="r_ys")
                op("sp", lambda e, ti=ti: e.dma_start(out=gL[:], in_=gS[ti]), writes=["gL"], dma="r_g")
                op("sp", lambda e, ti=ti: e.dma_start(out=bvL[:], in_=bvS[ti]), writes=["bvL"], dma="r_bv")
                op("sp", lambda e, ti=ti: e.dma_start(out=ocL[:].rearrange("p c t -> p (c t)"), in_=ocS[ti]), writes=["ocL"], dma="r_oc")
                op("dve", lambda e: e.memset(PB[0][:, :], 0.0), writes=["pb0"])
                op("dve", lambda e: e.memset(PB[3][:, :], 0.0), writes=["pb3"])
                for p in range(2):
                    prs = slice(64 * p, 64 * p + 64)
                    for ch in (1, 0):
                        gch = ti * 2 + ch
                        for c in range(4):
                            h = 2 * c + p
                            op("pe", lambda e, c=c, h=h, ch=ch, prs=prs, p=p: e.matmul(
                                PB[3 * p][:, h * 64:h * 64 + 64], lhsT=qtL[prs, ch, c, :], rhs=Tbf[1][prs, c, :],
                                start=False, stop=(ch == 0), skip_group_check=True), reads=["qtL", "Tbf1_%d" % p], writes=["pb%d" % (3 * p)])
                            op("pe", lambda e, c=c, ch=ch, prs=prs, p=p: e.matmul(
                                PB[1][:, c * 64:c * 64 + 64], lhsT=mtL[prs, ch, c, :], rhs=Tbf[1][prs, c, :],
                                start=True, stop=True), reads=["mtL", "Tbf1_%d" % p], writes=["pb1"])
                        op("dve", lambda e, ch=ch, prs=prs, p=p: e.tensor_tensor(
                            out=ctmp2[prs, :, :], in0=PB[1][prs, 0:256].rearrange("p (c i) -> p c i", c=4),
                            in1=nnL[prs, ch, :, :], op=ALU.add), reads=["pb1", "nnL"], writes=["ctmp2_%d" % p])
                        op("dve", lambda e, prs=prs: e.tensor_tensor(out=ctmp2[prs, :, :], in0=ctmp2[prs, :, :],
                                                                     in1=T32[1][prs, :, :], op=ALU.add),
                           reads=["ctmp2_%d" % p, "T32_1_%d" % p], writes=["ctmp2_%d" % p])
                        op("dve", lambda e, prs=prs, gch=gch: e.tensor_tensor(
                            out=Tbf[1][prs, :, :], in0=ctmp2[prs, :, :], in1=bcast(gC[1][prs, :, gch:gch + 1], [64, 4, 64]),
                            op=ALU.mult), reads=["ctmp2_%d" % p], writes=["Tbf1_%d" % p])
                        op("pool", lambda e, prs=prs, gch=gch: e.tensor_tensor(
                            out=T32[1][prs, :, :], in0=ctmp2[prs, :, :], in1=bcast(gC[1][prs, :, gch:gch + 1], [64, 4, 64]),
                            op=ALU.mult), reads=["ctmp2_%d" % p], writes=["T32_1_%d" % p])
                        yield
                if (not isS) and (ti % 2 == 0):
                    sq_ = (ti - NSAMP_T) // 2
                    for c in range(4):
                        op("pe", lambda e, c=c: e.transpose(out=PB[1][0:64, c * 128:c * 128 + 128], in_=T32[1][:, c, :], identity=identf[:]),
                           reads=["T32_1_0", "T32_1_1", "identf"], writes=["pb1"])
                    for c in range(4):
                        op("dve", lambda e, c=c: e.tensor_copy(out=otr2[:, :], in_=PB[1][0:64, c * 128:c * 128 + 128]),
                           reads=["pb1"], writes=["otr2", "ctmp2_0", "ctmp2_1"])
                        op("sp", lambda e, c=c, sq_=sq_: e.dma_start(
                            out=ns_d[1, sq_, 2 * c:2 * c + 2].rearrange("q i j -> i q j"),
                            in_=otr2[:, :].rearrange("i (q j) -> i q j", q=2)), reads=["otr2", "ctmp2_0", "ctmp2_1"], writes=["ns"], dma="s_ns")
                for p in range(2):
                    op("dve", lambda e, p=p: e.tensor_tensor(
                        out=yy[:].rearrange("p (c q j) -> p c q j", c=4, q=2)[:, :, p, :],
                        in0=PB[3 * p][:, :].rearrange("p (c q j) -> p c q j", c=4, q=2)[:, :, p, :],
                        in1=ysL[:].rearrange("p (c q j) -> p c q j", c=4, q=2)[:, :, p, :], op=ALU.add),
                       reads=["pb%d" % (3 * p), "ysL"], writes=["ysL"])
                yield
                y3 = lambda t: t[:].rearrange("p (h j) -> p h j", h=8)
                op("dve", lambda e: e.tensor_reduce(out=s8[:, 0, :], in_=y3(yy), axis=AX.X, op=ALU.add), reads=["ysL"], writes=["s8a"])
                op("dve", lambda e: e.tensor_tensor(out=ysq[:], in0=yy[:], in1=yy[:], op=ALU.mult), reads=["ysL"], writes=["yn"])
                op("dve", lambda e: e.tensor_reduce(out=s8[:, 1, :], in_=y3(ysq), axis=AX.X, op=ALU.add), reads=["yn"], writes=["s8b"])
                op("dve", lambda e: e.tensor_scalar(out=s8[:, 2, :], in0=s8[:, 0, :], scalar1=1.0 / 64, scalar2=None, op0=ALU.mult),
                   reads=["s8a"], writes=["s8c"])
                op("dve", lambda e: e.tensor_tensor(out=s8[:, 3, :], in0=s8[:, 2, :], in1=s8[:, 2, :], op=ALU.mult),
                   reads=["s8c"], writes=["s8d"])
                op("dve", lambda e: e.scalar_tensor_tensor(out=s8[:, 4, :], in0=s8[:, 1, :], scalar=1.0 / 64, in1=s8[:, 3, :],
                                                           op0=ALU.mult, op1=ALU.subtract), reads=["s8b", "s8d"], writes=["s8e"])
                op("dve", lambda e: e.tensor_scalar(out=s8[:, 4, :], in0=s8[:, 4, :], scalar1=GN_EPS, scalar2=None, op0=ALU.add),
                   reads=["s8e"], writes=["s8e"])
                op("pool", lambda e: e.tensor_tensor(out=s8[:, 5, :], in0=s8[:, 4, :], in1=nhalf[:, 0:8], op=ALU.pow),
                   reads=["s8e", "nhalf"], writes=["s8f"])
                yield
                op("dve", lambda e: e.tensor_tensor(out=y3(yn), in0=y3(yy), in1=bcast(s8[:, 2, :].unsqueeze(2), [128, 8, 64]),
                                                    op=ALU.subtract), reads=["ysL", "s8c"], writes=["yn"])
                op("dve", lambda e: e.tensor_tensor(out=y3(yn), in0=y3(yn), in1=bcast(s8[:, 5, :].unsqueeze(2), [128, 8, 64]),
                                                     op=ALU.mult), reads=["yn", "s8f"], writes=["yn"])
                op("dve", lambda e: e.tensor_tensor(out=yn[:], in0=yn[:], in1=lnwb[:], op=ALU.mult), reads=["yn", "lnwb"], writes=["yn"])
                op("dve", lambda e: e.tensor_tensor(out=yn[:], in0=yn[:], in1=lnbb[:], op=ALU.add), reads=["yn", "lnbb"], writes=["yn"])
                op("dve", lambda e: e.tensor_tensor(out=yn[:], in0=yn[:], in1=bvL[:], op=ALU.add), reads=["yn", "bvL"], writes=["yn"])
                op("dve", lambda e: e.tensor_tensor(out=otm[:], in0=yn[:], in1=gL[:], op=ALU.mult), reads=["yn", "gL"], writes=["otm"])
                for c in range(4):
                    op("pe", lambda e, c=c: e.transpose(out=pbf(3)[:, c * 128:c * 128 + 128], in_=otm[:, c * 128:c * 128 + 128],
                                                        identity=identb[:]), reads=["otm", "identb"], writes=["pb3"])
                op("act", lambda e: e.activation(out=ofm[:], in_=pbf(3)[:, 0:512].rearrange("p (c t) -> p c t", c=4), func=AF.Copy),
                   reads=["pb3"], writes=["ofm"])
                yield
                if ti == NTILE - 1 or ti == NSAMP_T - 1:
                    op("sp", lambda e, m=m: e.dma_start(out=gateA[:], in_=gatesS[:, m * 2 * D:m * 2 * D + D]),
                       writes=["gateA"], dma="l5")
                op("sp", lambda e, ti=ti: e.dma_start(out=xt2[:], in_=x_d[ti * 128:ti * 128 + 128, :]), writes=["xt2"], dma="r_x")
                x1 = x1b[ti % 2]
                x1n = "x1b%d" % (ti % 2)
                h2T = h2Tb[ti % 2]
                h2n = "h2Tb%d" % (ti % 2)
                for half in range(2):
                    for kc in range(4):
                        op("pe", lambda e, half=half, kc=kc: e.matmul(PB[0][:, :], lhsT=ofm[:, kc, :],
                                                                      rhs=woutb[:, kc, half * 512:half * 512 + 512],
                                                                      start=(kc == 0), stop=(kc == 3)),
                           reads=["ofm", "woutb"], writes=["pb0"])
                    for kc in range(4):
                        op("pe", lambda e, half=half, kc=kc: e.matmul(PB[3][:, :], lhsT=ocL[:, kc, :],
                                                                      rhs=woutb[:, 4 + kc, half * 512:half * 512 + 512],
                                                                      start=(kc == 0), stop=(kc == 3)),
                           reads=["ocL", "woutb"], writes=["pb3"])
                    hs = slice(half * 512, half * 512 + 512)
                    op("act", lambda e, hs=hs, x1=x1: e.activation(out=x1[:, hs], in_=PB[0][:, :], func=AF.Copy),
                       reads=["pb0"], writes=[x1n])
                    op("dve", lambda e, hs=hs, ti=ti, x1=x1: e.scalar_tensor_tensor(
                        out=x1[:, hs], in0=PB[3][:, :], scalar=rcall[:, ti:ti + 1], in1=x1[:, hs],
                        op0=ALU.mult, op1=ALU.add), reads=["pb3", x1n], writes=[x1n])
                    op("dve", lambda e, hs=hs, x1=x1: e.tensor_tensor(out=x1[:, hs], in0=x1[:, hs], in1=gateA[:, hs], op=ALU.mult),
                       reads=[x1n, "gateA"], writes=[x1n])
                    op("dve", lambda e, hs=hs, x1=x1: e.tensor_tensor(out=x1[:, hs], in0=x1[:, hs], in1=xt2[:, hs], op=ALU.add),
                       reads=[x1n, "xt2"], writes=[x1n])
                    yield
                op("act", lambda e, x1=x1: e.activation(out=xs2[:], in_=x1[:], func=AF.Square, accum_out=st2[:, 0:1]),
                   reads=[x1n], writes=["xs2", "st2a"])
                op("dve", lambda e: e.tensor_scalar(out=st2[:, 1:2], in0=st2[:, 0:1], scalar1=1.0 / D, scalar2=NORM_EPS,
                                                    op0=ALU.mult, op1=ALU.add), reads=["st2a"], writes=["st2b"])
                op("pool", lambda e: e.tensor_tensor(out=st2[:, 2:3], in0=st2[:, 1:2], in1=nhalf[:, 0:1], op=ALU.pow),
                   reads=["st2b", "nhalf"], writes=["st2c"])
                op("dve", lambda e, x1=x1: e.tensor_scalar(out=xs2[:], in0=x1[:], scalar1=st2[:, 2:3], scalar2=None, op0=ALU.mult),
                   reads=[x1n, "st2c"], writes=["xs2"])
                yield
                for kc in range(8):
                    op("pe", lambda e, kc=kc: e.transpose(out=pbf(1)[:, kc * 128:kc * 128 + 128], in_=xs2[:, kc * 128:kc * 128 + 128],
                                                          identity=identb[:]), reads=["xs2", "identb"], writes=["pb1"])
                for kc in range(8):
                    op("act" if kc % 2 else "dve", (lambda e, kc=kc, m=m, h2T=h2T: e.activation(
                        out=h2T[:, kc, :], in_=pbf(1)[:, kc * 128:kc * 128 + 128], func=AF.Identity,
                        scale=modc[:, m, 2, kc:kc + 1], bias=modc[:, m, 3, kc:kc + 1])) if kc % 2 else
                       (lambda e, kc=kc, m=m, h2T=h2T: e.tensor_scalar(
                           out=h2T[:, kc, :], in0=pbf(1)[:, kc * 128:kc * 128 + 128], scalar1=modc[:, m, 2, kc:kc + 1],
                           scalar2=modc[:, m, 3, kc:kc + 1], op0=ALU.mult, op1=ALU.add)),
                       reads=["pb1", "modc"], writes=[h2n])
                yield

            def back(ti):
                isS = ti < NSAMP_T
                m = 0 if isS else 1
                x1 = x1b[ti % 2]
                x1n = "x1b%d" % (ti % 2)
                h2T = h2Tb[ti % 2]
                h2n = "h2Tb%d" % (ti % 2)
                if ti == NTILE - 1 or ti == NSAMP_T - 1:
                    op("sp", lambda e, m=m: e.dma_start(out=gateF[:], in_=gatesS[:, m * 2 * D + D:(m + 1) * 2 * D]),
                       writes=["gateF"], dma="l6")
                def tr_block(cb):
                    c4 = cb % 4
                    r_ = rl_[cb % 2]
                    for i4 in range(4):
                        op("pe", lambda e, r_=r_, i4=i4: e.transpose(out=pbf(2)[:, i4 * 128:i4 * 128 + 128],
                                                                     in_=r_[:, i4 * 128:i4 * 128 + 128], identity=identb[:]),
                           reads=[r_.name, "identb"], writes=["pb2"])
                    if cb % 2:
                        op("act", lambda e, c4=c4: e.activation(out=afm[:, c4 * 4:c4 * 4 + 4, :],
                                                                in_=pbf(2)[:, 0:512].rearrange("p (c t) -> p c t", c=4), func=AF.Copy),
                           reads=["pb2"], writes=["afm%d" % c4])
                    else:
                        op("dve", lambda e, c4=c4: e.tensor_copy(out=afm[:, c4 * 4:c4 * 4 + 4, :],
                                                                 in_=pbf(2)[:, 0:512].rearrange("p (c t) -> p c t", c=4)),
                           reads=["pb2"], writes=["afm%d" % c4])

                for hf in range(2):
                    for c4 in range(4):
                        cb = hf * 4 + c4
                        bank = (6, 7)[cb % 2]
                        for kc in range(8):
                            op("pe", lambda e, kc=kc, cb=cb, bank=bank, h2T=h2T: e.matmul(
                                PB[bank][:, :], lhsT=h2T[:, kc, :], rhs=wff1b[:, kc, cb * 512:cb * 512 + 512],
                                start=(kc == 0), stop=(kc == 7)), reads=["wff1b", h2n], writes=["pb%d" % bank])
                        r_ = rl_[cb % 2]
                        op("act", lambda e, bank=bank, r_=r_: e.activation(out=r_[:], in_=PB[bank][:, :], func=AF.Relu),
                           reads=["pb%d" % bank], writes=[r_.name])
                        op("dve", lambda e, r_=r_: e.tensor_tensor(out=r_[:], in0=r_[:], in1=r_[:], op=ALU.mult),
                           reads=[r_.name], writes=[r_.name])
                        if c4 > 0:
                            tr_block(cb - 1)
                        yield
                    tr_block(hf * 4 + 3)
                    for half in range(2):
                        for o16 in range(16):
                            kc = hf * 16 + o16
                            op("pe", lambda e, half=half, kc=kc, o16=o16: e.matmul(PB[4 + half][:, :], lhsT=afm[:, o16, :],
                                                                                   rhs=wff2b[:, kc, half * 512:half * 512 + 512],
                                                                                   start=(kc == 0), stop=(kc == 31)),
                               reads=["afm%d" % (o16 // 4), "wff2b"], writes=["pb%d" % (4 + half)])
                        yield
                for half in range(2):
                    hs = slice(half * 512, half * 512 + 512)
                    op("dve", lambda e, half=half, hs=hs: e.tensor_tensor(out=PB[4 + half][:, :], in0=PB[4 + half][:, :],
                                                                          in1=gateF[:, hs], op=ALU.mult),
                       reads=["pb%d" % (4 + half), "gateF"], writes=["pb%d" % (4 + half)])
                    op("dve", lambda e, half=half, hs=hs, x1=x1: e.tensor_tensor(out=x1[:, hs], in0=PB[4 + half][:, :], in1=x1[:, hs], op=ALU.add),
                       reads=["pb%d" % (4 + half), x1n], writes=[x1n])
                op("act", lambda e, x1=x1: e.activation(out=afm[:].rearrange("p a t -> p (a t)")[:, 0:D], in_=x1[:], func=AF.Square,
                                                        accum_out=st2[:, 3:4]),
                   reads=[x1n], writes=["afm0", "afm1", "st2d"])
                op("dve", lambda e: e.tensor_scalar(out=st2[:, 4:5], in0=st2[:, 3:4], scalar1=1.0 / D, scalar2=NORM_EPS,
                                                    op0=ALU.mult, op1=ALU.add), reads=["st2d"], writes=["st2e"])
                op("pool", lambda e: e.tensor_tensor(out=st2[:, 5:6], in0=st2[:, 4:5], in1=nhalf[:, 0:1], op=ALU.pow),
                   reads=["st2e", "nhalf"], writes=["st2f"])
                op("dve", lambda e, x1=x1: e.scalar_tensor_tensor(out=x1[:], in0=x1[:], scalar=st2[:, 5:6], in1=fgb[:],
                                                                  op0=ALU.mult, op1=ALU.mult), reads=[x1n, "st2f", "fgb"], writes=[x1n])
                op("sp", lambda e, ti=ti, x1=x1: e.dma_start(out=y_d[ti * 128:ti * 128 + 128, :], in_=x1[:]), reads=[x1n], writes=["y"], dma="s_y")
                yield
            gF = front(NTILE - 1)
            for _ in gF:
                pass
            for ti in range(NTILE - 1, -1, -1):
                gB = back(ti)
                gF = front(ti - 1) if ti > 0 else None
                nb = 0
                while gB is not None or gF is not None:
                    if gB is not None:
                        try:
                            next(gB)
                            nb += 1
                        except StopIteration:
                            gB = None
                    if gF is not None and (gB is None or nb % 1 == 0):
                        try:
                            next(gF)
                        except StopIteration:
                            gF = None
            build_and_emit(nc, P, bar)
    return nc


_NC_CACHE = {}


def kernel(x_prompt, x_sample, c, state_rwkv_fwd, state_rwkv_bwd, c_ctx, w_mod, b_mod,
           norm1_g, w_in, mu_shift, w0_f, w_w2_f, w0_b, w_w2_b, a0_f, w_a2_f, a0_b, w_a2_b,
           w_g2, k_k, k_a, r_k, ln_x_w, ln_x_b, conv_w, conv_gain, w_out, norm2_g,
           w_ff1, w_ff2, final_g):
    f = lambda a: np.ascontiguousarray(np.asarray(a, dtype=np.float32))
    x_prompt, x_sample, c, c_ctx = f(x_prompt), f(x_sample), f(c), f(c_ctx)
    sf, sbk = f(state_rwkv_fwd), f(state_rwkv_bwd)
    bm = f(b_mod)[0]
    rows = []

    def addv(v):
        v = f(v).reshape(-1)
        rows.append(v.reshape(-1, 128))
    addv(mu_shift[0])
    for v in (w0_f, w0_b, a0_f, a0_b, k_k, k_a, r_k):
        addv(v[0])
    cw = f(conv_w)[0]
    addv(cw[0]); addv(cw[1]); addv(cw[2])
    addv(conv_gain[0]); addv(norm1_g[0]); addv(norm2_g[0])
    addv(bm[0:D]); addv(bm[D:2 * D]); addv(bm[3 * D:4 * D]); addv(bm[4 * D:5 * D])
    vr = np.concatenate(rows, axis=0)
    assert vr.shape[0] == NVR
    vecs = np.zeros((128, 128), np.float32)
    vecs[:NVR] = vr
    common = {
        "vecs": vecs, "w_mod": f(w_mod)[0], "b_mod": f(b_mod), "w_in": f(w_in)[0],
        "w_w2": np.concatenate([f(w_w2_f)[0], f(w_w2_b)[0]], axis=0),
        "w_a2": np.concatenate([f(w_a2_f)[0], f(w_a2_b)[0]], axis=0),
        "w_g2": f(w_g2)[0], "ln_w": f(ln_x_w), "ln_b": f(ln_x_b), "final_g": f(final_g).reshape(1, D),
        "w_out": f(w_out)[0], "w_ff1": f(w_ff1)[0], "w_ff2": f(w_ff2)[0],
    }
    in_maps = []
    for i in range(NCORES):
        xm = np.concatenate([x_sample[i], x_prompt[4 * i:4 * i + 4].reshape(1024, D)], axis=0)
        mm = dict(common)
        mm["x"] = np.ascontiguousarray(xm)
        mm["c2"] = np.ascontiguousarray(np.stack([c[i], c_ctx], axis=0))
        mm["st"] = np.ascontiguousarray(np.stack([sf[i, 0], sbk[i, 0]], axis=0))
        in_maps.append(mm)
    if "nc" not in _NC_CACHE:
        _NC_CACHE["nc"] = build_program()
    nc = _NC_CACHE["nc"]
    res = run_bass_kernel_spmd(nc, in_maps, core_ids=list(range(NCORES)))
    y_prompt = np.zeros((32, 256, D), np.float32)
    y_sample = np.zeros((8, 4096, D), np.float32)
    nsf = np.zeros((32, 1, 8, 64, 64), np.float32)
    nsb = np.zeros((32, 1, 8, 64, 64), np.float32)
    for i in range(NCORES):
        r = res.results[i]
        y_sample[i] = r["y"][:4096]
        y_prompt[4 * i:4 * i + 4] = r["y"][4096:].reshape(4, 256, D)
        nsf[4 * i:4 * i + 4, 0] = r["ns"][0]
        nsb[4 * i:4 * i + 4, 0] = r["ns"][1]
    return (y_prompt, y_sample, nsf, nsb)
```

```python
import numpy as np
from contextlib import ExitStack
import concourse.bass as bass
import concourse.mybir as mybir
from concourse.ap import AP
from concourse.bass_utils import run_bass_kernel_spmd

F32 = mybir.dt.float32
BF16 = mybir.dt.bfloat16
AF = mybir.ActivationFunctionType
ALU = mybir.AluOpType
AX = mybir.AxisListType

ENG_NAMES = ("pe", "act", "dve", "pool", "sp")
NCORES = 8
NTOK = 5120
NTILE = 40
NSAMP_T = 32
D = 1024
KAPPA = 0.5 * float(np.exp(-0.5))
NORM_EPS = 1e-6
GN_EPS = 64 * 1e-5


class _Op:
    __slots__ = ("eng", "fn", "deps", "dma", "milestone", "need", "known_after")

    def __init__(self, eng, fn, dma):
        self.eng = eng
        self.fn = fn
        self.deps = []
        self.dma = dma
        self.milestone = None
        self.need = False
        self.known_after = None


class Prog:
    def __init__(self):
        self.ops = {e: [] for e in ENG_NAMES}
        self.last_writer = {}
        self.readers = {}

    cutoff = False

    def op(self, eng, fn, reads=(), writes=(), dma=None):
        if self.cutoff:
            return None
        pbr = [r for r in reads if isinstance(r, str) and r.startswith("pb")]
        if pbr:
            reads = [r for r in reads if r not in pbr]
            writes = list(writes) + [r for r in pbr if r not in writes]
        o = _Op(eng, fn, dma)
        seen = set()

        def add(d, raw):
            if d is None or id(d) in seen:
                return
            if d.dma is None and d.eng == eng and dma is None:
                if eng == "pe" or not raw:
                    return
            seen.add(id(d))
            o.deps.append(d)

        for r in reads:
            add(self.last_writer.get(r), True)
        for r in writes:
            add(self.last_writer.get(r), False)
            for rd in self.readers.get(r, ()):
                add(rd, False)
        self.ops[eng].append(o)
        for r in writes:
            self.last_writer[r] = o
            self.readers[r] = []
        for r in reads:
            if r not in writes:
                self.readers.setdefault(r, []).append(o)
        return o


def build_and_emit(nc, prog, bar):
    with ExitStack() as es:
        eng_sem = {e: es.enter_context(nc.semaphore("s_" + e)) for e in ENG_NAMES}
        for e in ENG_NAMES:
            for o in prog.ops[e]:
                for d in o.deps:
                    d.need = True
        dma_keys, seen = [], set()
        for e in ENG_NAMES:
            for o in prog.ops[e]:
                if o.dma is not None and o.dma not in seen:
                    seen.add(o.dma)
                    dma_keys.append(o.dma)
        dma_sem = {k: es.enter_context(nc.semaphore("d_%d" % i)) for i, k in enumerate(dma_keys)}
        dma_cnt = {k: 0 for k in dma_keys}
        final = {}
        for e in ENG_NAMES:
            m = 0
            for o in prog.ops[e]:
                if o.dma is not None:
                    dma_cnt[o.dma] += 1
                    o.milestone = 16 * dma_cnt[o.dma]
                elif o.need:
                    m += 1
                    o.milestone = m
            final[e] = m + 1
        pending = {e: 0 for e in ENG_NAMES}
        known = {e: {} for e in ENG_NAMES}
        waits = {}
        progress = True
        while progress:
            progress = False
            for e in ENG_NAMES:
                lst = prog.ops[e]
                kn = known[e]
                while pending[e] < len(lst):
                    o = lst[pending[e]]
                    if any(d.known_after is None for d in o.deps):
                        break
                    w = []
                    for d in o.deps:
                        key = ("dma", d.dma) if d.dma is not None else ("eng", d.eng)
                        if kn.get(key, 0) >= d.milestone:
                            continue
                        w.append(d)
                        for k2, v2 in d.known_after.items():
                            if kn.get(k2, 0) < v2:
                                kn[k2] = v2
                        if kn.get(key, 0) < d.milestone:
                            kn[key] = d.milestone
                    waits[id(o)] = w
                    ka = dict(kn)
                    if o.dma is None and o.milestone is not None:
                        ka[("eng", e)] = o.milestone
                    o.known_after = ka
                    pending[e] += 1
                    progress = True
        for e in ENG_NAMES:
            assert pending[e] == len(prog.ops[e]), "dependency cycle on " + e

        def run_engine(e, engobj):
            for o in prog.ops[e]:
                for d in waits[id(o)]:
                    s = dma_sem[d.dma] if d.dma is not None else eng_sem[d.eng]
                    engobj.wait_ge(s, d.milestone)
                ins = o.fn(engobj)
                if o.dma is not None:
                    ins.then_inc(dma_sem[o.dma], 16)
                elif o.milestone is not None:
                    ins.then_inc(eng_sem[e], 1)
            if e != "sp":
                bar[e](engobj).then_inc(eng_sem[e], 1)
            for k in dma_keys:
                engobj.wait_ge(dma_sem[k], 16 * dma_cnt[k])
            for e2 in ENG_NAMES:
                if e2 != "sp" and e2 != e:
                    engobj.wait_ge(eng_sem[e2], final[e2])

        with nc.Block() as block:
            @block.tensor
            def _(eng):
                run_engine("pe", eng)

            @block.scalar
            def _(eng):
                run_engine("act", eng)

            @block.vector
            def _(eng):
                run_engine("dve", eng)

            @block.gpsimd
            def _(eng):
                run_engine("pool", eng)

            @block.sync
            def _(eng):
                run_engine("sp", eng)


VR = {}
_r = 0
for _n, _c in (("mu", 15), ("w0f", 4), ("w0b", 4), ("a0f", 4), ("a0b", 4), ("kk", 4), ("ka", 4),
               ("rk", 4), ("cw0", 4), ("cw1", 4), ("cw2", 4), ("cg", 4), ("g1", 8), ("g2", 8),
               ("bsha", 8), ("bsca", 8), ("bshf", 8), ("bscf", 8)):
    VR[_n] = _r
    _r += _c
NVR = _r


def bcast(ap, shape):
    return ap.broadcast_to(list(shape))


def build_program(dbg=False, stop_after=9, p1_groups=20, p1_stage=99, p0_stage=99):
    nc = bass.Bass("TRN2", target_bir_lowering=False)

    def din(name, shape, dt=F32):
        return nc.dram_tensor(name, list(shape), dt, kind="ExternalInput").ap()

    x_d = din("x", [NTOK, D])
    c2_d = din("c2", [2, D])
    st_d = din("st", [2, 8, 64, 64])
    vecs_d = din("vecs", [128, 128])
    wmod_d = din("w_mod", [D, 6 * D])
    bmod_d = din("b_mod", [1, 6 * D])
    win_d = din("w_in", [D, 3456])
    ww2_d = din("w_w2", [128, 512])
    wa2_d = din("w_a2", [128, 512])
    wg2_d = din("w_g2", [128, 512])
    lnw_d = din("ln_w", [1, 512])
    lnb_d = din("ln_b", [1, 512])
    fg_d = din("final_g", [1, D])
    wout_d = din("w_out", [D, D])
    wff1_d = din("w_ff1", [D, 4 * D])
    wff2_d = din("w_ff2", [4 * D, D])
    y_d = nc.dram_tensor("y", [NTOK, D], F32, kind="ExternalOutput").ap()
    ns_d = nc.dram_tensor("ns", [2, 4, 8, 64, 64], F32, kind="ExternalOutput").ap()

    def dscr(name, shape, dt):
        return nc.dram_tensor(name, list(shape), dt, kind="Internal").ap()

    mtS = dscr("mtS", [NTILE, 128, 1024], BF16)
    nnS = dscr("nnS", [NTILE, 128, 512], F32)
    qtS = dscr("qtS", [NTILE, 128, 1024], BF16)
    ysS = dscr("ysS", [NTILE, 128, 512], F32)
    gS = dscr("gS", [NTILE, 128, 512], BF16)
    bvS = dscr("bvS", [NTILE, 128, 512], BF16)
    ocS = dscr("ocS", [NTILE, 128, 512], BF16)
    gatesS = dscr("gatesS", [128, 4 * D], F32)
    wcvS = dscr("wcvS", [12, 128, 8, 128], BF16)

    with ExitStack() as es0:
        def sb(es, name, shape, dt):
            return es.enter_context(nc.sbuf_tensor(name, list(shape), dt))

        def pst(es, name, shape, dt):
            return es.enter_context(nc.psum_tensor(name, list(shape), dt))

        identb = sb(es0, "identb", [128, 128], BF16)
        identf = sb(es0, "identf", [128, 128], F32)
        ind8 = sb(es0, "ind8", [128, 4, 8], BF16)
        onescol = sb(es0, "onescol", [128, 1], BF16)
        nhalf = sb(es0, "nhalf", [128, 8], F32)
        vc = sb(es0, "vc", [128, 128], F32)
        dc = sb(es0, "dc", [128, 64], F32)
        modc = sb(es0, "modc", [128, 2, 4, 8], F32)
        T32 = [sb(es0, "T32_%d" % d, [128, 4, 64], F32) for d in range(2)]
        Tbf = [sb(es0, "Tbf_%d" % d, [128, 4, 64], BF16) for d in range(2)]
        T32i = sb(es0, "T32i", [128, 4, 64], F32)
        gC = [None, sb(es0, "gC_1", [128, 4, 80], F32)]
        rcall = sb(es0, "rcall", [128, NTILE], F32)
        bart = sb(es0, "bart", [128, 8], F32)
        barb = sb(es0, "barb", [128, 8], BF16)
        PB = [pst(es0, "pb%d" % i, [128, 512], F32) for i in range(8)]

        def pbf(i):
            return PB[i][:].bitcast(BF16)

        bar = {
            "pe": lambda e: e.matmul(PB[7][0:8, 0:8], lhsT=barb[0:8, 0:8], rhs=barb[0:8, 0:8], start=True, stop=True),
            "act": lambda e: e.activation(out=bart[:, 0:1], in_=bart[:, 1:2], func=AF.Copy),
            "dve": lambda e: e.tensor_copy(out=bart[:, 2:3], in_=bart[:, 3:4]),
            "pool": lambda e: e.tensor_copy(out=bart[:, 4:5], in_=bart[:, 5:6]),
        }

        with ExitStack() as es:
            P = Prog()
            op = P.op
            vraw = sb(es, "vraw", [128, 128], F32)
            c2T = sb(es, "c2T", [128, 8, 2], F32)
            c2raw = sb(es, "c2raw", [2, D], F32)
            th0 = sb(es, "th0", [128, 8, 2], F32)
            silu = sb(es, "silu", [128, 8, 2], F32)
            srep = sb(es, "srep", [128, 2, 8, 128], F32)
            wms = [sb(es, "wms%d" % i, [128, 8, 512], F32) for i in range(2)]
            bmb = sb(es, "bmb", [128, 2, D], F32)
            straw = sb(es, "straw", [64, 2, 4, 128], F32)
            modp = sb(es, "modp", [128, 32, 2], F32)
            gates = sb(es, "gates", [128, 2, 2, D], F32)
            wtmp0 = [sb(es, "wtmp0_%d" % i, [128, 1536], BF16) for i in range(8)]
            for kc in range(8):
                op("pool", lambda e, kc=kc: e.dma_start(out=wtmp0[kc][:], in_=win_d[kc * 128:kc * 128 + 128, 1920:3456]),
                   writes=["wtmp0_%d" % kc], dma="wt%d" % kc)

            op("pool", lambda e: e.memset(bart[:], 0.0), writes=["bart"])
            op("pool", lambda e: e.memset(barb[:], 0.0), writes=["barb"])
            op("pool", lambda e: e.memset(identf[:], 1.0), writes=["identf"])
            op("pool", lambda e: e.affine_select(out=identf[:], in_=identf[:], pattern=[[-1, 128]],
                                                 compare_op=ALU.is_equal, fill=0.0, base=0, channel_multiplier=1),
               reads=["identf"], writes=["identf"])
            op("pool", lambda e: e.tensor_copy(out=identb[:], in_=identf[:]), reads=["identf"], writes=["identb"])
            op("pool", lambda e: e.memset(ind8[:], 0.0), writes=["ind8"])
            for c in range(4):
                for p in range(2):
                    op("pool", lambda e, c=c, p=p: e.memset(ind8[64 * p:64 * p + 64, c, 2 * c + p:2 * c + p + 1], 1.0),
                       reads=["ind8"], writes=["ind8"])
            op("pool", lambda e: e.memset(onescol[:], 1.0), writes=["onescol"])
            op("pool", lambda e: e.memset(nhalf[:], -0.5), writes=["nhalf"])
            op("pool", lambda e: e.memset(gC[1][:], 1.0), writes=["gC1"])
            op("pool", lambda e: e.memset(rcall[:], 1.0), writes=["rcall"])

            P.cutoff = p0_stage < 1
            op("sp", lambda e: e.dma_start(out=vraw[:], in_=vecs_d[:, :]), writes=["vraw"], dma="l0")
            op("sp", lambda e: e.dma_start(out=c2raw[:], in_=c2_d[:, :]), writes=["c2raw"], dma="l1")
            op("sp", lambda e: e.dma_start(out=bmb[:, 0, :], in_=bcast(bmod_d[0:1, 2 * D:3 * D], [128, D])),
               writes=["bmb0"], dma="l5")
            op("sp", lambda e: e.dma_start(out=bmb[:, 1, :], in_=bcast(bmod_d[0:1, 5 * D:6 * D], [128, D])),
               writes=["bmb1"], dma="l6")
            for d in range(2):
                op("sp", lambda e, d=d: e.dma_start(
                    out=straw[:, d, :, :].rearrange("i c (p j) -> i c p j", p=2),
                    in_=st_d[d].rearrange("(c p) i j -> i c p j", p=2)), writes=["straw%d" % d], dma="l7%d" % d)

            P.cutoff = p0_stage < 2
            op("pe", lambda e: e.transpose(out=PB[0][:, 0:128], in_=vraw[:], identity=identf[:]),
               reads=["vraw", "identf"], writes=["pb0"])
            op("dve", lambda e: e.tensor_copy(out=vc[:], in_=PB[0][:, 0:128]), reads=["pb0"], writes=["vc"])
            op("dve", lambda e: e.tensor_scalar(out=dc[:, 0:8], in0=vc[:, VR["w0f"]:VR["w0f"] + 8], scalar1=0.5,
                                                scalar2=None, op0=ALU.mult), reads=["vc"], writes=["dc"])
            op("dve", lambda e: e.tensor_scalar(out=dc[:, 8:16], in0=vc[:, VR["a0f"]:VR["a0f"] + 8], scalar1=0.5,
                                                scalar2=None, op0=ALU.mult), reads=["vc"], writes=["dc"])
            op("dve", lambda e: e.tensor_scalar(out=dc[:, 16:20], in0=vc[:, VR["ka"]:VR["ka"] + 4], scalar1=0.5,
                                                scalar2=None, op0=ALU.mult), reads=["vc"], writes=["dc"])
            op("dve", lambda e: e.tensor_scalar(out=dc[:, 20:24], in0=vc[:, VR["ka"]:VR["ka"] + 4], scalar1=-0.5,
                                                scalar2=1.0, op0=ALU.mult, op1=ALU.add), reads=["vc"], writes=["dc"])
            op("dve", lambda e: e.tensor_scalar(out=dc[:, 32:47], in0=vc[:, VR["mu"]:VR["mu"] + 15], scalar1=0.5,
                                                scalar2=None, op0=ALU.mult), reads=["vc"], writes=["dc"])
            op("dve", lambda e: e.tensor_scalar(out=dc[:, 48:63], in0=vc[:, VR["mu"]:VR["mu"] + 15], scalar1=-1.0,
                                                scalar2=1.0, op0=ALU.mult, op1=ALU.add), reads=["vc"], writes=["dc"])
            P.cutoff = p0_stage < 3
            for d in range(2):
                for c in range(4):
                    op("pe", lambda e, d=d, c=c: e.transpose(out=PB[1][:, c * 64:c * 64 + 64], in_=straw[:, d, c, :],
                                                             identity=identf[0:64, 0:64]),
                       reads=["straw%d" % d, "identf"], writes=["pb1"])
                op("dve", lambda e, d=d: e.tensor_copy(out=T32[d][:], in_=PB[1][:, 0:256].rearrange("p (c i) -> p c i", c=4)),
                   reads=["pb1"], writes=["T32_%d" % d])
                op("act", lambda e, d=d: e.activation(out=Tbf[d][:], in_=T32[d][:], func=AF.Copy),
                   reads=["T32_%d" % d], writes=["Tbf_%d" % d])
                if d == 1:
                    op("dve", lambda e: e.tensor_copy(out=T32i[:], in_=T32[1][:]), reads=["T32_1"], writes=["T32i"])
            P.cutoff = p0_stage < 4
            for kc in range(8):
                op("pe", lambda e, kc=kc: e.transpose(out=PB[5][:, kc * 2:kc * 2 + 2], in_=c2raw[0:2, kc * 128:kc * 128 + 128],
                                                      identity=identf[0:2, 0:2]), reads=["c2raw", "identf"], writes=["pb5"])
            op("dve", lambda e: e.tensor_copy(out=c2T[:], in_=PB[5][:, 0:16].rearrange("p (k m) -> p k m", m=2)),
               reads=["pb5"], writes=["c2T"])
            op("act", lambda e: e.activation(out=th0[:], in_=c2T[:], func=AF.Tanh, scale=0.5), reads=["c2T"], writes=["th0"])
            op("dve", lambda e: e.tensor_scalar(out=th0[:], in0=th0[:], scalar1=0.5, scalar2=0.5, op0=ALU.mult, op1=ALU.add),
               reads=["th0"], writes=["th0"])
            op("dve", lambda e: e.tensor_tensor(out=silu[:], in0=th0[:], in1=c2T[:], op=ALU.mult),
               reads=["th0", "c2T"], writes=["silu"])
            for m in range(2):
                op("dve", lambda e, m=m: e.tensor_copy(out=srep[:, m, :, :], in_=bcast(silu[:, :, m:m + 1], [128, 8, 128])),
                   reads=["silu"], writes=["srep%d" % m])
            P.cutoff = p0_stage < 5
            for b in range(12):
                q, half = b // 2, b % 2
                w = wms[b % 2]
                wn = "wms%d" % (b % 2)
                op("sp", lambda e, b=b, w=w: e.dma_start(out=w[:], in_=wmod_d[:, 512 * b:512 * b + 512].rearrange(
                    "(kc p) n -> p kc n", p=128)), writes=[wn], dma="wm%d" % (b % 2))
                if 2 <= b < 10:
                    kcw = b - 2
                    op("sp", lambda e, kcw=kcw: e.dma_start(out=wcvS[:, :, kcw, :].rearrange("o p n -> p o n"),
                                                            in_=wtmp0[kcw][:].rearrange("p (o n) -> p o n", o=12)),
                       reads=["wtmp0_%d" % kcw], writes=["wcvS"], dma="wts")
                if q in (2, 5):
                    gi = 0 if q == 2 else 1
                    for m in range(2):
                        bank = 2 + m
                        for kc in range(8):
                            op("pe", lambda e, m=m, kc=kc, w=w, bank=bank: e.matmul(
                                PB[bank][:, :], lhsT=srep[:, m, kc, :], rhs=w[:, kc, :], start=(kc == 0), stop=(kc == 7)),
                               reads=[wn, "srep%d" % m], writes=["pb%d" % bank])
                        op("dve", lambda e, m=m, gi=gi, half=half, bank=bank: e.tensor_tensor(
                            out=gates[:, m, gi, 512 * half:512 * half + 512], in0=PB[bank][:, :],
                            in1=bmb[:, gi, 512 * half:512 * half + 512], op=ALU.add),
                           reads=["pb%d" % bank, "bmb%d" % gi], writes=["gates"])
                else:
                    qi = {0: 0, 1: 1, 3: 2, 4: 3}[q]
                    for cc in range(4):
                        col = qi * 8 + half * 4 + cc
                        for kc in range(8):
                            op("pe", lambda e, kc=kc, cc=cc, col=col, w=w: e.matmul(
                                PB[4][:, 2 * col:2 * col + 2], lhsT=w[:, kc, cc * 128:cc * 128 + 128], rhs=silu[:, kc, :],
                                start=(kc == 0), stop=(kc == 7)), reads=[wn, "silu"], writes=["pb4"])
            P.cutoff = p0_stage < 6
            op("dve", lambda e: e.tensor_copy(out=modp[:], in_=PB[4][:, 0:64].rearrange("p (c m) -> p c m", m=2)),
               reads=["pb4"], writes=["modp"])
            for m in range(2):
                for (dst, qsc, bname, gname) in ((0, 1, "bsca", "g1"), (2, 3, "bscf", "g2")):
                    op("dve", lambda e, m=m, dst=dst, qsc=qsc, bname=bname: e.scalar_tensor_tensor(
                        out=modc[:, m, dst, :], in0=modp[:, qsc * 8:qsc * 8 + 8, m], scalar=1.0,
                        in1=vc[:, VR[bname]:VR[bname] + 8], op0=ALU.add, op1=ALU.add),
                       reads=["modp", "vc"], writes=["modc"])
                    op("dve", lambda e, m=m, dst=dst, gname=gname: e.tensor_tensor(
                        out=modc[:, m, dst, :], in0=modc[:, m, dst, :], in1=vc[:, VR[gname]:VR[gname] + 8], op=ALU.mult),
                       reads=["modc", "vc"], writes=["modc"])
                for (dst, qsh, bname) in ((1, 0, "bsha"), (3, 2, "bshf")):
                    op("dve", lambda e, m=m, dst=dst, qsh=qsh, bname=bname: e.tensor_tensor(
                        out=modc[:, m, dst, :], in0=modp[:, qsh * 8:qsh * 8 + 8, m], in1=vc[:, VR[bname]:VR[bname] + 8],
                        op=ALU.add), reads=["modp", "vc"], writes=["modc"])
            P.cutoff = p0_stage < 7
            op("sp", lambda e: e.dma_start(out=gatesS[:, :], in_=gates[:].rearrange("p a b n -> p (a b n)")),
               reads=["gates"], writes=["gatesS"], dma="sg")
            build_and_emit(nc, P, bar)

        G = 256
        if stop_after < 1:
            return nc
        with ExitStack() as es:
            P = Prog()
            op = P.op
            winb = sb(es, "winb", [128, 8, 1920], BF16)
            wcv = [sb(es, "wcv%d" % i, [128, 8, 128], BF16) for i in range(4)]
            ww2 = sb(es, "ww2", [128, 512], BF16)
            wa2 = sb(es, "wa2", [128, 512], BF16)
            wg2 = sb(es, "wg2", [128, 512], BF16)
            op("pool", lambda e: e.dma_start(out=ww2[:], in_=ww2_d[:, :]), writes=["ww2"], dma="g0")
            op("pool", lambda e: e.dma_start(out=wa2[:], in_=wa2_d[:, :]), writes=["wa2"], dma="g1")
            op("pool", lambda e: e.dma_start(out=wg2[:], in_=wg2_d[:, :]), writes=["wg2"], dma="g2")
            gC[0] = sb(es, "gC_0", [128, 4, 80], F32)
            ones64 = sb(es, "ones64", [128, 64], F32)
            op("pool", lambda e: e.memset(gC[0][:], 1.0), writes=["gC0"])
            op("pool", lambda e: e.memset(ones64[:], 1.0), writes=["ones64"])
            mSL = sb(es, "mSL", [128, 128], F32)
            mSU = sb(es, "mSU", [128, 128], F32)
            mLI = sb(es, "mLI", [128, 128], F32)
            mUI = sb(es, "mUI", [128, 128], F32)
            for (mt_, pat, cm, cmp_, zr) in ((mSL, -1, 1, ALU.is_gt, (64, 0)), (mLI, -1, 1, ALU.is_ge, (64, 0)),
                                            (mSU, 1, -1, ALU.is_gt, (0, 64)), (mUI, 1, -1, ALU.is_ge, (0, 64))):
                nm = mt_.name
                op("pool", lambda e, t=mt_: e.memset(t[:], 1.0), writes=[nm])
                op("pool", lambda e, t=mt_, pat=pat, cm=cm, cmp_=cmp_: e.affine_select(
                    out=t[:], in_=t[:], pattern=[[pat, 128]], compare_op=cmp_, fill=0.0, base=0,
                    channel_multiplier=cm), reads=[nm], writes=[nm])
                op("pool", lambda e, t=mt_, zr=zr: e.memset(t[zr[0]:zr[0] + 64, zr[1]:zr[1] + 64], 0.0),
                   reads=[nm], writes=[nm])
            xt0_ = sb(es, "xt0", [128, D], F32)
            xt = [xt0_, xt0_]
            xs = [sb(es, "xs%d" % i, [128, D], BF16) for i in range(2)]
            st1 = sb(es, "st1", [128, 8], F32)
            hT = sb(es, "hT", [128, 8, G], BF16)
            zpS = [sb(es, "zpS%d" % i, [128, 4, 66], F32) for i in range(2)]
            zpP = [sb(es, "zpP%d" % i, [128, 1, 258], F32) for i in range(2)]
            upS = sb(es, "upS", [128, 4, 66], F32)
            upP = sb(es, "upP", [128, 1, 258], F32)
            sT = sb(es, "sT", [128, G], F32)
            uT = sb(es, "uT", [128, G], F32)
            rk = sb(es, "rk", [128, 8, G], F32)
            vbf = sb(es, "vbf", [128, 4, G], BF16)
            lor = sb(es, "lor", [128, 3, G], BF16)
            tgl = sb(es, "tgl", [128, G], F32)
            gcs = sb(es, "gcs", [128, G], F32)
            gbs = sb(es, "gbs", [128, G], F32)
            ct = [sb(es, "ct%d" % i, [128, G], F32) for i in range(3)]
            sqc = sb(es, "sqc", [128, 4, G], BF16)
            ocb = sb(es, "ocb", [128, 4, G], BF16)
            kr = sb(es, "kr", [128, 4, G], F32)
            kkt = kr
            sqk = sqc
            rnx = sb(es, "rnx", [128, 512], F32)
            st8 = sb(es, "st8", [128, 4, 8], F32)
            pr = sb(es, "pr", [128, 4, G], BF16)
            prd = sb(es, "prd", [128, G], BF16)
            tw = sb(es, "tw", [128, G], F32)
            Lc = sb(es, "Lc", [128, G], F32)
            nLx = sb(es, "nLx", [128, G], F32)
            Ein = sb(es, "Ein", [128, G], F32)
            Einv = sb(es, "Einv", [128, G], F32)
            Eex = sb(es, "Eex", [128, G], F32)
            ta = sb(es, "ta", [128, G], F32)
            qq = sb(es, "qq", [128, G], F32)
            kd = sb(es, "kd", [128, G], F32)
            bq = sb(es, "bq", [128, G], F32)
            Rt = [sb(es, "Rt%d" % d, [128, 4, G], BF16) for d in range(2)]
            Kt = [sb(es, "Kt%d" % d, [128, 4, G], BF16) for d in range(2)]
            Bt = [sb(es, "Bt%d" % d, [128, 4, G], BF16) for d in range(2)]
            At = [sb(es, "At%d" % d, [128, 4, G], BF16) for d in range(2)]
            Atm = [sb(es, "Atm%d" % d, [128, 512], BF16) for d in range(2)]
            Btm = [sb(es, "Btm%d" % d, [128, 512], BF16) for d in range(2)]
            Ktm = [sb(es, "Ktm%d" % d, [128, 512], BF16) for d in range(2)]
            Vtm = sb(es, "Vtm", [128, 512], BF16)
            bon = sb(es, "bon", [128, 16], F32)
            bvt = sb(es, "bvt", [128, 512], BF16)
            gtm = sb(es, "gtm", [128, 512], BF16)
            ysum = sb(es, "ysum", [128, 512], F32)
            UB = {}
            for d in range(2):
                for p in range(2):
                    u = {}
                    for nm in ("ab", "abT", "akT", "rbT", "rkT", "P1", "PT0", "PT1", "X0", "X1"):
                        u[nm] = sb(es, "u%d%d_%s" % (d, p, nm), [128, 4, 128], BF16)
                    u["P0"] = u["akT"]
                    UB[(d, p)] = u
            mtT = [sb(es, "mtT%d" % d, [128, 2, 4, 128], BF16) for d in range(2)]
            nnT = [sb(es, "nnT%d" % d, [128, 2, 4, 64], F32) for d in range(2)]
            qtT = [sb(es, "qtT%d" % d, [128, 2, 4, 128], BF16) for d in range(2)]
            ctmp = sb(es, "ctmp", [128, 4, 64], F32)
            otr = sb(es, "otr", [64, 128], F32)

            for d in range(2):
                op("pool", lambda e, d=d: e.memset(qtT[d][:], 0.0), writes=["qtT%d" % d])
                op("pool", lambda e, d=d: e.memset(mtT[d][:], 0.0), writes=["mtT%d" % d])
            for t_ in zpS + zpP + [upS, upP]:
                op("pool", lambda e, t_=t_: e.memset(t_[:], 0.0), writes=[t_.name])
            for kc in range(8):
                op("pool", lambda e, kc=kc: e.dma_start(out=winb[:, kc, :], in_=win_d[kc * 128:kc * 128 + 128, 0:1920]),
                   writes=["winb%d" % kc], dma="wi%d" % kc)

            WINB = ["winb%d" % kc for kc in range(8)]

            zrot = [0]
            ubk = [0]
            UBANKS = [4, 5, 6, 7, 0, 1]
            CONV_ORDER = [19, 23, 15, 20, 24, 16, 21, 25, 17, 22, 26, 18]

            def load_wcv(pos):
                oc = CONV_ORDER[pos]
                wsl = pos % 4
                op("sp", lambda e, oc=oc, wsl=wsl: e.dma_start(out=wcv[wsl][:].rearrange("p k n -> p (k n)"),
                                                               in_=wcvS[oc - 15].rearrange("p k n -> p (k n)")),
                   reads=["wcvS"], writes=["wcv%d" % wsl], dma="wcv%d" % wsl)
            P.cutoff = p1_stage < 1

            def next_ubank():
                b = UBANKS[ubk[0] % len(UBANKS)]
                ubk[0] += 1
                return b

            def do_group(gi):
                isS = gi < 16
                m = 0 if isS else 1
                tok0 = gi * G
                rows, rl = (4, 64) if isS else (1, 256)
                zp = zpS if isS else zpP
                up_ = upS if isS else upP

                def v3(ap2):
                    return ap2.rearrange("p (r l) -> p r l", r=rows)

                for pos_ in range(4):
                    load_wcv(pos_)
                for tl in range(2):
                    ti = gi * 2 + tl
                    X, XS = xt[tl], xs[tl]
                    xn, xsn = "xt0", "xs%d" % tl
                    op("sp", lambda e, X=X, ti=ti: e.dma_start(out=X[:], in_=x_d[ti * 128:ti * 128 + 128, :]),
                       writes=[xn], dma="x0")
                    op("act", lambda e, X=X, XS=XS, tl=tl: e.activation(out=XS[:], in_=X[:], func=AF.Square,
                                                                        accum_out=st1[:, tl:tl + 1]),
                       reads=[xn], writes=[xsn, "st1a%d" % tl])
                    op("dve", lambda e, tl=tl: e.tensor_scalar(out=st1[:, 2 + tl:3 + tl], in0=st1[:, tl:tl + 1],
                                                               scalar1=1.0 / D, scalar2=NORM_EPS, op0=ALU.mult, op1=ALU.add),
                       reads=["st1a%d" % tl], writes=["st1b%d" % tl])
                    op("pool", lambda e, tl=tl: e.tensor_tensor(out=st1[:, 4 + tl:5 + tl], in0=st1[:, 2 + tl:3 + tl],
                                                                in1=nhalf[:, 0:1], op=ALU.pow),
                       reads=["st1b%d" % tl, "nhalf"], writes=["st1c%d" % tl])
                    op("dve", lambda e, X=X, XS=XS, tl=tl: e.tensor_scalar(out=XS[:], in0=X[:], scalar1=st1[:, 4 + tl:5 + tl],
                                                                           scalar2=None, op0=ALU.mult),
                       reads=[xn, "st1c%d" % tl], writes=[xsn])
                    for kc in range(8):
                        op("pe", lambda e, XS=XS, kc=kc, tl=tl: e.transpose(
                            out=pbf(0)[:, kc * 128:kc * 128 + 128] if tl == 0 else pbf(1)[:, kc * 128:kc * 128 + 128],
                            in_=XS[:, kc * 128:kc * 128 + 128], identity=identb[:]),
                           reads=[xsn, "identb"], writes=["pb%d" % tl])
                    for kc in range(8):
                        op("act", lambda e, kc=kc, tl=tl, m=m: e.activation(
                            out=hT[:, kc, tl * 128:tl * 128 + 128], in_=pbf(tl)[:, kc * 128:kc * 128 + 128],
                            func=AF.Identity, scale=modc[:, m, 0, kc:kc + 1], bias=modc[:, m, 1, kc:kc + 1]),
                           reads=["pb%d" % tl, "modc"], writes=["hT%d" % kc])
                HT = ["hT%d" % kc for kc in range(8)]

                def zmm(oc):
                    slot = zrot[0] % 2
                    zrot[0] += 1
                    bank = 2 + slot
                    apz = PB[bank][:, 0:256]
                    rn_ = "pb%d" % bank
                    if oc < 15:
                        for kc in range(8):
                            op("pe", lambda e, kc=kc, oc=oc, apz=apz: e.matmul(
                                apz, lhsT=winb[:, kc, oc * 128:oc * 128 + 128], rhs=hT[:, kc, :], start=(kc == 0), stop=(kc == 7)),
                               reads=[WINB[kc], HT[kc]], writes=[rn_])
                    else:
                        pos = CONV_ORDER.index(oc)
                        wsl = pos % 4
                        for kc in range(8):
                            op("pe", lambda e, kc=kc, wsl=wsl, apz=apz: e.matmul(
                                apz, lhsT=wcv[wsl][:, kc, :], rhs=hT[:, kc, :], start=(kc == 0), stop=(kc == 7)),
                               reads=["wcv%d" % wsl, HT[kc]], writes=[rn_])
                        if pos + 4 < 12:
                            load_wcv(pos + 4)
                    return apz, rn_

                P.cutoff = p1_stage < 2
                import os
                A2N = int(os.environ.get("A2N", "15"))
                A2K = int(os.environ.get("A2K", "99"))
                for oc in range(A2N):
                    apz, rn_ = zmm(oc)
                    zb = zp[oc % 2]
                    zbn = zb.name
                    if A2K < 1:
                        continue
                    op("act", lambda e, apz=apz, zb=zb: e.activation(out=zb[:, :, 1:rl + 1], in_=v3(apz), func=AF.Copy),
                       reads=[rn_], writes=[zbn])
                    if A2K < 2:
                        continue
                    op("act", lambda e, apz=apz, oc=oc: e.activation(out=v3(sT[:]), in_=v3(apz), func=AF.Copy,
                                                                     scale=dc[:, 48 + oc:49 + oc]),
                       reads=[rn_, "dc"], writes=["sT"])
                    op("dve", lambda e, zb=zb, oc=oc: e.scalar_tensor_tensor(out=v3(uT[:]), in0=zb[:, :, 0:rl], scalar=dc[:, 32 + oc:33 + oc],
                                                                             in1=v3(sT[:]), op0=ALU.mult, op1=ALU.add),
                       reads=["sT", zbn, "dc"], writes=["uT"])
                    if oc < 8:
                        dst, dn = rk[:, oc, :], "rk%d" % oc
                    elif oc < 12:
                        dst, dn = vbf[:, oc - 8, :], "vbf%d" % (oc - 8)
                    else:
                        dst, dn = (tgl[:] if oc == 14 else uT[:]), ("tgl" if oc == 14 else "uT")
                    if oc < 12 or oc == 14:
                        op("dve", lambda e, zb=zb, dst=dst, oc=oc: e.scalar_tensor_tensor(
                            out=v3(dst), in0=zb[:, :, 2:rl + 2], scalar=dc[:, 32 + oc:33 + oc],
                            in1=v3(uT[:]), op0=ALU.mult, op1=ALU.add), reads=["uT", zbn, "dc"], writes=[dn])
                    else:
                        op("dve", lambda e, zb=zb, oc=oc: e.scalar_tensor_tensor(
                            out=v3(sT[:]), in0=zb[:, :, 2:rl + 2], scalar=dc[:, 32 + oc:33 + oc],
                            in1=v3(uT[:]), op0=ALU.mult, op1=ALU.add), reads=["uT", zbn, "dc"], writes=["sT"])
                        if oc == 12:
                            op("act", lambda e: e.activation(out=lor[:, 0, :], in_=sT[:], func=AF.Tanh),
                               reads=["sT"], writes=["lor0"])
                        else:
                            op("act", lambda e: e.activation(out=lor[:, 1, :], in_=sT[:], func=AF.Copy),
                               reads=["sT"], writes=["lor1"])
                    if oc == 14:
                        op("act", lambda e: e.activation(out=tgl[:], in_=tgl[:], func=AF.Tanh, scale=0.5),
                           reads=["tgl"], writes=["tgl"])
                        op("dve", lambda e: e.tensor_scalar(out=lor[:, 2, :], in0=tgl[:], scalar1=0.5, scalar2=0.5,
                                                            op0=ALU.mult, op1=ALU.add), reads=["tgl"], writes=["lor2"])

                P.cutoff = p1_stage < 3
                for j in range(4):
                    apz, rn_ = zmm(19 + j)
                    op("act", lambda e, apz=apz: e.activation(out=gcs[:], in_=apz, func=AF.Copy), reads=[rn_], writes=["gcs"])
                    apz2, rn2 = zmm(23 + j)
                    op("dve", lambda e, apz2=apz2: e.tensor_tensor(out=up_[:, :, 1:rl + 1], in0=v3(apz2), in1=v3(gcs[:]),
                                                                   op=ALU.mult), reads=[rn2, "gcs"], writes=["upad"])
                    apz3, rn3 = zmm(15 + j)
                    op("act", lambda e, apz3=apz3: e.activation(out=gbs[:], in_=apz3, func=AF.Copy), reads=[rn3], writes=["gbs"])
                    op("act", lambda e, j=j: e.activation(out=v3(ct[0][:]), in_=up_[:, :, 0:rl], func=AF.Copy,
                                                          scale=vc[:, VR["cw0"] + j:VR["cw0"] + j + 1]),
                       reads=["upad", "vc"], writes=["ct0"])
                    op("dve", lambda e, j=j: e.scalar_tensor_tensor(out=v3(ct[1][:]), in0=up_[:, :, 1:rl + 1],
                                                                    scalar=vc[:, VR["cw1"] + j:VR["cw1"] + j + 1],
                                                                    in1=v3(ct[0][:]), op0=ALU.mult, op1=ALU.add),
                       reads=["upad", "ct0", "vc"], writes=["ct1"])
                    op("dve", lambda e, j=j: e.scalar_tensor_tensor(out=v3(ct[2][:]), in0=up_[:, :, 2:rl + 2],
                                                                    scalar=vc[:, VR["cw2"] + j:VR["cw2"] + j + 1],
                                                                    in1=v3(ct[1][:]), op0=ALU.mult, op1=ALU.add),
                       reads=["upad", "ct1", "vc"], writes=["ct2"])
                    op("dve", lambda e: e.tensor_tensor(out=ct[0][:], in0=ct[2][:], in1=gbs[:], op=ALU.mult),
                       reads=["ct2", "gbs"], writes=["ct0"])
                    op("act", lambda e, j=j: e.activation(out=sqc[:, j, :], in_=ct[0][:], func=AF.Square),
                       reads=["ct0"], writes=["sqc%d" % j])
                    op("dve", lambda e, j=j: e.tensor_scalar(out=ocb[:, j, :], in0=ct[0][:],
                                                             scalar1=vc[:, VR["cg"] + j:VR["cg"] + j + 1], scalar2=None,
                                                             op0=ALU.mult), reads=["ct0", "vc"], writes=["ocb%d" % j])
                for tl in range(2):
                    for j in range(4):
                        op("pe", lambda e, tl=tl, j=j: e.matmul(PB[1][:, 16 + tl:17 + tl], lhsT=sqc[:, j, tl * 128:tl * 128 + 128],
                                                                rhs=onescol[:, 0:1], start=(j == 0), stop=(j == 3)),
                           reads=["sqc%d" % j, "onescol"], writes=["pb1"])
                for tl in range(2):
                    ti = gi * 2 + tl
                    op("dve", lambda e, tl=tl: e.tensor_scalar(out=st1[:, 6 + tl:7 + tl], in0=PB[1][:, 16 + tl:17 + tl],
                                                               scalar1=1.0 / 512, scalar2=NORM_EPS, op0=ALU.mult, op1=ALU.add),
                       reads=["pb1"], writes=["st1d%d" % tl])
                    op("pool", lambda e, tl=tl, ti=ti: e.tensor_tensor(out=rcall[:, ti:ti + 1], in0=st1[:, 6 + tl:7 + tl],
                                                                       in1=nhalf[:, 0:1], op=ALU.pow),
                       reads=["st1d%d" % tl, "nhalf"], writes=["rcall"])
                    op("sp", lambda e, tl=tl, ti=ti: e.dma_start(
                        out=ocS[ti].rearrange("p (c t) -> p c t", c=4), in_=ocb[:, :, tl * 128:tl * 128 + 128]),
                       reads=["ocb%d" % j for j in range(4)], writes=["ocS"], dma="s_oc")

                P.cutoff = p1_stage < 4
                for c in range(4):
                    op("dve", lambda e, c=c: e.tensor_scalar(out=kr[:, c, :], in0=rk[:, 4 + c, :],
                                                             scalar1=vc[:, VR["kk"] + c:VR["kk"] + c + 1], scalar2=None,
                                                             op0=ALU.mult), reads=["rk%d" % (4 + c), "vc"], writes=["kr%d" % c])
                    op("act", lambda e, c=c: e.activation(out=sqk[:, c, :], in_=kr[:, c, :], func=AF.Square),
                       reads=["kr%d" % c], writes=["sqc%d" % c])
                    op("dve", lambda e, c=c: e.tensor_scalar(out=pr[:, c, :], in0=rk[:, c, :],
                                                             scalar1=vc[:, VR["rk"] + c:VR["rk"] + c + 1], scalar2=None,
                                                             op0=ALU.mult), reads=["rk%d" % c, "vc"], writes=["pr%d" % c])
                for tl in range(2):
                    for c in range(4):
                        op("pe", lambda e, tl=tl, c=c: e.matmul(PB[1][:, 32 + tl * 8:40 + tl * 8], lhsT=sqk[:, c, tl * 128:tl * 128 + 128],
                                                                rhs=ind8[:, c, :], start=(c == 0), stop=(c == 3)),
                           reads=["sqc%d" % c, "ind8"], writes=["pb1"])
                for tl in range(2):
                    op("dve", lambda e, tl=tl: e.tensor_scalar(out=st8[:, 0, :], in0=PB[1][:, 32 + tl * 8:40 + tl * 8],
                                                               scalar1=1e-12, scalar2=None, op0=ALU.add),
                       reads=["pb1"], writes=["st8a"])
                    op("pool", lambda e: e.tensor_tensor(out=st8[:, 1, :], in0=st8[:, 0, :], in1=nhalf[:, 0:8], op=ALU.pow),
                       reads=["st8a", "nhalf"], writes=["st8b"])
                    op("dve", lambda e: e.tensor_copy(out=rnx[:].rearrange("p (h j) -> p h j", h=8),
                                                      in_=bcast(st8[:, 1, :].unsqueeze(2), [128, 8, 64])),
                       reads=["st8b"], writes=["rnx"])
                    for c in range(4):
                        op("pe", lambda e, c=c: e.transpose(out=PB[0][:, c * 128:c * 128 + 128], in_=rnx[:, c * 128:c * 128 + 128],
                                                            identity=identf[:]), reads=["rnx", "identf"], writes=["pb0"])
                    op("dve", lambda e, tl=tl: e.tensor_tensor(out=kkt[:, :, tl * 128:tl * 128 + 128],
                                                               in0=PB[0][:, :].rearrange("p (c t) -> p c t", c=4),
                                                               in1=kr[:, :, tl * 128:tl * 128 + 128], op=ALU.mult),
                       reads=["pb0"] + ["kr%d" % c for c in range(4)], writes=["kkt%d" % tl])
                KKT = ["kkt0", "kkt1"]

                P.cutoff = p1_stage < 5
                op("dve", lambda e: e.memset(PB[1][:, 64:80], 0.0), writes=["pb1"])
                def a3b(d):
                    dr = slice(64 * d, 64 * d + 64)
                    for c in range(4):
                        lp = PB[0][:, 256:512]
                        op("pe", lambda e, c=c, dr=dr, lp=lp: e.matmul(lp, lhsT=ww2[dr, c * 128:c * 128 + 128], rhs=lor[dr, 0, :],
                                                                       start=True, stop=True),
                           reads=["ww2", "lor0"], writes=["pb0"])
                        op("act", lambda e, c=c, d=d, lp=lp: e.activation(out=tw[:], in_=lp, func=AF.Tanh, scale=0.5,
                                                                          bias=dc[:, d * 4 + c:d * 4 + c + 1]),
                           reads=["pb0", "dc"], writes=["tw"])
                        for ch in range(4):
                            if d == 0:
                                sl = slice(ch * 64, ch * 64 + 64)
                            else:
                                sl = slice(ch * 64 + 63, ch * 64 - 1 if ch > 0 else None, -1)
                            op("dve", lambda e, sl=sl: e.tensor_tensor_scan(out=Lc[:, sl], data0=tw[:, sl], data1=ones64[:],
                                                                            initial=0.0, op0=ALU.add, op1=ALU.add),
                               reads=["tw", "ones64"], writes=["Lc"])
                        op("dve", lambda e: e.scalar_tensor_tensor(out=nLx[:], in0=tw[:], scalar=1.0, in1=Lc[:],
                                                                   op0=ALU.add, op1=ALU.subtract), reads=["tw", "Lc"], writes=["nLx"])
                        op("act", lambda e: e.activation(out=Ein[:], in_=Lc[:], func=AF.Exp, scale=-KAPPA), reads=["Lc"], writes=["Ein"])
                        op("act", lambda e: e.activation(out=Einv[:], in_=Lc[:], func=AF.Exp, scale=KAPPA), reads=["Lc"], writes=["Einv"])
                        op("act", lambda e: e.activation(out=Eex[:], in_=nLx[:], func=AF.Exp, scale=KAPPA), reads=["nLx"], writes=["Eex"])
                        cs = 63 if d == 0 else 0
                        op("pool", lambda e, d=d, c=c, cs=cs, gi=gi: e.tensor_copy(
                            out=gC[d][:, c, gi * 4:gi * 4 + 4], in_=Ein[:].rearrange("p (ch t) -> p ch t", t=64)[:, :, cs]),
                           reads=["Ein"], writes=["gC%d" % d])
                        ap_ = PB[0][:, 0:256]
                        op("pe", lambda e, c=c, dr=dr, ap_=ap_: e.matmul(ap_, lhsT=wa2[dr, c * 128:c * 128 + 128], rhs=lor[dr, 1, :],
                                                                         start=True, stop=True),
                           reads=["wa2", "lor1"], writes=["pb0"])
                        op("act", lambda e, c=c, d=d, ap_=ap_: e.activation(out=ta[:], in_=ap_, func=AF.Tanh, scale=0.5,
                                                                            bias=dc[:, 8 + d * 4 + c:8 + d * 4 + c + 1]),
                           reads=["pb0", "dc"], writes=["ta"])
                        op("dve", lambda e, c=c: e.tensor_scalar(out=qq[:], in0=ta[:], scalar1=dc[:, 16 + c:17 + c],
                                                                 scalar2=dc[:, 20 + c:21 + c], op0=ALU.mult, op1=ALU.add),
                           reads=["ta", "dc"], writes=["qq"])
                        op("dve", lambda e, c=c: e.tensor_tensor(out=kd[:], in0=qq[:], in1=rk[:, 4 + c, :], op=ALU.mult),
                           reads=["qq", "rk%d" % (4 + c)], writes=["kd"])
                        op("pool", lambda e, c=c: e.tensor_tensor(out=prd[:], in0=kd[:], in1=pr[:, c, :], op=ALU.mult),
                           reads=["kd", "pr%d" % c], writes=["prd"])
                        for tl in range(2):
                            op("pe", lambda e, tl=tl, c=c, d=d: e.matmul(PB[1][:, 64 + tl * 8:72 + tl * 8],
                                                                         lhsT=prd[:, tl * 128:tl * 128 + 128], rhs=ind8[:, c, :],
                                                                         start=False, stop=(c == 3 and d == 1), skip_group_check=True),
                               reads=["prd", "ind8"], writes=["pb1"])
                        op("dve", lambda e, c=c: e.scalar_tensor_tensor(out=bq[:], in0=ta[:], scalar=1.0, in1=kkt[:, c, :],
                                                                        op0=ALU.add, op1=ALU.mult),
                           reads=["ta"] + KKT, writes=["bq"])
                        op("pool", lambda e, c=c, d=d: e.tensor_tensor(out=Rt[d][:, c, :], in0=rk[:, c, :], in1=Ein[:], op=ALU.mult),
                           reads=["rk%d" % c, "Ein"], writes=["Rt%d_%d" % (d, c)])
                        op("pool", lambda e, c=c, d=d: e.tensor_tensor(out=Kt[d][:, c, :], in0=kd[:], in1=Einv[:], op=ALU.mult),
                           reads=["kd", "Einv"], writes=["Kt%d_%d" % (d, c)])
                        op("dve", lambda e, c=c, d=d: e.scalar_tensor_tensor(out=Bt[d][:, c, :], in0=bq[:], scalar=0.5, in1=Einv[:],
                                                                             op0=ALU.mult, op1=ALU.mult),
                           reads=["bq", "Einv"], writes=["Bt%d_%d" % (d, c)])
                        op("dve", lambda e, c=c, d=d: e.scalar_tensor_tensor(out=At[d][:, c, :], in0=kkt[:, c, :], scalar=-1.0,
                                                                             in1=Eex[:], op0=ALU.mult, op1=ALU.mult),
                           reads=KKT + ["Eex"], writes=["At%d_%d" % (d, c)])
                        yield
                for _ in a3b(0):
                    pass

                P.cutoff = p1_stage < 6
                def fmres(nm, d):
                    return ["%s%d_%d" % (nm, d, c) for c in range(4)]

                def do_tile(tl):
                    ti = gi * 2 + tl
                    cols = slice(tl * 128, tl * 128 + 128)
                    def tm_entries(dirs, with_v):
                        ent = []
                        for d in dirs:
                            ent += [(At[d], Atm[d], fmres("At", d), "Atm%d" % d), (Bt[d], Btm[d], fmres("Bt", d), "Btm%d" % d),
                                    (Kt[d], Ktm[d], fmres("Kt", d), "Ktm%d" % d)]
                        if with_v:
                            ent.append((vbf, Vtm, ["vbf%d" % c for c in range(4)], "Vtm"))
                        return ent

                    def tm_transposes(entries):
                        for qi_, (src, dstt, srcn, dstn) in enumerate(entries):
                            half = qi_ % 2
                            for c in range(4):
                                op("pe", lambda e, src=src, c=c, half=half: e.transpose(
                                    out=pbf(0)[:, half * 512 + c * 128:half * 512 + c * 128 + 128], in_=src[:, c, cols], identity=identb[:]),
                                   reads=srcn + ["identb"], writes=["pb0"])
                            if qi_ % 2 == 0:
                                op("act", lambda e, dstt=dstt, half=half: e.activation(out=dstt[:], in_=pbf(0)[:, half * 512:half * 512 + 512],
                                                                                       func=AF.Copy), reads=["pb0"], writes=[dstn])
                            else:
                                op("dve", lambda e, dstt=dstt, half=half: e.tensor_copy(out=dstt[:], in_=pbf(0)[:, half * 512:half * 512 + 512]),
                                   reads=["pb0"], writes=[dstn])

                    def bv_part():
                        op("pool", lambda e: e.tensor_tensor(out=bvt[:].rearrange("p (h j) -> p h j", h=8),
                                                             in0=Vtm[:].rearrange("p (h j) -> p h j", h=8),
                                                             in1=bcast(bon[:, tl * 8:tl * 8 + 8].unsqueeze(2), [128, 8, 64]), op=ALU.mult),
                           reads=["Vtm", "bon"], writes=["bvt"])
                        op("sp", lambda e: e.dma_start(out=bvS[ti], in_=bvt[:]), reads=["bvt"], writes=["bvS"], dma="s_bv")

                    def g_part():
                        op("pe", lambda e: e.matmul(PB[3][:, :], lhsT=lor[:, 2, cols], rhs=wg2[:], start=True, stop=True),
                           reads=["lor2", "wg2"], writes=["pb3"])
                        op("act", lambda e: e.activation(out=gtm[:], in_=PB[3][:, :], func=AF.Copy), reads=["pb3"], writes=["gtm"])
                        op("sp", lambda e: e.dma_start(out=gS[ti], in_=gtm[:]), reads=["gtm"], writes=["gS"], dma="s_g")
                        op("dve", lambda e: e.memset(PB[2][:, :], 0.0), writes=["pb2"])
                        op("dve", lambda e: e.memset(PB[3][:, :], 0.0), writes=["pb3"])

                    first_y = {}
                    def do_unit(d, p):
                        if d == 0:
                            masks = {"ab": mSL, "abT": mSU, "akT": mSU, "rbT": mUI, "rkT": mUI}
                        else:
                            masks = {"ab": mSU, "abT": mSL, "akT": mSL, "rbT": mLI, "rkT": mLI}
                        if True:
                            pb_ = 64 * p
                            ypsn = "pb%d" % (2 + p)
                            ub = UB[(d, p)]
                            un = lambda s, p=p, d=d: "u%d%d_%s" % (d, p, "akT" if s == "P0" else s)
                            wcol, ucol = pb_, 64 - pb_
                            prs = slice(pb_, pb_ + 64)
                            P.cutoff = p1_stage < 7.05
                            prods = (("ab", At, Bt, "At", "Bt"), ("abT", Bt, At, "Bt", "At"), ("akT", Kt, At, "Kt", "At"),
                                     ("rbT", Bt, Rt, "Bt", "Rt"), ("rkT", Kt, Rt, "Kt", "Rt"))
                            for (nm, Lm, Rm, ln_, rn__) in prods:
                                bk = next_ubank()
                                for c in range(4):
                                    op("pe", lambda e, bk=bk, c=c, Lm=Lm, Rm=Rm, d=d, prs=prs: e.matmul(
                                        PB[bk][:, c * 128:c * 128 + 128], lhsT=Lm[d][prs, c, cols], rhs=Rm[d][prs, c, cols],
                                        start=True, stop=True), reads=fmres(ln_, d) + fmres(rn__, d), writes=["pb%d" % bk])
                                mk = masks[nm]
                                op("dve", lambda e, bk=bk, nm=nm, ub=ub, mk=mk: e.tensor_tensor(
                                    out=ub[nm][:], in0=PB[bk][:, :].rearrange("p (c t) -> p c t", c=4),
                                    in1=bcast(mk[:].unsqueeze(1), [128, 4, 128]), op=ALU.mult),
                                   reads=["pb%d" % bk, mk.name], writes=[un(nm)])
                                yield
                            P.cutoff = p1_stage < 7.15
                            bk = next_ubank()
                            for c in range(4):
                                h = 2 * c + p
                                op("pe", lambda e, bk=bk, c=c, h=h, ub=ub: e.matmul(
                                    PB[bk][:, c * 64:c * 64 + 64], lhsT=ub["akT"][:, c, :], rhs=Vtm[:, h * 64:h * 64 + 64],
                                    start=True, stop=True), reads=[un("akT"), "Vtm"], writes=["pb%d" % bk])
                            op("act", lambda e, bk=bk, ub=ub, ucol=ucol: e.activation(
                                out=ub["X0"][:, :, ucol:ucol + 64], in_=PB[bk][:, 0:256].rearrange("p (c i) -> p c i", c=4),
                                func=AF.Copy), reads=["pb%d" % bk], writes=[un("X0")])
                            op("act", lambda e, ub=ub, wcol=wcol, d=d, p=p: e.activation(
                                out=ub["X0"][:, :, wcol:wcol + 64],
                                in_=Atm[d][:].rearrange("p (c q j) -> p c q j", c=4, q=2)[:, :, p, :], func=AF.Copy),
                               reads=["Atm%d" % d], writes=[un("X0")])
                            yield
                            P.cutoff = p1_stage < 7.25
                            Pk = [ub["ab"], ub["P1"], ub["P0"]]
                            PTk = [ub["abT"], ub["PT1"], ub["PT0"]]
                            Pn = [un("ab"), un("P1"), un("P0")]
                            PTn = [un("abT"), un("PT1"), un("PT0")]

                            def lvl(k):
                                return 0 if k == 0 else 1 + ((k - 1) % 2)
                            Xs = [ub["X0"], ub["X1"]]
                            Xn = [un("X0"), un("X1")]
                            for k in range(6):
                                a_, b_ = lvl(k), lvl(k + 1)
                                bk = next_ubank()
                                for c in range(4):
                                    op("pe", lambda e, bk=bk, c=c, a_=a_, k=k, PTk=PTk, Xs=Xs: e.matmul(
                                        PB[bk][:, c * 128:c * 128 + 128], lhsT=PTk[a_][:, c, :], rhs=Xs[k % 2][:, c, :],
                                        start=True, stop=True), reads=[PTn[a_], Xn[k % 2]], writes=["pb%d" % bk])
                                op("dve", lambda e, bk=bk, k=k, Xs=Xs: e.tensor_tensor(
                                    out=Xs[(k + 1) % 2][:], in0=PB[bk][:, :].rearrange("p (c t) -> p c t", c=4), in1=Xs[k % 2][:],
                                    op=ALU.add), reads=["pb%d" % bk, Xn[k % 2]], writes=[Xn[(k + 1) % 2]])
                                if k < 5:
                                    bk = next_ubank()
                                    for c in range(4):
                                        op("pe", lambda e, bk=bk, c=c, a_=a_, Pk=Pk, PTk=PTk: e.matmul(
                                            PB[bk][:, c * 128:c * 128 + 128], lhsT=Pk[a_][:, c, :], rhs=PTk[a_][:, c, :],
                                            start=True, stop=True), reads=[Pn[a_], PTn[a_]], writes=["pb%d" % bk])
                                    bk2 = None
                                    if k < 4:
                                        bk2 = next_ubank()
                                        for c in range(4):
                                            op("pe", lambda e, bk2=bk2, c=c, a_=a_, Pk=Pk, PTk=PTk: e.matmul(
                                                PB[bk2][:, c * 128:c * 128 + 128], lhsT=PTk[a_][:, c, :], rhs=Pk[a_][:, c, :],
                                                start=True, stop=True), reads=[Pn[a_], PTn[a_]], writes=["pb%d" % bk2])
                                    op("act", lambda e, bk=bk, b_=b_, PTk=PTk: e.activation(
                                        out=PTk[b_][:], in_=PB[bk][:, :].rearrange("p (c t) -> p c t", c=4), func=AF.Copy),
                                       reads=["pb%d" % bk], writes=[PTn[b_]])
                                    if bk2 is not None:
                                        op("act", lambda e, bk2=bk2, b_=b_, Pk=Pk: e.activation(
                                            out=Pk[b_][:], in_=PB[bk2][:, :].rearrange("p (c t) -> p c t", c=4), func=AF.Copy),
                                           reads=["pb%d" % bk2], writes=[Pn[b_]])
                                yield
                            X6, X6n = Xs[0], Xn[0]
                            P.cutoff = p1_stage < 7.35
                            for ch in range(2):
                                bk = next_ubank()
                                chr_ = slice(ch * 64, ch * 64 + 64)
                                for c in range(4):
                                    h = 2 * c + p
                                    op("pe", lambda e, bk=bk, ch=ch, c=c, h=h, chr_=chr_, X6=X6, d=d: e.matmul(
                                        PB[bk][:, c * 64:c * 64 + 64], lhsT=X6[chr_, c, :],
                                        rhs=Btm[d][chr_, h * 64:h * 64 + 64], start=True, stop=True),
                                       reads=[X6n, "Btm%d" % d], writes=["pb%d" % bk])
                                op("act", lambda e, bk=bk, d=d, prs=prs, pb_=pb_, ch=ch: e.activation(
                                    out=mtT[d][prs, ch, :, pb_:pb_ + 64],
                                    in_=PB[bk][prs, 0:256].rearrange("p (c j) -> p c j", c=4), func=AF.Copy),
                                   reads=["pb%d" % bk], writes=["mtT%d_%d" % (d, p)])
                                yield
                            P.cutoff = p1_stage < 7.45
                            for ch in range(2):
                                bk = next_ubank()
                                chr_ = slice(ch * 64, ch * 64 + 64)
                                for c in range(4):
                                    h = 2 * c + p
                                    o_ = PB[bk][:, c * 64:c * 64 + 64]
                                    op("pe", lambda e, o_=o_, c=c, chr_=chr_, X6=X6, d=d, ucol=ucol: e.matmul(
                                        o_, lhsT=Btm[d][chr_, c * 128:c * 128 + 128], rhs=X6[chr_, c, ucol:ucol + 64],
                                        start=True, stop=False), reads=[X6n, "Btm%d" % d], writes=["pb%d" % bk])
                                    op("pe", lambda e, o_=o_, c=c, h=h, chr_=chr_, d=d: e.matmul(
                                        o_, lhsT=Ktm[d][chr_, c * 128:c * 128 + 128], rhs=Vtm[chr_, h * 64:h * 64 + 64],
                                        start=False, stop=True), reads=["Ktm%d" % d, "Vtm"], writes=["pb%d" % bk])
                                op("act", lambda e, bk=bk, d=d, prs=prs, ch=ch: e.activation(
                                    out=nnT[d][prs, ch, :, :], in_=PB[bk][prs, 0:256].rearrange("p (c i) -> p c i", c=4), func=AF.Copy),
                                   reads=["pb%d" % bk], writes=["nnT%d_%d" % (d, p)])
                                yield
                            P.cutoff = p1_stage < 7.55
                            bk = next_ubank()
                            for c in range(4):
                                op("pe", lambda e, bk=bk, c=c, X6=X6, ub=ub: e.matmul(
                                    PB[bk][:, c * 128:c * 128 + 128], lhsT=X6[:, c, :], rhs=ub["rbT"][:, c, :], start=True, stop=True),
                                   reads=[X6n, un("rbT")], writes=["pb%d" % bk])
                            for ch in range(2):
                                op("dve", lambda e, bk=bk, ch=ch, d=d, prs=prs: e.tensor_tensor(
                                    out=qtT[d][prs, ch, :, ch * 64:ch * 64 + 64],
                                    in0=PB[bk][prs, :].rearrange("p (c t) -> p c t", c=4)[:, :, ch * 64:ch * 64 + 64],
                                    in1=Rt[d][prs, :, tl * 128 + ch * 64:tl * 128 + ch * 64 + 64], op=ALU.add),
                                   reads=["pb%d" % bk] + fmres("Rt", d), writes=["qtT%d_%d" % (d, p)])
                            yield
                            P.cutoff = p1_stage < 7.65
                            for c in range(4):
                                h = 2 * c + p
                                yo = PB[2 + p][:, h * 64:h * 64 + 64]
                                st_ = (d == 0)
                                op("pe", lambda e, yo=yo, c=c, X6=X6, ub=ub, ucol=ucol, st_=st_: e.matmul(
                                    yo, lhsT=ub["rbT"][:, c, :], rhs=X6[:, c, ucol:ucol + 64], start=False, stop=False, skip_group_check=True),
                                   reads=[un("rbT"), X6n], writes=[ypsn])
                                op("pe", lambda e, yo=yo, c=c, h=h, ub=ub, d=d: e.matmul(
                                    yo, lhsT=ub["rkT"][:, c, :], rhs=Vtm[:, h * 64:h * 64 + 64], start=False, stop=(d == 1), skip_group_check=True),
                                   reads=[un("rkT"), "Vtm"], writes=[ypsn])
                            P.cutoff = p1_stage < 7.75
                            if d == 0:
                                for ch in range(2):
                                    gch = ti * 2 + ch
                                    bk = next_ubank()
                                    for c in range(4):
                                        h = 2 * c + p
                                        op("pe", lambda e, c=c, h=h, ch=ch, prs=prs, p=p: e.matmul(
                                            PB[2 + p][:, h * 64:h * 64 + 64], lhsT=qtT[0][prs, ch, c, :], rhs=Tbf[0][prs, c, :],
                                            start=False, stop=False, skip_group_check=True), reads=["qtT0_%d" % p, "Tbf0_%d" % p], writes=[ypsn])
                                        op("pe", lambda e, bk=bk, c=c, ch=ch, prs=prs: e.matmul(
                                            PB[bk][:, c * 64:c * 64 + 64], lhsT=mtT[0][prs, ch, c, :], rhs=Tbf[0][prs, c, :],
                                            start=True, stop=True), reads=["mtT0_%d" % p, "Tbf0_%d" % p], writes=["pb%d" % bk])
                                    op("dve", lambda e, bk=bk, ch=ch, prs=prs: e.tensor_tensor(
                                        out=ctmp[prs, :, :], in0=PB[bk][prs, 0:256].rearrange("p (c i) -> p c i", c=4),
                                        in1=nnT[0][prs, ch, :, :], op=ALU.add), reads=["pb%d" % bk, "nnT0_%d" % p], writes=["ctmp%d" % p])
                                    op("dve", lambda e, prs=prs: e.tensor_tensor(out=ctmp[prs, :, :], in0=ctmp[prs, :, :],
                                                                                 in1=T32[0][prs, :, :], op=ALU.add),
                                       reads=["ctmp%d" % p, "T32_0_%d" % p], writes=["ctmp%d" % p])
                                    op("dve", lambda e, prs=prs, gch=gch: e.tensor_tensor(
                                        out=Tbf[0][prs, :, :], in0=ctmp[prs, :, :],
                                        in1=bcast(gC[0][prs, :, gch:gch + 1], [64, 4, 64]), op=ALU.mult),
                                       reads=["ctmp%d" % p, "gC0"], writes=["Tbf0_%d" % p])
                                    op("pool", lambda e, prs=prs, gch=gch: e.tensor_tensor(
                                        out=T32[0][prs, :, :], in0=ctmp[prs, :, :],
                                        in1=bcast(gC[0][prs, :, gch:gch + 1], [64, 4, 64]), op=ALU.mult),
                                       reads=["ctmp%d" % p, "gC0"], writes=["T32_0_%d" % p])
                                    yield
                    def run(gens, extra=None):
                        while gens or extra is not None:
                            for g_ in list(gens):
                                try:
                                    next(g_)
                                except StopIteration:
                                    gens.remove(g_)
                            if extra is not None:
                                try:
                                    next(extra)
                                except StopIteration:
                                    extra = None

                    if tl == 0:
                        tm_transposes(tm_entries([0], True))
                        g_part()
                        UBANKS[:] = [4, 5, 6, 7]
                        run([do_unit(0, 0), do_unit(0, 1)], extra=a3b(1))
                        UBANKS[:] = [4, 5, 6, 7, 0, 1]
                        op("dve", lambda e: e.tensor_copy(out=bon[:], in_=PB[1][:, 64:80]), reads=["pb1"], writes=["bon"])
                        tm_transposes(tm_entries([1], False))
                        bv_part()
                        run([do_unit(1, 0), do_unit(1, 1)])
                    else:
                        tm_transposes(tm_entries([0, 1], True))
                        bv_part()
                        g_part()
                        run([do_unit(0, 0), do_unit(0, 1), do_unit(1, 0), do_unit(1, 1)])
                    P.cutoff = p1_stage < 8
                    for p in range(2):
                        op("act", lambda e, p=p: e.activation(
                            out=ysum[:].rearrange("p (c q j) -> p c q j", c=4, q=2)[:, :, p, :],
                            in_=PB[2 + p][:, :].rearrange("p (c q j) -> p c q j", c=4, q=2)[:, :, p, :], func=AF.Copy),
                           reads=["pb%d" % (2 + p)], writes=["ysum"])
                    op("sp", lambda e, ti=ti: e.dma_start(out=ysS[ti], in_=ysum[:]), reads=["ysum"], writes=["ysS"], dma="s_ys")
                    op("sp", lambda e, ti=ti: e.dma_start(out=mtS[ti], in_=mtT[1][:].rearrange("p a c j -> p (a c j)")),
                       reads=["mtT1_0", "mtT1_1"], writes=["mtS"], dma="s_mt")
                    op("sp", lambda e, ti=ti: e.dma_start(out=nnS[ti], in_=nnT[1][:].rearrange("p a c j -> p (a c j)")),
                       reads=["nnT1_0", "nnT1_1"], writes=["nnS"], dma="s_nn")
                    op("sp", lambda e, ti=ti: e.dma_start(out=qtS[ti], in_=qtT[1][:].rearrange("p a c j -> p (a c j)")),
                       reads=["qtT1_0", "qtT1_1"], writes=["qtS"], dma="s_qt")

                for tl in range(2):
                    do_tile(tl)
                if not isS:
                    sq_ = gi - 16
                    for c in range(4):
                        op("pe", lambda e, c=c: e.transpose(out=PB[0][0:64, c * 128:c * 128 + 128], in_=T32[0][:, c, :], identity=identf[:]),
                           reads=["T32_0_0", "T32_0_1", "identf"], writes=["pb0"])
                    for c in range(4):
                        op("dve", lambda e, c=c: e.tensor_copy(out=otr[:, :], in_=PB[0][0:64, c * 128:c * 128 + 128]),
                           reads=["pb0"], writes=["otr"])
                        op("sp", lambda e, c=c, sq_=sq_: e.dma_start(
                            out=ns_d[0, sq_, 2 * c:2 * c + 2].rearrange("q i j -> i q j"),
                            in_=otr[:, :].rearrange("i (q j) -> i q j", q=2)), reads=["otr"], writes=["ns"], dma="s_ns")
                if gi >= 15:
                    op("pool", lambda e: e.memset(T32[0][:], 0.0), writes=["T32_0_0", "T32_0_1"])
                    op("pool", lambda e: e.memset(Tbf[0][:], 0.0), writes=["Tbf0_0", "Tbf0_1"])
            for gi in range(p1_groups):
                do_group(gi)
            build_and_emit(nc, P, bar)

        if stop_after < 2:
            return nc
        with ExitStack() as es:
            P = Prog()
            op = P.op
            woutb = sb(es, "woutb", [128, 8, D], BF16)
            wff1b = sb(es, "wff1b", [128, 8, 4 * D], BF16)
            wff2b = sb(es, "wff2b", [128, 32, D], BF16)
            gateA = sb(es, "gateA", [128, D], F32)
            gateF = sb(es, "gateF", [128, D], F32)
            lnwb = sb(es, "lnwb", [128, 512], F32)
            lnbb = sb(es, "lnbb", [128, 512], F32)
            fgb = sb(es, "fgb", [128, D], F32)
            op("sp", lambda e: e.dma_start(out=lnwb[:], in_=bcast(lnw_d[0:1, :], [128, 512])), writes=["lnwb"], dma="l2")
            op("sp", lambda e: e.dma_start(out=lnbb[:], in_=bcast(lnb_d[0:1, :], [128, 512])), writes=["lnbb"], dma="l3")
            op("sp", lambda e: e.dma_start(out=fgb[:], in_=bcast(fg_d[0:1, :], [128, D])), writes=["fgb"], dma="l4")
            xt2 = sb(es, "xt2", [128, D], F32)
            x1b = [sb(es, "x1b%d" % i, [128, D], F32) for i in range(2)]
            xs2 = sb(es, "xs2", [128, D], BF16)
            junk2 = xs2
            h2Tb = [sb(es, "h2Tb%d" % i, [128, 8, 128], BF16) for i in range(2)]
            afm = sb(es, "afm", [128, 16, 128], BF16)
            rl_ = [sb(es, "rl%d" % i, [128, 512], BF16) for i in range(2)]
            mtL = sb(es, "mtL", [128, 2, 4, 128], BF16)
            nnL = sb(es, "nnL", [128, 2, 4, 64], F32)
            qtL = sb(es, "qtL", [128, 2, 4, 128], BF16)
            ysL = sb(es, "ysL", [128, 512], F32)
            gL = sb(es, "gL", [128, 512], BF16)
            bvL = sb(es, "bvL", [128, 512], BF16)
            ocL = sb(es, "ocL", [128, 4, 128], BF16)
            yn = sb(es, "yn", [128, 512], F32)
            yy = ysL
            ysq = yn
            otm = sb(es, "otm", [128, 512], BF16)
            ofm = sb(es, "ofm", [128, 4, 128], BF16)
            s8 = sb(es, "s8", [128, 6, 8], F32)
            st2 = sb(es, "st2", [128, 8], F32)
            ctmp2 = sb(es, "ctmp2", [128, 4, 64], F32)
            otr2 = ctmp2[0:64, 0:2, :].rearrange("p a i -> p (a i)")

            for kc in range(8):
                op("pool", lambda e, kc=kc: e.dma_start(out=woutb[:, kc, :], in_=wout_d[kc * 128:kc * 128 + 128, :]),
                   writes=["woutb%d" % kc], dma="wo")
                op("pool", lambda e, kc=kc: e.dma_start(out=wff1b[:, kc, :], in_=wff1_d[kc * 128:kc * 128 + 128, :]),
                   writes=["wff1b%d" % kc], dma="w1")
            for kc in range(32):
                op("pool", lambda e, kc=kc: e.dma_start(out=wff2b[:, kc, :], in_=wff2_d[kc * 128:kc * 128 + 128, :]),
                   writes=["wff2b%d" % kc], dma="w2")
            WO = ["woutb%d" % kc for kc in range(8)]
            W1 = ["wff1b%d" % kc for kc in range(8)]
            W2 = ["wff2b%d" % kc for kc in range(32)]

            def front(ti):
                isS = ti < NSAMP_T
                m = 0 if isS else 1
                if (not isS) and (ti % 2 == 1):
                    op("pool", lambda e: e.memset(T32[1][:], 0.0), writes=["T32_1_0", "T32_1_1"])
                    op("pool", lambda e: e.memset(Tbf[1][:], 0.0), writes=["Tbf1_0", "Tbf1_1"])
                if ti == NSAMP_T - 1:
                    op("dve", lambda e: e.tensor_copy(out=T32[1][:], in_=T32i[:]), writes=["T32_1_0", "T32_1_1"])
                    op("act", lambda e: e.activation(out=Tbf[1][:], in_=T32i[:], func=AF.Copy), writes=["Tbf1_0", "Tbf1_1"])
                op("sp", lambda e, ti=ti: e.dma_start(out=mtL[:].rearrange("p a c j -> p (a c j)"), in_=mtS[ti]), writes=["mtL"], dma="r_mt")
                op("sp", lambda e, ti=ti: e.dma_start(out=nnL[:].rearrange("p a c j -> p (a c j)"), in_=nnS[ti]), writes=["nnL"], dma="r_nn")
                op("sp", lambda e, ti=ti: e.dma_start(out=qtL[:].rearrange("p a c j -> p (a c j)"), in_=qtS[ti]), writes=["qtL"], dma="r_qt")
                op("sp", lambda e, ti=ti: e.dma_start(out=ysL[:], in_=ysS[ti]), writes=["ysL"], dma="r_ys")
                op("sp", lambda e, ti=ti: e.dma_start(out=gL[:], in_=gS[ti]), writes=["gL"], dma="r_g")
                op("sp", lambda e, ti=ti: e.dma_start(out=bvL[:], in_=bvS[ti]), writes=["bvL"], dma="r_bv")
                op("sp", lambda e, ti=ti: e.dma_start(out=ocL[:].rearrange("p c t -> p (c t)"), in_=ocS[ti]), writes=["ocL"], dma="r_oc")
                op("dve", lambda e: e.memset(PB[0][:, :], 0.0), writes=["pb0"])
                op("dve", lambda e: e.memset(PB[3][:, :], 0.0), writes=["pb3"])
                for p in range(2):
                    prs = slice(64 * p, 64 * p + 64)
                    for ch in (1, 0):
                        gch = ti * 2 + ch
                        for c in range(4):
                            h = 2 * c + p
                            op("pe", lambda e, c=c, h=h, ch=ch, prs=prs, p=p: e.matmul(
                                PB[3 * p][:, h * 64:h * 64 + 64], lhsT=qtL[prs, ch, c, :], rhs=Tbf[1][prs, c, :],
                                start=False, stop=(ch == 0), skip_group_check=True), reads=["qtL", "Tbf1_%d" % p], writes=["pb%d" % (3 * p)])
                            op("pe", lambda e, c=c, ch=ch, prs=prs, p=p: e.matmul(
                                PB[1][:, c * 64:c * 64 + 64], lhsT=mtL[prs, ch, c, :], rhs=Tbf[1][prs, c, :],
                                start=True, stop=True), reads=["mtL", "Tbf1_%d" % p], writes=["pb1"])
                        op("dve", lambda e, ch=ch, prs=prs, p=p: e.tensor_tensor(
                            out=ctmp2[prs, :, :], in0=PB[1][prs, 0:256].rearrange("p (c i) -> p c i", c=4),
                            in1=nnL[prs, ch, :, :], op=ALU.add), reads=["pb1", "nnL"], writes=["ctmp2_%d" % p])
                        op("dve", lambda e, prs=prs: e.tensor_tensor(out=ctmp2[prs, :, :], in0=ctmp2[prs, :, :],
                                                                     in1=T32[1][prs, :, :], op=ALU.add),
                           reads=["ctmp2_%d" % p, "T32_1_%d" % p], writes=["ctmp2_%d" % p])
                        op("dve", lambda e, prs=prs, gch=gch: e.tensor_tensor(
                            out=Tbf[1][prs, :, :], in0=ctmp2[prs, :, :], in1=bcast(gC[1][prs, :, gch:gch + 1], [64, 4, 64]),
                            op=ALU.mult), reads=["ctmp2_%d" % p], writes=["Tbf1_%d" % p])
                        op("pool", lambda e, prs=prs, gch=gch: e.tensor_tensor(
                            out=T32[1][prs, :, :], in0=ctmp2[prs, :, :], in1=bcast(gC[1][prs, :, gch:gch + 1], [64, 4, 64]),
                            op=ALU.mult), reads=["ctmp2_%d" % p], writes=["T32_1_%d" % p])
                        yield
                if (not isS) and (ti % 2 == 0):
                    sq_ = (ti - NSAMP_T) // 2
                    for c in range(4):
                        op("pe", lambda e, c=c: e.transpose(out=PB[1][0:64, c * 128:c * 128 + 128], in_=T32[1][:, c, :], identity=identf[:]),
                           reads=["T32_1_0", "T32_1_1", "identf"], writes=["pb1"])
                    for c in range(4):
                        op("dve", lambda e, c=c: e.tensor_copy(out=otr2[:, :], in_=PB[1][0:64, c * 128:c * 128 + 128]),
                           reads=["pb1"], writes=["otr2", "ctmp2_0", "ctmp2_1"])
                        op("sp", lambda e, c=c, sq_=sq_: e.dma_start(
                            out=ns_d[1, sq_, 2 * c:2 * c + 2].rearrange("q i j -> i q j"),
                            in_=otr2[:, :].rearrange("i (q j) -> i q j", q=2)), reads=["otr2", "ctmp2_0", "ctmp2_1"], writes=["ns"], dma="s_ns")
                for p in range(2):
                    op("dve", lambda e, p=p: e.tensor_tensor(
                        out=yy[:].rearrange("p (c q j) -> p c q j", c=4, q=2)[:, :, p, :],
                        in0=PB[3 * p][:, :].rearrange("p (c q j) -> p c q j", c=4, q=2)[:, :, p, :],
                        in1=ysL[:].rearrange("p (c q j) -> p c q j", c=4, q=2)[:, :, p, :], op=ALU.add),
                       reads=["pb%d" % (3 * p), "ysL"], writes=["ysL"])
                yield
                y3 = lambda t: t[:].rearrange("p (h j) -> p h j", h=8)
                op("dve", lambda e: e.tensor_reduce(out=s8[:, 0, :], in_=y3(yy), axis=AX.X, op=ALU.add), reads=["ysL"], writes=["s8a"])
                op("dve", lambda e: e.tensor_tensor(out=ysq[:], in0=yy[:], in1=yy[:], op=ALU.mult), reads=["ysL"], writes=["yn"])
                op("dve", lambda e: e.tensor_reduce(out=s8[:, 1, :], in_=y3(ysq), axis=AX.X, op=ALU.add), reads=["yn"], writes=["s8b"])
                op("dve", lambda e: e.tensor_scalar(out=s8[:, 2, :], in0=s8[:, 0, :], scalar1=1.0 / 64, scalar2=None, op0=ALU.mult),
                   reads=["s8a"], writes=["s8c"])
                op("dve", lambda e: e.tensor_tensor(out=s8[:, 3, :], in0=s8[:, 2, :], in1=s8[:, 2, :], op=ALU.mult),
                   reads=["s8c"], writes=["s8d"])
                op("dve", lambda e: e.scalar_tensor_tensor(out=s8[:, 4, :], in0=s8[:, 1, :], scalar=1.0 / 64, in1=s8[:, 3, :],
                                                           op0=ALU.mult, op1=ALU.subtract), reads=["s8b", "s8d"], writes=["s8e"])
                op("dve", lambda e: e.tensor_scalar(out=s8[:, 4, :], in0=s8[:, 4, :], scalar1=GN_EPS, scalar2=None, op0=ALU.add),
                   reads=["s8e"], writes=["s8e"])
                op("pool", lambda e: e.tensor_tensor(out=s8[:, 5, :], in0=s8[:, 4, :], in1=nhalf[:, 0:8], op=ALU.pow),
                   reads=["s8e", "nhalf"], writes=["s8f"])
                yield
                op("dve", lambda e: e.tensor_tensor(out=y3(yn), in0=y3(yy), in1=bcast(s8[:, 2, :].unsqueeze(2), [128, 8, 64]),
                                                    op=ALU.subtract), reads=["ysL", "s8c"], writes=["yn"])
                op("dve", lambda e: e.tensor_tensor(out=y3(yn), in0=y3(yn), in1=bcast(s8[:, 5, :].unsqueeze(2), [128, 8, 64]),
                                                     op=ALU.mult), reads=["yn", "s8f"], writes=["yn"])
                op("dve", lambda e: e.tensor_tensor(out=yn[:], in0=yn[:], in1=lnwb[:], op=ALU.mult), reads=["yn", "lnwb"], writes=["yn"])
                op("dve", lambda e: e.tensor_tensor(out=yn[:], in0=yn[:], in1=lnbb[:], op=ALU.add), reads=["yn", "lnbb"], writes=["yn"])
                op("dve", lambda e: e.tensor_tensor(out=yn[:], in0=yn[:], in1=bvL[:], op=ALU.add), reads=["yn", "bvL"], writes=["yn"])
                op("dve", lambda e: e.tensor_tensor(out=otm[:], in0=yn[:], in1=gL[:], op=ALU.mult), reads=["yn", "gL"], writes=["otm"])
                for c in range(4):
                    op("pe", lambda e, c=c: e.transpose(out=pbf(3)[:, c * 128:c * 128 + 128], in_=otm[:, c * 128:c * 128 + 128],
                                                        identity=identb[:]), reads=["otm", "identb"], writes=["pb3"])
                op("act", lambda e: e.activation(out=ofm[:], in_=pbf(3)[:, 0:512].rearrange("p (c t) -> p c t", c=4), func=AF.Copy),
                   reads=["pb3"], writes=["ofm"])
                yield
                if ti == NTILE - 1 or ti == NSAMP_T - 1:
                    op("sp", lambda e, m=m: e.dma_start(out=gateA[:], in_=gatesS[:, m * 2 * D:m * 2 * D + D]),
                       writes=["gateA"], dma="l5")
                op("sp", lambda e, ti=ti: e.dma_start(out=xt2[:], in_=x_d[ti * 128:ti * 128 + 128, :]), writes=["xt2"], dma="r_x")
                x1 = x1b[ti % 2]
                x1n = "x1b%d" % (ti % 2)
                h2T = h2Tb[ti % 2]
                h2n = "h2Tb%d" % (ti % 2)
                for half in range(2):
                    for kc in range(4):
                        op("pe", lambda e, half=half, kc=kc: e.matmul(PB[0][:, :], lhsT=ofm[:, kc, :],
                                                                      rhs=woutb[:, kc, half * 512:half * 512 + 512],
                                                                      start=(kc == 0), stop=(kc == 3)),
                           reads=["ofm"] + WO, writes=["pb0"])
                    for kc in range(4):
                        op("pe", lambda e, half=half, kc=kc: e.matmul(PB[3][:, :], lhsT=ocL[:, kc, :],
                                                                      rhs=woutb[:, 4 + kc, half * 512:half * 512 + 512],
                                                                      start=(kc == 0), stop=(kc == 3)),
                           reads=["ocL"] + WO, writes=["pb3"])
                    hs = slice(half * 512, half * 512 + 512)
                    op("act", lambda e, hs=hs, x1=x1: e.activation(out=x1[:, hs], in_=PB[0][:, :], func=AF.Copy),
                       reads=["pb0"], writes=[x1n])
                    op("dve", lambda e, hs=hs, ti=ti, x1=x1: e.scalar_tensor_tensor(
                        out=x1[:, hs], in0=PB[3][:, :], scalar=rcall[:, ti:ti + 1], in1=x1[:, hs],
                        op0=ALU.mult, op1=ALU.add), reads=["pb3", x1n], writes=[x1n])
                    op("dve", lambda e, hs=hs, x1=x1: e.tensor_tensor(out=x1[:, hs], in0=x1[:, hs], in1=gateA[:, hs], op=ALU.mult),
                       reads=[x1n, "gateA"], writes=[x1n])
                    op("dve", lambda e, hs=hs, x1=x1: e.tensor_tensor(out=x1[:, hs], in0=x1[:, hs], in1=xt2[:, hs], op=ALU.add),
                       reads=[x1n, "xt2"], writes=[x1n])
                    yield
                op("act", lambda e, x1=x1: e.activation(out=xs2[:], in_=x1[:], func=AF.Square, accum_out=st2[:, 0:1]),
                   reads=[x1n], writes=["xs2", "st2a"])
                op("dve", lambda e: e.tensor_scalar(out=st2[:, 1:2], in0=st2[:, 0:1], scalar1=1.0 / D, scalar2=NORM_EPS,
                                                    op0=ALU.mult, op1=ALU.add), reads=["st2a"], writes=["st2b"])
                op("pool", lambda e: e.tensor_tensor(out=st2[:, 2:3], in0=st2[:, 1:2], in1=nhalf[:, 0:1], op=ALU.pow),
                   reads=["st2b", "nhalf"], writes=["st2c"])
                op("dve", lambda e, x1=x1: e.tensor_scalar(out=xs2[:], in0=x1[:], scalar1=st2[:, 2:3], scalar2=None, op0=ALU.mult),
                   reads=[x1n, "st2c"], writes=["xs2"])
                yield
                for kc in range(8):
                    op("pe", lambda e, kc=kc: e.transpose(out=pbf(1)[:, kc * 128:kc * 128 + 128], in_=xs2[:, kc * 128:kc * 128 + 128],
                                                          identity=identb[:]), reads=["xs2", "identb"], writes=["pb1"])
                for kc in range(8):
                    op("act" if kc % 2 else "dve", (lambda e, kc=kc, m=m, h2T=h2T: e.activation(
                        out=h2T[:, kc, :], in_=pbf(1)[:, kc * 128:kc * 128 + 128], func=AF.Identity,
                        scale=modc[:, m, 2, kc:kc + 1], bias=modc[:, m, 3, kc:kc + 1])) if kc % 2 else
                       (lambda e, kc=kc, m=m, h2T=h2T: e.tensor_scalar(
                           out=h2T[:, kc, :], in0=pbf(1)[:, kc * 128:kc * 128 + 128], scalar1=modc[:, m, 2, kc:kc + 1],
                           scalar2=modc[:, m, 3, kc:kc + 1], op0=ALU.mult, op1=ALU.add)),
                       reads=["pb1", "modc"], writes=[h2n])
                yield

            def back(ti):
                isS = ti < NSAMP_T
                m = 0 if isS else 1
                x1 = x1b[ti % 2]
                x1n = "x1b%d" % (ti % 2)
                h2T = h2Tb[ti % 2]
                h2n = "h2Tb%d" % (ti % 2)
                if ti == NTILE - 1 or ti == NSAMP_T - 1:
                    op("sp", lambda e, m=m: e.dma_start(out=gateF[:], in_=gatesS[:, m * 2 * D + D:(m + 1) * 2 * D]),
                       writes=["gateF"], dma="l6")
                def tr_block(cb):
                    c4 = cb % 4
                    r_ = rl_[cb % 2]
                    for i4 in range(4):
                        op("pe", lambda e, r_=r_, i4=i4: e.transpose(out=pbf(2)[:, i4 * 128:i4 * 128 + 128],
                                                                     in_=r_[:, i4 * 128:i4 * 128 + 128], identity=identb[:]),
                           reads=[r_.name, "identb"], writes=["pb2"])
                    if cb % 2:
                        op("act", lambda e, c4=c4: e.activation(out=afm[:, c4 * 4:c4 * 4 + 4, :],
                                                                in_=pbf(2)[:, 0:512].rearrange("p (c t) -> p c t", c=4), func=AF.Copy),
                           reads=["pb2"], writes=["afm%d" % c4])
                    else:
                        op("dve", lambda e, c4=c4: e.tensor_copy(out=afm[:, c4 * 4:c4 * 4 + 4, :],
                                                                 in_=pbf(2)[:, 0:512].rearrange("p (c t) -> p c t", c=4)),
                           reads=["pb2"], writes=["afm%d" % c4])

                for hf in range(2):
                    for c4 in range(4):
                        cb = hf * 4 + c4
                        bank = (6, 7)[cb % 2]
                        for kc in range(8):
                            op("pe", lambda e, kc=kc, cb=cb, bank=bank, h2T=h2T: e.matmul(
                                PB[bank][:, :], lhsT=h2T[:, kc, :], rhs=wff1b[:, kc, cb * 512:cb * 512 + 512],
                                start=(kc == 0), stop=(kc == 7)), reads=W1 + [h2n], writes=["pb%d" % bank])
                        r_ = rl_[cb % 2]
                        op("act", lambda e, bank=bank, r_=r_: e.activation(out=r_[:], in_=PB[bank][:, :], func=AF.Relu),
                           reads=["pb%d" % bank], writes=[r_.name])
                        op("dve", lambda e, r_=r_: e.tensor_tensor(out=r_[:], in0=r_[:], in1=r_[:], op=ALU.mult),
                           reads=[r_.name], writes=[r_.name])
                        if c4 > 0:
                            tr_block(cb - 1)
                        yield
                    tr_block(hf * 4 + 3)
                    for half in range(2):
                        for o16 in range(16):
                            kc = hf * 16 + o16
                            op("pe", lambda e, half=half, kc=kc, o16=o16: e.matmul(PB[4 + half][:, :], lhsT=afm[:, o16, :],
                                                                                   rhs=wff2b[:, kc, half * 512:half * 512 + 512],
                                                                                   start=(kc == 0), stop=(kc == 31)),
                               reads=["afm%d" % (o16 // 4)] + W2, writes=["pb%d" % (4 + half)])
                        yield
                for half in range(2):
                    hs = slice(half * 512, half * 512 + 512)
                    op("dve", lambda e, half=half, hs=hs: e.tensor_tensor(out=PB[4 + half][:, :], in0=PB[4 + half][:, :],
                                                                          in1=gateF[:, hs], op=ALU.mult),
                       reads=["pb%d" % (4 + half), "gateF"], writes=["pb%d" % (4 + half)])
                    op("dve", lambda e, half=half, hs=hs, x1=x1: e.tensor_tensor(out=x1[:, hs], in0=PB[4 + half][:, :], in1=x1[:, hs], op=ALU.add),
                       reads=["pb%d" % (4 + half), x1n], writes=[x1n])
                op("act", lambda e, x1=x1: e.activation(out=afm[:].rearrange("p a t -> p (a t)")[:, 0:D], in_=x1[:], func=AF.Square,
                                                        accum_out=st2[:, 3:4]),
                   reads=[x1n], writes=["afm0", "afm1", "st2d"])
                op("dve", lambda e: e.tensor_scalar(out=st2[:, 4:5], in0=st2[:, 3:4], scalar1=1.0 / D, scalar2=NORM_EPS,
                                                    op0=ALU.mult, op1=ALU.add), reads=["st2d"], writes=["st2e"])
                op("pool", lambda e: e.tensor_tensor(out=st2[:, 5:6], in0=st2[:, 4:5], in1=nhalf[:, 0:1], op=ALU.pow),
                   reads=["st2e", "nhalf"], writes=["st2f"])
                op("dve", lambda e, x1=x1: e.scalar_tensor_tensor(out=x1[:], in0=x1[:], scalar=st2[:, 5:6], in1=fgb[:],
                                                                  op0=ALU.mult, op1=ALU.mult), reads=[x1n, "st2f", "fgb"], writes=[x1n])
                op("sp", lambda e, ti=ti, x1=x1: e.dma_start(out=y_d[ti * 128:ti * 128 + 128, :], in_=x1[:]), reads=[x1n], writes=["y"], dma="s_y")
                yield
            gF = front(NTILE - 1)
            for _ in gF:
                pass
            for ti in range(NTILE - 1, -1, -1):
                gB = back(ti)
                gF = front(ti - 1) if ti > 0 else None
                nb = 0
                while gB is not None or gF is not None:
                    if gB is not None:
                        try:
                            next(gB)
                            nb += 1
                        except StopIteration:
                            gB = None
                    if gF is not None and (gB is None or nb % 1 == 0):
                        try:
                            next(gF)
                        except StopIteration:
                            gF = None
            build_and_emit(nc, P, bar)
    return nc


_NC_CACHE = {}


def kernel(x_prompt, x_sample, c, state_rwkv_fwd, state_rwkv_bwd, c_ctx, w_mod, b_mod,
           norm1_g, w_in, mu_shift, w0_f, w_w2_f, w0_b, w_w2_b, a0_f, w_a2_f, a0_b, w_a2_b,
           w_g2, k_k, k_a, r_k, ln_x_w, ln_x_b, conv_w, conv_gain, w_out, norm2_g,
           w_ff1, w_ff2, final_g):
    f = lambda a: np.ascontiguousarray(np.asarray(a, dtype=np.float32))
    x_prompt, x_sample, c, c_ctx = f(x_prompt), f(x_sample), f(c), f(c_ctx)
    sf, sbk = f(state_rwkv_fwd), f(state_rwkv_bwd)
    bm = f(b_mod)[0]
    rows = []

    def addv(v):
        v = f(v).reshape(-1)
        rows.append(v.reshape(-1, 128))
    addv(mu_shift[0])
    for v in (w0_f, w0_b, a0_f, a0_b, k_k, k_a, r_k):
        addv(v[0])
    cw = f(conv_w)[0]
    addv(cw[0]); addv(cw[1]); addv(cw[2])
    addv(conv_gain[0]); addv(norm1_g[0]); addv(norm2_g[0])
    addv(bm[0:D]); addv(bm[D:2 * D]); addv(bm[3 * D:4 * D]); addv(bm[4 * D:5 * D])
    vr = np.concatenate(rows, axis=0)
    assert vr.shape[0] == NVR
    vecs = np.zeros((128, 128), np.float32)
    vecs[:NVR] = vr
    common = {
        "vecs": vecs, "w_mod": f(w_mod)[0], "b_mod": f(b_mod), "w_in": f(w_in)[0],
        "w_w2": np.concatenate([f(w_w2_f)[0], f(w_w2_b)[0]], axis=0),
        "w_a2": np.concatenate([f(w_a2_f)[0], f(w_a2_b)[0]], axis=0),
        "w_g2": f(w_g2)[0], "ln_w": f(ln_x_w), "ln_b": f(ln_x_b), "final_g": f(final_g).reshape(1, D),
        "w_out": f(w_out)[0], "w_ff1": f(w_ff1)[0], "w_ff2": f(w_ff2)[0],
    }
    in_maps = []
    for i in range(NCORES):
        xm = np.concatenate([x_sample[i], x_prompt[4 * i:4 * i + 4].reshape(1024, D)], axis=0)
        mm = dict(common)
        mm["x"] = np.ascontiguousarray(xm)
        mm["c2"] = np.ascontiguousarray(np.stack([c[i], c_ctx], axis=0))
        mm["st"] = np.ascontiguousarray(np.stack([sf[i, 0], sbk[i, 0]], axis=0))
        in_maps.append(mm)
    if "nc" not in _NC_CACHE:
        _NC_CACHE["nc"] = build_program()
    nc = _NC_CACHE["nc"]
    res = run_bass_kernel_spmd(nc, in_maps, core_ids=list(range(NCORES)))
    y_prompt = np.zeros((32, 256, D), np.float32)
    y_sample = np.zeros((8, 4096, D), np.float32)
    nsf = np.zeros((32, 1, 8, 64, 64), np.float32)
    nsb = np.zeros((32, 1, 8, 64, 64), np.float32)
    for i in range(NCORES):
        r = res.results[i]
        y_sample[i] = r["y"][:4096]
        y_prompt[4 * i:4 * i + 4] = r["y"][4096:].reshape(4, 256, D)
        nsf[4 * i:4 * i + 4, 0] = r["ns"][0]
        nsb[4 * i:4 * i + 4, 0] = r["ns"][1]
    return (y_prompt, y_sample, nsf, nsb)
```
